# Optimizing a Trainium2 kernel written in Bass

```python
import math
import jax, jax.numpy as jnp
from jax import lax
import numpy as np

D_MODEL = 1024
BATCH = 8
SEQ = 2048
DEPTH = 2
DEC_BATCH = 128
DEC_SEQ = 8
PAST_LEN = 16384
PAGE_SIZE = 128

N_MIXERS = 2
N_LAYERS_A = (DEPTH + N_MIXERS - 1) // N_MIXERS
N_LAYERS_B = DEPTH // N_MIXERS

A_HEADS = 8
A_DK = 128
A_DV = 128
A_QK = A_HEADS * A_DK
A_VW = A_HEADS * A_DV
A_CONV_CH = 2 * A_QK + A_VW
CONV_W = 4
A_PROJ = A_CONV_CH + A_VW + 2 * A_HEADS

B_HEADS = 4
B_DK = 256
B_DV = 512
B_QK = B_HEADS * B_DK
B_VW = B_HEADS * B_DV
B_PROJ = 2 * B_QK + 2 * B_VW

CHUNK = 64
ROPE_BASE = 10000.0
EPS = 1e-6
F32 = jnp.float32

kernel_name = 'hybrid_gdn_retention_step'


def _rmsnorm(x, w):
    xf = x.astype(F32)
    return xf * lax.rsqrt(jnp.mean(xf * xf, -1, keepdims=True) + EPS) * w.astype(F32)


def _l2norm(x):
    return x * lax.rsqrt(jnp.sum(x * x, -1, keepdims=True) + EPS)


def _chunk_size(L):
    return L if L <= CHUNK else math.gcd(L, CHUNK)


def _to_chunks(t, C):
    B, L = t.shape[:2]
    t = t.reshape((B, L // C, C) + t.shape[2:])
    return jnp.swapaxes(jnp.moveaxis(t, 1, 0), 2, 3)


def _from_chunks(o):
    n, B, H, C, d = o.shape
    return jnp.transpose(o, (1, 0, 3, 2, 4)).reshape(B, n * C, H, d)


def _short_conv(x, buf, w):
    L = x.shape[1]
    xe = jnp.concatenate([buf.astype(F32), x], axis=1)
    out = sum(xe[:, j:j + L] * w[j] for j in range(CONV_W))
    return jax.nn.silu(out), xe[:, L:]


def _gated_delta_chunked(q, k, v, beta, g, S0):
    L = q.shape[1]
    C = _chunk_size(L)
    dv = v.shape[-1]
    idx = jnp.arange(C)
    strict = idx[:, None] > idx[None, :]
    incl = idx[:, None] >= idx[None, :]
    eye = jnp.eye(C, dtype=F32)

    def step(S, inp):
        qc, kc, vc, bc, gc = inp
        G = jnp.cumsum(gc, axis=-1)
        diff = G[..., :, None] - G[..., None, :]
        dec = jnp.where(incl, jnp.exp(jnp.where(incl, diff, 0.0)), 0.0)
        kk = jnp.einsum('bhid,bhjd->bhij', kc, kc)
        low = jnp.where(strict, bc[..., :, None] * kk * dec, 0.0)
        eG = jnp.exp(G)
        rhs = jnp.concatenate([bc[..., None] * vc, (bc * eG)[..., None] * kc], axis=-1)
        sol = lax.linalg.triangular_solve(low + eye, rhs, left_side=True, lower=True,
                                          unit_diagonal=True)
        u = sol[..., :dv] - jnp.einsum('bhck,bhkv->bhcv', sol[..., dv:], S)
        qk = jnp.einsum('bhid,bhjd->bhij', qc, kc) * dec
        o = eG[..., None] * jnp.einsum('bhck,bhkv->bhcv', qc, S) + jnp.einsum('bhij,bhjv->bhiv', qk, u)
        S_new = (jnp.exp(G[..., -1])[..., None, None] * S
                 + jnp.einsum('bhck,bhcv->bhkv', kc * jnp.exp(G[..., -1:] - G)[..., None], u))
        return S_new, o

    xs = tuple(_to_chunks(t.astype(F32), C) for t in (q, k, v, beta, g))
    S, o = lax.scan(step, S0.astype(F32), xs)
    return _from_chunks(o), S


def _retention_chunked(q, k, v, S0):
    L = q.shape[1]
    C = _chunk_size(L)
    lg = jnp.log(1.0 - 2.0 ** (-5.0 - jnp.arange(B_HEADS, dtype=F32)))
    idx = jnp.arange(C, dtype=F32)
    diff = idx[:, None] - idx[None, :]
    D = jnp.where(diff >= 0, jnp.exp(lg[:, None, None] * jnp.maximum(diff, 0.0)), 0.0)
    q_dec = jnp.exp(lg[:, None] * (idx + 1.0))
    k_dec = jnp.exp(lg[:, None] * (C - 1.0 - idx))
    s_dec = jnp.exp(lg * C)

    def step(S, inp):
        qc, kc, vc = inp
        qk = jnp.einsum('bhid,bhjd->bhij', qc, kc) * D
        o = (jnp.einsum('bhij,bhjv->bhiv', qk, vc)
             + q_dec[..., None] * jnp.einsum('bhck,bhkv->bhcv', qc, S))
        S_new = s_dec[:, None, None] * S + jnp.einsum('bhck,bhcv->bhkv', kc * k_dec[..., None], vc)
        return S_new, o

    xs = tuple(_to_chunks(t.astype(F32), C) for t in (q, k, v))
    S, o = lax.scan(step, S0.astype(F32), xs)
    return _from_chunks(o), S


def _rotary(x, pos0):
    L, d = x.shape[1], x.shape[-1]
    half = d // 2
    inv = 1.0 / (ROPE_BASE ** jnp.linspace(0.0, 1.0, half, dtype=F32))
    pos = pos0 + jnp.arange(L, dtype=F32)
    ang = pos[:, None] * inv[None, :]
    cos = jnp.cos(ang)[None, :, None, :]
    sin = jnp.sin(ang)[None, :, None, :]
    x1, x2 = x[..., :half], x[..., half:]
    return jnp.concatenate([x1 * cos - x2 * sin, x1 * sin + x2 * cos], axis=-1)


def _gdn_layer(x, S0, buf, norm_w, w_in, conv_w, a_log, dt_bias, onorm_w, w_out):
    B, L, _ = x.shape
    p = _rmsnorm(x, norm_w) @ w_in.astype(F32)
    mixed, z, b_raw, a_raw = jnp.split(
        p, [A_CONV_CH, A_CONV_CH + A_VW, A_CONV_CH + A_VW + A_HEADS], axis=-1)
    mixed, new_buf = _short_conv(mixed, buf, conv_w.astype(F32))
    q, k, v = jnp.split(mixed, [A_QK, 2 * A_QK], axis=-1)
    q = _l2norm(q.reshape(B, L, A_HEADS, A_DK)) * (A_DK ** -0.5)
    k = _l2norm(k.reshape(B, L, A_HEADS, A_DK))
    v = v.reshape(B, L, A_HEADS, A_DV)
    beta = jax.nn.sigmoid(b_raw)
    g = -jnp.exp(a_log.astype(F32)) * jax.nn.softplus(a_raw + dt_bias.astype(F32))
    o, S = _gated_delta_chunked(q, k, v, beta, g, S0)
    o = _rmsnorm(o, onorm_w) * jax.nn.silu(z.reshape(B, L, A_HEADS, A_DV))
    y = x + (o.reshape(B, L, A_VW) @ w_out.astype(F32)).astype(x.dtype)
    return y, S.astype(x.dtype), new_buf.astype(x.dtype)


def _ret_layer(x, pos0, S0, norm_w, w_in, onorm_w, w_out):
    B, L, _ = x.shape
    p = _rmsnorm(x, norm_w) @ w_in.astype(F32)
    q, k, v, gate = jnp.split(p, [B_QK, 2 * B_QK, 2 * B_QK + B_VW], axis=-1)
    q = _rotary(q.reshape(B, L, B_HEADS, B_DK), pos0)
    k = _rotary(k.reshape(B, L, B_HEADS, B_DK), pos0) * (B_DK ** -0.5)
    v = v.reshape(B, L, B_HEADS, B_DV)
    o, S = _retention_chunked(q, k, v, S0)
    o = _rmsnorm(o, onorm_w) * jax.nn.silu(gate.reshape(B, L, B_HEADS, B_DV))
    y = x + (o.reshape(B, L, B_VW) @ w_out.astype(F32)).astype(x.dtype)
    return y, S.astype(x.dtype)


def _trunk(x, pos0, gdn_S, gdn_conv, ret_S, norm_w, w_in_a, conv_w_a, a_log_a, dt_bias_a,
           onorm_a, w_out_a, w_in_b, onorm_b, w_out_b, final_norm_w):
    sa, ca, sb = [], [], []
    for i in range(DEPTH):
        j = i // N_MIXERS
        if i % N_MIXERS == 0:
            x, s, c = _gdn_layer(x, gdn_S[j], gdn_conv[j], norm_w[i], w_in_a[j], conv_w_a[j],
                                 a_log_a[j], dt_bias_a[j], onorm_a[j], w_out_a[j])
            sa.append(s)
            ca.append(c)
        else:
            x, s = _ret_layer(x, pos0, ret_S[j], norm_w[i], w_in_b[j], onorm_b[j], w_out_b[j])
            sb.append(s)
    y = _rmsnorm(x, final_norm_w).astype(x.dtype)
    return y, jnp.stack(sa), jnp.stack(ca), jnp.stack(sb)


def setup_inputs(seed: int = 0) -> dict:
    key = jax.random.key(seed)
    ks = jax.random.split(key, 16)
    nrm = jax.random.normal
    dt = jnp.exp(jax.random.uniform(ks[8], (N_LAYERS_A, A_HEADS), F32,
                                    math.log(1e-3), math.log(1e-1)))
    return {
        'x_prompt': nrm(ks[0], (BATCH, SEQ, D_MODEL), F32),
        'x_sample': nrm(ks[1], (DEC_BATCH, DEC_SEQ, D_MODEL), F32),
        'state_gdn_ssm': 0.1 * nrm(ks[2], (N_LAYERS_A, DEC_BATCH, A_HEADS, A_DK, A_DV), F32),
        'state_gdn_conv': nrm(ks[3], (N_LAYERS_A, DEC_BATCH, CONV_W - 1, A_CONV_CH), F32),
        'state_ret': 0.1 * nrm(ks[4], (N_LAYERS_B, DEC_BATCH, B_HEADS, B_DK, B_DV), F32),
        'norm_w': 1.0 + 0.02 * nrm(ks[5], (DEPTH, D_MODEL), F32),
        'w_in_a': nrm(ks[6], (N_LAYERS_A, D_MODEL, A_PROJ), F32) * D_MODEL ** -0.5,
        'conv_w_a': nrm(ks[7], (N_LAYERS_A, CONV_W, A_CONV_CH), F32) * CONV_W ** -0.5,
        'a_log_a': jnp.log(jax.random.uniform(ks[9], (N_LAYERS_A, A_HEADS), F32, 1.0, 16.0)),
        'dt_bias_a': dt + jnp.log(-jnp.expm1(-dt)),
        'onorm_a': 1.0 + 0.02 * nrm(ks[10], (N_LAYERS_A, A_DV), F32),
        'w_out_a': nrm(ks[11], (N_LAYERS_A, A_VW, D_MODEL), F32) * A_VW ** -0.5,
        'w_in_b': nrm(ks[12], (N_LAYERS_B, D_MODEL, B_PROJ), F32) * D_MODEL ** -0.5,
        'onorm_b': 1.0 + 0.02 * nrm(ks[13], (N_LAYERS_B, B_HEADS, B_DV), F32),
        'w_out_b': nrm(ks[14], (N_LAYERS_B, B_VW, D_MODEL), F32) * B_VW ** -0.5,
        'final_norm_w': 1.0 + 0.02 * nrm(ks[15], (D_MODEL,), F32),
    }


def reference(x_prompt, x_sample, state_gdn_ssm, state_gdn_conv, state_ret, norm_w, w_in_a,
              conv_w_a, a_log_a, dt_bias_a, onorm_a, w_out_a, w_in_b, onorm_b, w_out_b,
              final_norm_w):
    bp = x_prompt.shape[0]
    dtp = x_prompt.dtype
    z_sa = jnp.zeros((N_LAYERS_A, bp, A_HEADS, A_DK, A_DV), dtp)
    z_ca = jnp.zeros((N_LAYERS_A, bp, CONV_W - 1, A_CONV_CH), dtp)
    z_sb = jnp.zeros((N_LAYERS_B, bp, B_HEADS, B_DK, B_DV), dtp)
    y_prompt, sa_p, ca_p, sb_p = _trunk(x_prompt, 0.0, z_sa, z_ca, z_sb, norm_w, w_in_a,
                                        conv_w_a, a_log_a, dt_bias_a, onorm_a, w_out_a,
                                        w_in_b, onorm_b, w_out_b, final_norm_w)
    y_sample, sa_s, ca_s, sb_s = _trunk(x_sample, float(PAST_LEN), state_gdn_ssm, state_gdn_conv,
                                        state_ret, norm_w, w_in_a, conv_w_a, a_log_a,
                                        dt_bias_a, onorm_a, w_out_a, w_in_b, onorm_b,
                                        w_out_b, final_norm_w)
    return (y_prompt, y_sample, sa_p, ca_p, sb_p, sa_s, ca_s, sb_s)
```

```python
import numpy as np
import ml_dtypes
from contextlib import ExitStack
import concourse.bass as bass
import concourse.mybir as mybir
from concourse.bass_utils import run_bass_kernel_spmd

F32 = mybir.dt.float32
F32R = mybir.dt.float32r
BF16 = mybir.dt.bfloat16
AF = mybir.ActivationFunctionType
ALU = mybir.AluOpType
AX = mybir.AxisListType

NCORES = 8
D = 1024
LP = 2048
NPT = LP // 128
NSEQ = 16
EPS = 1e-6
NEG = -32768.0


class Res:
    __slots__ = ("name", "w", "r", "excl", "strict")

    def __init__(self, name):
        self.name = name
        self.excl = False
        self.strict = False
        self.w = None
        self.r = {}


class V:
    __slots__ = ("ap", "res")

    def __init__(self, ap, res):
        self.ap = ap
        self.res = res


class TT:
    def __init__(self, t, name, nres=1):
        self.t = t
        self.res = Res(name)

    def __getitem__(self, idx):
        return V(self.t[idx], self.res)

    def v(self, ap):
        return V(ap, self.res)


class STT(TT):
    def __init__(self, t, name, n, slot_size):
        self.t = t
        self.n = n
        self.ss = slot_size
        self.slots = [Res(f"{name}_{i}") for i in range(n // slot_size)]
        self.res = tuple(self.slots)

    def __getitem__(self, idx):
        key = idx[1] if isinstance(idx, tuple) and len(idx) > 1 else slice(None)
        if isinstance(key, int):
            lo = hi = key
        else:
            lo = key.start or 0
            hi = (key.stop if key.stop is not None else self.n) - 1
        rs = tuple(self.slots[lo // self.ss:hi // self.ss + 1])
        return V(self.t[idx], rs if len(rs) > 1 else rs[0])


def _flat(xs):
    out = []
    for x in xs:
        r = x.res if isinstance(x, (V, TT)) else x
        if isinstance(r, tuple):
            out.extend(r)
        else:
            out.append(r)
    return out


class Chan:
    def __init__(self, sem):
        self.sem = sem
        self.count = 0


class EngQ:
    def __init__(self, name, eng, sem):
        self.name = name
        self.eng = eng
        self.sem = sem
        self.count = 0
        self.seen = {}


class KB:
    def __init__(self, nc, es):
        self.nc = nc
        self.es = es
        self.q = {}
        for name, eng in (("pe", nc.tensor), ("act", nc.scalar), ("dve", nc.vector),
                          ("pool", nc.gpsimd), ("sp", nc.sync)):
            sem = es.enter_context(nc.semaphore("sem_" + name))
            self.q[name] = EngQ(name, eng, sem)
        self.chans = []
        self.n_instr = 0

    def sbs(self, name, shape, dt, slot_size, es=None):
        t = (es or self.es).enter_context(self.nc.sbuf_tensor("s_" + name, list(shape), dt))
        return STT(t, name, shape[1], slot_size)

    def sb(self, name, shape, dt, es=None):
        t = (es or self.es).enter_context(self.nc.sbuf_tensor("s_" + name, list(shape), dt))
        return TT(t, name)

    def chan(self, name):
        sem = self.es.enter_context(self.nc.semaphore("ch_" + name))
        c = Chan(sem)
        self.chans.append(c)
        return c

    def _need(self, q, ev):
        sem, val, owner = ev
        if q.seen.get(sem.num, 0) >= val:
            return
        q.eng.wait_ge(sem, val)
        q.seen[sem.num] = val

    def _deps(self, q, reads, writes):
        me = q.name
        for r in reads:
            if r is None:
                continue
            if r.w is not None:
                self._need(q, r.w)
            if r.excl:
                for ev in r.r.values():
                    if ev[2] != me:
                        self._need(q, ev)
        for w in writes:
            if w is None:
                continue
            if w.w is not None:
                if w.w[2] != me or me == "pool" or w.strict:
                    self._need(q, w.w)
            for ev in w.r.values():
                if ev[2] != me or me == "pool":
                    self._need(q, ev)

    def _record(self, ev, reads, writes):
        for r in reads:
            if r is not None:
                r.r[ev[0].num] = ev
        for w in writes:
            if w is not None:
                w.w = ev
                w.r = {}

    def op(self, eng, fn, reads, writes, inc=True):
        q = self.q[eng]
        reads = _flat(reads)
        writes = _flat(writes)
        self._deps(q, reads, writes)
        ins = fn(q.eng)
        self.n_instr += 1
        if inc:
            ins.then_inc(q.sem, 1)
            q.count += 1
            ev = (q.sem, q.count, q.name)
        else:
            ev = (q.sem, q.count + 1, q.name)
        self._record(ev, reads, writes)
        return ins

    def dma(self, qname, out, in_, chan, **kw):
        q = self.q[qname]
        reads = _flat([in_])
        writes = _flat([out])
        self._deps(q, reads, writes)
        ins = q.eng.dma_start(out=out.ap, in_=in_.ap, **kw)
        ins.then_inc(chan.sem, 16)
        chan.count += 16
        ev = (chan.sem, chan.count, "dma")
        self._record(ev, reads, writes)
        self.n_instr += 1
        return ev

    def barrier(self):
        evs = [(q.sem, q.count, q.name) for q in self.q.values() if q.count > 0]
        evs += [(c.sem, c.count, "dma") for c in self.chans if c.count > 0]
        for q in self.q.values():
            for ev in evs:
                if ev[2] == q.name:
                    continue
                self._need(q, ev)

    def final_wait(self):
        q = self.q["sp"]
        for c in self.chans:
            if c.count > 0:
                self._need(q, (c.sem, c.count, "dma"))
        for qq in self.q.values():
            if qq.name != "sp" and qq.count > 0:
                self._need(q, (qq.sem, qq.count, qq.name))

    def mm(self, out, lhsT, rhs, start=True, stop=True, inc=None):
        if inc is None:
            inc = stop
        return self.op("pe", lambda e: e.matmul(out.ap, lhsT.ap, rhs.ap, start=start, stop=stop,
                                                skip_group_check=True),
                       [lhsT, rhs], [out], inc=inc)

    def tr(self, out, in_, ident, inc=True):
        return self.op("pe", lambda e: e.transpose(out.ap, in_.ap, ident.ap), [in_, ident], [out], inc=inc)

    def act(self, out, in_, func, bias=None, scale=None, accum=None, extra_reads=()):
        kw = {}
        reads = [in_] + list(extra_reads)
        writes = [out]
        if bias is not None:
            if isinstance(bias, V):
                kw["bias"] = bias.ap
                reads.append(bias)
            else:
                kw["bias"] = bias
        if scale is not None:
            if isinstance(scale, V):
                kw["scale"] = scale.ap
                reads.append(scale)
            else:
                kw["scale"] = scale
        if accum is not None:
            kw["accum_out"] = accum.ap
            writes.append(accum)
        return self.op("act", lambda e: e.activation(out.ap, in_.ap, func, **kw), reads, writes)

    def tt(self, eng, out, in0, in1, op):
        return self.op(eng, lambda e: e.tensor_tensor(out.ap, in0.ap, in1.ap, op), [in0, in1], [out])

    def ts(self, eng, out, in0, s1, op0, s2=None, op1=None):
        reads = [in0]
        a1 = s1
        a2 = s2
        if isinstance(s1, V):
            reads.append(s1)
            a1 = s1.ap
        if isinstance(s2, V):
            reads.append(s2)
            a2 = s2.ap
        if op1 is None:
            return self.op(eng, lambda e: e.tensor_scalar(out.ap, in0.ap, a1, None, op0), reads, [out])
        return self.op(eng, lambda e: e.tensor_scalar(out.ap, in0.ap, a1, a2, op0, op1), reads, [out])

    def stt(self, out, in0, scalar, in1, op0, op1):
        reads = [in0, in1]
        a = scalar
        if isinstance(scalar, V):
            reads.append(scalar)
            a = scalar.ap
        return self.op("dve", lambda e: e.scalar_tensor_tensor(out.ap, in0.ap, a, in1.ap, op0, op1),
                       reads, [out])

    def copy(self, eng, out, in_):
        if eng == "act":
            return self.op("act", lambda e: e.copy(out.ap, in_.ap), [in_], [out])
        return self.op(eng, lambda e: e.tensor_copy(out.ap, in_.ap), [in_], [out])

    def memset(self, eng, out, val):
        return self.op(eng, lambda e: e.memset(out.ap, val), [], [out])

    def reduce_sum(self, out, in_):
        return self.op("dve", lambda e: e.tensor_reduce(out.ap, in_.ap, AX.X, ALU.add), [in_], [out])


def bc(v, shape):
    return V(v.ap.broadcast_to(list(shape)), v.res)


def un(v, axis):
    return V(v.ap.unsqueeze(axis), v.res)


def _mask_set(blk):
    i = np.arange(128)
    same = (i[:, None] // blk) == (i[None, :] // blk)
    triU = ((i[:, None] <= i[None, :]) & same).astype(np.float32)
    blkm = same.astype(np.float32)
    triSU = ((i[:, None] > i[None, :]) & same).astype(np.float32)
    strict = ((i[:, None] > i[None, :]) & same).astype(np.float32)
    incl = (i[:, None] >= i[None, :]) & same
    maskneg = np.where(incl, 0.0, NEG).astype(np.float32)
    masknegT = np.ascontiguousarray(maskneg.T)
    return dict(triU=triU, blk=blkm, triSU=triSU, strict=strict, maskneg=maskneg, masknegT=masknegT)


def _cst_layout():
    off = {}
    o = 0

    def add(name, n):
        nonlocal o
        off[name] = (o, n)
        o += n
    add("identf", 128)
    for s in ("p", "s"):
        for nm in ("triU", "blk", "triSU", "strict", "maskneg", "masknegT"):
            add(nm + "_" + s, 128)
    add("nwT0", 8)
    add("nwT1", 8)
    add("cwT", 96)
    add("dtb", 8)
    add("alog", 8)
    add("onwa", 1)
    add("onwbT", 16)
    add("seqmask", 16)
    add("rdec_p", 8)
    add("rdec_s", 8)
    return off, o


CST_OFF, CST_N = _cst_layout()


def _build_cst(norm_w, conv_w_a, a_log_a, dt_bias_a, onorm_a, onorm_b):
    c = np.zeros((128, CST_N), np.float32)

    def put(name, arr):
        o, n = CST_OFF[name]
        c[:, o:o + n] = arr
    put("identf", np.eye(128, dtype=np.float32))
    for s, blk in (("p", 128), ("s", 8)):
        m = _mask_set(blk)
        for nm in ("triU", "blk", "triSU", "strict", "maskneg", "masknegT"):
            put(nm + "_" + s, m[nm])
    put("nwT0", norm_w[0].reshape(8, 128).T)
    put("nwT1", norm_w[1].reshape(8, 128).T)
    cw = conv_w_a[0].reshape(4, 24, 128)
    put("cwT", np.transpose(cw, (2, 1, 0)).reshape(128, 96))
    put("dtb", np.broadcast_to(dt_bias_a[0][None, :], (128, 8)))
    put("alog", np.broadcast_to(a_log_a[0][None, :], (128, 8)))
    put("onwa", onorm_a[0].reshape(128, 1))
    put("onwbT", onorm_b[0].reshape(16, 128).T)
    sm = np.zeros((128, 16), np.float32)
    sm[np.arange(128), np.arange(128) // 8] = 1.0
    put("seqmask", sm)
    gam = 1.0 - 2.0 ** (-5.0 - np.arange(4, dtype=np.float64))
    for nm, blk in (("rdec_p", 128), ("rdec_s", 8)):
        t = (np.arange(128) % blk).astype(np.float64)
        qd = gam[None, :] ** (t[:, None] + 1.0)
        kd = gam[None, :] ** (blk - 1.0 - t[:, None]) * 256.0 ** -0.5
        put(nm, np.concatenate([qd, kd], 1))
    return c


def ext(ap):
    return V(ap, None)


class PsumPool:
    def __init__(self, kb, n=8):
        self.banks = [TT(kb.es.enter_context(kb.nc.psum_tensor(f"psb{i}", [128, 512], F32)), f"psb{i}")
                      for i in range(n)]
        for b in self.banks:
            b.res.excl = True
        self.i = 0

        self.held = set()

    def next(self, hold=False):
        while True:
            b = self.banks[self.i % len(self.banks)]
            self.i += 1
            if id(b) not in self.held:
                break
        if hold:
            self.held.add(id(b))
        return b

    def release(self, b):
        self.held.discard(id(b))


def bfv(bank):
    return V(bank.t[:].bitcast(BF16), bank.res)


class Ring:
    def __init__(self, kb, name, n, shape, dt, es=None, chan=True):
        self.slots = [kb.sb(f"{name}{i}", shape, dt, es) for i in range(n)]
        self.chans = [kb.chan(f"{name}{i}") for i in range(n)] if chan else None
        self.i = 0

    def next(self):
        j = self.i % len(self.slots)
        self.i += 1
        return self.slots[j], (self.chans[j] if self.chans else None)


class Cut(Exception):
    pass


L0_R2 = 2


def build_program(n_ptiles=NPT, do_sample=True, do_l1=True, dbg=False, cut=None):
    def chk(n):
        if cut is not None and cut == n:
            raise Cut()

    nc = bass.Bass("TRN2", target_bir_lowering=False)
    es = ExitStack()
    kb = KB(nc, es)

    def din(name, shape, dt=F32):
        return nc.dram_tensor(name, list(shape), dt, kind="ExternalInput").ap()

    def dout(name, shape, dt=F32):
        return nc.dram_tensor(name, list(shape), dt, kind="ExternalOutput").ap()

    xp_d = din("xp", [LP, D])
    xs_d = din("xs", [128, D])
    sg_d = din("sg", [NSEQ, 8, 128, 128])
    sc_d = din("sc", [48, 3072])
    sr_d = din("sr", [NSEQ, 4, 256, 512])
    wia_d = din("wia", [D, 4112])
    woa_d = din("woa", [D, D])
    wib_d = din("wib", [D, 6144])
    wob_d = din("wob", [2048, D])
    cst_d = din("cst", [128, CST_N])
    idb_d = din("idb", [128, 128], BF16)
    fnw_d = din("fnw", [1, D])
    rot_d = din("rot", [NPT + 1, 128, 2, 128])
    dmask_d = din("dmask", [2, 128, 4, 128])

    yp_d = dout("yp", [LP, D])
    ys_d = dout("ys", [128, D])
    sap_d = dout("sap", [8, 128, 128])
    cap_d = dout("cap", [3, 3072])
    sbp_d = dout("sbp", [4, 256, 512])
    sas_d = dout("sas", [NSEQ, 8, 128, 128])
    cas_d = dout("cas", [48, 3072])
    sbs_d = dout("sbs", [NSEQ, 4, 256, 512])
    x1_d = nc.dram_tensor("x1s", [LP + 128, D], F32, kind="Internal").ap()
    x1_res = [Res(f"x1_{i}") for i in range(NPT + 1)]
    dbg_d = {}
    if dbg:
        for nm, shp in (("d_y1", [128 * (n_ptiles + 1), D]), ("d_mixed", [128, 1536]), ("d_misc", [128, 64]),
                        ("d_dec", [128, 512]), ("d_Y", [128, 512]), ("d_on", [128, 1024])):
            dbg_d[nm] = dout(nm, shp)

    P = PsumPool(kb)
    ch_c = kb.chan("const")
    ch_dbg = kb.chan("dbg")

    cst = kb.sb("cst", [128, CST_N], F32)
    idb = kb.sb("idb_s", [128, 128], BF16)
    kb.dma("sp", cst[:], ext(cst_d), ch_c)
    kb.dma("sp", idb[:], ext(idb_d), ch_c)
    cst.res.w = (ch_c.sem, ch_c.count, "dma")
    idb.res.w = (ch_c.sem, ch_c.count, "dma")

    def C(name, lo=0, hi=None):
        o, n = CST_OFF[name]
        hi = n if hi is None else hi
        return cst[:, o + lo:o + hi]

    neghalf = kb.sb("neghalf", [128, 16], F32)
    kb.memset("dve", neghalf[:], -0.5)
    identf = C("identf")

    def load_masks(s):
        kb.copy("dve", mr["triU"][:], C("triU_" + s))
        kb.ts("dve", mr["negtriU"][:], C("triU_" + s), -1.0, ALU.mult)
        kb.ts("dve", negtriUf[:], C("triU_" + s), -1.0, ALU.mult)
        kb.copy("dve", mr["ones"][:], onesf[:])
        kb.ts("dve", mr["negones"][:], onesf[:], -1.0, ALU.mult)
        kb.copy("dve", mr["ident"][:], identf)
        for nm in ("maskneg", "masknegT"):
            src = C(nm + "_" + s)
            kb.copy("dve", V(mr[nm].t[:].rearrange("p (h j) -> p h j", h=4), mr[nm].res),
                    bc(un(src, 1), [128, 4, 128]))

    xt = kb.sb("xt", [128, D], F32)
    yt = kb.sb("yt", [128, D], F32)
    ch_xt = kb.chan("xt")
    ch_yt = kb.chan("yt")
    xs_b = kb.sb("xs_b", [128, D], BF16)
    xs_b.res.strict = True
    xnT = [kb.sb(f"xnT{i}", [128, 8, 128], BF16) for i in range(2)]
    st4 = kb.sb("st4", [128, 16], F32)
    es0 = ExitStack()
    mr = {}
    for nm in ("triU", "negtriU", "ones", "negones", "ident", "maskneg", "masknegT"):
        w = 512 if nm.startswith("maskneg") else 128
        mr[nm] = kb.sb("mr_" + nm, [128, w], F32R, es0)
    onesf = kb.sb("onesf", [128, 128], F32, es0)
    kb.memset("dve", onesf[:], 1.0)
    negtriUf = kb.sb("negtriUf", [128, 128], F32, es0)


    ch_w0 = kb.chan("w0")
    wia = [kb.sb(f"wia{k}", [128, 4112], BF16, es0) for k in range(8)]
    pieces = ((0, 1536), (1536, 3072), (3072, 4112))
    for k in range(8):
        for (c0, c1) in pieces:
            kb.dma("pool", wia[k][:, c0:c1], ext(wia_d[k * 128:(k + 1) * 128, c0:c1]), ch_w0)
    woa = [kb.sb(f"woa{k}", [128, 1024], BF16, es0) for k in range(8)]
    for k in range(8):
        st, chn = (xt, ch_xt) if k % 2 == 0 else (yt, ch_yt)
        kb.dma("sp", st[:], ext(woa_d[k * 128:(k + 1) * 128, :]), chn)
        kb.ts("dve", woa[k][:], st[:], C("onwa"), ALU.mult)
    for k in range(8):
        wia[k].res.w = (ch_w0.sem, ch_w0.count, "dma")

    diag = kb.sb("diag", [128, 96, 128], BF16, es0)
    for i in range(96):
        kb.ts("dve", diag[:, i, :], idb[:], C("cwT", i, i + 1), ALU.mult)
    negA = kb.sb("negA", [128, 8], F32, es0)
    kb.act(negA[:], C("alog"), AF.Exp)
    kb.ts("dve", negA[:], negA[:], -1.0, ALU.mult)

    def front_end(src_v, nwname, par):
        kb.dma("sp", xt[:], src_v, ch_xt)
        kb.act(xs_b[:], xt[:], AF.Square, accum=st4[:, 0:1])
        kb.ts("dve", st4[:, 1:2], st4[:, 0:1], 1.0 / D, ALU.mult, EPS, ALU.add)
        kb.tt("pool", st4[:, 2:3], st4[:, 1:2], neghalf[:, 0:1], ALU.pow)
        kb.ts("dve", xs_b[:], xt[:], st4[:, 2:3], ALU.mult)
        bank = P.next()
        pv = bfv(bank)
        for k in range(8):
            kb.tr(V(pv.ap[:, k * 128:(k + 1) * 128], pv.res), xs_b[:, k * 128:(k + 1) * 128], idb[:], inc=(k == 7))
        kb.tt("dve", xnT[par][:], V(pv.ap.rearrange("p (k t) -> p k t", k=8), pv.res),
              bc(un(C(nwname), 2), [128, 8, 128]), ALU.mult)

    pT = [kb.sb(f"pT{i}", [128, 12, 176], BF16, es0) for i in range(2)]
    hist = [kb.sb(f"hist{i}", [128, 12, 3], BF16, es0) for i in range(2)]
    for h_ in hist:
        kb.memset("pool", h_[:], 0.0)
    mixed = [kb.sb(f"mixed{i}", [128, 1536], BF16, es0) for i in range(2)]
    zs = [kb.sb(f"zs{i}", [128, 512], BF16, es0) for i in range(2)]
    ba = kb.sb("ba", [128, 16], F32, es0)
    sc8 = kb.sb("sc8", [128, 64], F32, es0)
    E = kb.sb("E", [128, 24], F32, es0)
    sqb = kb.sb("sqb", [128, 1024], BF16, es0)
    r8 = kb.sb("r8", [128, 16], F32, es0)
    sv = {nm: kb.sb("sv_" + nm, [128, 4, 128], BF16, es0) for nm in ("qn", "qe", "kn", "kw", "kd", "vb")}
    qT = kb.sb("qT", [128, 8, 128], BF16, es0)
    knT = kb.sb("knT", [128, 4, 128], BF16, es0)
    gm = kb.sb("gm", [128, 4, 128], F32R, es0)
    gb = kb.sb("gb", [128, 4, 128], F32R, es0)
    dec = kb.sb("dec", [128, 4, 128], F32, es0)
    decT = kb.sb("decT", [128, 4, 128], BF16, es0)
    Pb = [kb.sb(f"Pb{i}", [128, 4, 128], BF16, es0) for i in range(2)]
    PTb = [kb.sb(f"PTb{i}", [128, 4, 128], BF16, es0) for i in range(2)]
    Yb = [kb.sb(f"Yb{i}", [128, 4, 128], BF16, es0) for i in range(2)]
    Mb = kb.sb("Mb", [128, 4, 128], BF16, es0)
    MTb = kb.sb("MTb", [128, 4, 128], BF16, es0)
    negidb = kb.sb("negidb", [128, 128], BF16, es0)
    kb.ts("dve", negidb[:], idb[:], -1.0, ALU.mult)
    negWT = kb.sb("negWT", [128, 4, 128], BF16, es0)
    qkdT = kb.sb("qkdT", [128, 4, 128], BF16, es0)
    S = kb.sbs("S", [128, 8, 128], F32, 4, es0)
    Sbf = kb.sbs("Sbf", [128, 8, 128], BF16, 4, es0)
    ub = kb.sb("ub", [128, 4, 128], BF16, es0)
    otmp = kb.sb("otmp", [128, 512], BF16, es0)
    on = kb.sb("on", [128, 1024], BF16, es0)
    onT = kb.sb("onT", [128, 8, 128], BF16, es0)
    ch_out = kb.chan("out_small")
    ch_sbf = [kb.chan("sbf0"), kb.chan("sbf1")]
    ch_sf = [kb.chan("sf0"), kb.chan("sf1")]
    ch_sfo = [kb.chan("sfo0"), kb.chan("sfo1")]
    if do_sample:
        cv = kb.sb("cv", [128, 12, 128], BF16, es0)
        histT = kb.sb("histT", [128, 24, 48], BF16, es0)
        abc = kb.sb("abc", [128, 64], F32, es0)
        gmsk = kb.sb("gmsk", [128, 16, 4], F32, es0)
        uTb = kb.sb("uTb", [128, 4, 128], BF16, es0)

    beta = sc8[:, 0:8]
    negbeta = sc8[:, 8:16]
    gv = sc8[:, 24:32]

    def v3(tt_, h=4):
        return V(tt_.t[:].rearrange("p (h d) -> p h d", h=h), tt_.res)

    def sc_b(vw):
        return bc(un(vw, 2), [128, 4, 128])

    sc8_2 = [sc8, kb.sb("sc8_b", [128, 64], F32, es0)]
    E_2 = [E, kb.sb("E_b", [128, 24], F32, es0)]
    sv_2 = [sv, {nm: kb.sb("svb_" + nm, [128, 4, 128], BF16, es0) for nm in ("qn", "qe", "kn", "kw", "kd", "vb")}]
    qT_2 = [qT, kb.sb("qT_b", [128, 8, 128], BF16, es0)]
    knT_2 = [knT, kb.sb("knT_b", [128, 4, 128], BF16, es0)]
    osq = kb.sb("osq", [128, 512], BF16, es0)

    def gdn_prologue(ti, is_sample):
        sc8 = sc8_2[ti % 2]
        E = E_2[ti % 2]
        beta = sc8[:, 0:8]
        negbeta = sc8[:, 8:16]
        gv = sc8[:, 24:32]
        par = ti % 2
        src = ext(xs_d[:, :]) if is_sample else ext(xp_d[ti * 128:(ti + 1) * 128, :])
        front_end(src, "nwT0", par)
        xn = xnT[par]
        chk(2)
        bk = P.next()
        for k in range(8):
            kb.mm(bk[:, 0:16], xn[:, k, :], wia[k][:, 4096:4112], start=(k == 0), stop=(k == 7))
        kb.copy("dve", ba[:], bk[:, 0:16])
        kb.act(sc8[:, 56:64], ba[:, 0:8], AF.Tanh, scale=0.5)
        kb.ts("dve", negbeta, sc8[:, 56:64], -0.5, ALU.mult, -0.5, ALU.add)
        kb.ts("dve", beta, sc8[:, 56:64], 0.5, ALU.mult, 0.5, ALU.add)
        kb.tt("dve", sc8[:, 16:24], ba[:, 8:16], C("dtb"), ALU.add)
        kb.act(sc8[:, 16:24], sc8[:, 16:24], AF.Exp)
        kb.act(sc8[:, 16:24], sc8[:, 16:24], AF.Ln, bias=1.0)
        kb.tt("dve", gv, sc8[:, 16:24], negA[:], ALU.mult)
        sfx = "_s" if is_sample else "_p"
        bk = P.next()
        kb.mm(bk[:, 0:8], C("triU" + sfx), gv)
        kb.mm(bk[:, 8:16], C("blk" + sfx), gv)
        kb.mm(bk[:, 16:24], C("triSU" + sfx), gv)
        kb.act(E[:], bk[:, 0:24], AF.Exp)
        chk(3)


    def gdn_stage1a(ti, hg, is_sample):
        par = ti % 2
        xn = xnT[par]
        h0 = hg * 4
        pt = pT[hg]
        chunks = [h0 + i for i in range(4)] + [8 + h0 + i for i in range(4)] + [16 + h0 + i for i in range(4)]
        if is_sample:
            F4 = V(pt.t[:].rearrange("p c (s r) -> p c s r", r=11), pt.res)
            hT4 = V(histT.t[:].rearrange("p c (s r) -> p c s r", r=3), histT.res)
        else:
            kb.copy("pool", pt[:, :, 0:3], hist[hg][:])
        for grp in range(3):
            bk = P.next()
            for ci in range(4):
                col = chunks[grp * 4 + ci] * 128
                for k in range(8):
                    kb.mm(bk[:, ci * 128:(ci + 1) * 128], wia[k][:, col:col + 128], xn[:, k, :],
                          start=(k == 0), stop=(k == 7), inc=(k == 7 and ci == 3))
            eng = "act" if grp % 2 == 0 else "dve"
            if is_sample:
                c0 = chunks[grp * 4]
                kb.copy(eng, V(F4.ap[:, grp * 4:(grp + 1) * 4, :, 3:11], pt.res),
                        V(bk.t[:].rearrange("p (c s t) -> p c s t", c=4, s=16), bk.res))
                kb.copy("pool", V(F4.ap[:, grp * 4:(grp + 1) * 4, :, 0:3], pt.res),
                        V(hT4.ap[:, c0:c0 + 4, :, :], histT.res))
            else:
                kb.copy(eng, pt[:, grp * 4:(grp + 1) * 4, 3:131], v3(bk))
                yield
        if not is_sample:
            kb.copy("pool", hist[hg][:], pt[:, :, 128:131])
        mx = mixed[hg]
        if is_sample:
            for c in range(12):
                cg = chunks[c]
                cvv = V(cv.t[:, c, :].rearrange("p (s t) -> p s t", t=8), cv.res)
                kb.ts("dve", cvv, V(F4.ap[:, c, :, 0:8], pt.res), C("cwT", cg * 4, cg * 4 + 1), ALU.mult)
                for j in range(1, 4):
                    kb.stt(cvv, V(F4.ap[:, c, :, j:j + 8], pt.res), C("cwT", cg * 4 + j, cg * 4 + j + 1),
                           cvv, ALU.mult, ALU.add)
        for grp in range(3):
            bk = P.next()
            for ci in range(4):
                cg = chunks[grp * 4 + ci]
                if is_sample:
                    kb.mm(bk[:, ci * 128:(ci + 1) * 128], cv[:, grp * 4 + ci, :], idb[:], inc=(ci == 3))
                else:
                    for j in range(4):
                        kb.mm(bk[:, ci * 128:(ci + 1) * 128], pt[:, grp * 4 + ci, j:j + 128],
                              diag[:, cg * 4 + j, :], start=(j == 0), stop=(j == 3), inc=(j == 3 and ci == 3))
            kb.act(mx[:, grp * 512:(grp + 1) * 512], bk[:], AF.Silu)
            yield
        yield

    def gdn_stage1b(ti, hg, ii, is_sample):
        par = ti % 2
        xn = xnT[par]
        sc8 = sc8_2[ti % 2]
        E = E_2[ti % 2]
        sv, qT, knT = sv_2[ii % 2], qT_2[ii % 2], knT_2[ii % 2]
        sfx = "_s" if is_sample else "_p"
        h0 = hg * 4
        mx = mixed[hg]
        bk = P.next()
        for k in range(8):
            kb.mm(bk[:], xn[:, k, :], wia[k][:, 3072 + hg * 512:3072 + (hg + 1) * 512],
                  start=(k == 0), stop=(k == 7))
        kb.act(zs[hg][:], bk[:], AF.Silu)
        yield
        chk(4)
        kb.act(sqb[:], mx[:, 0:1024], AF.Square)
        kb.reduce_sum(r8[:, 0:8], v3(sqb, 8))
        yield
        kb.ts("dve", r8[:, 0:8], r8[:, 0:8], EPS, ALU.add)
        kb.tt("pool", r8[:, 8:16], r8[:, 0:8], neghalf[:, 0:8], ALU.pow)
        yield
        rq = r8[:, 8:12]
        rk = r8[:, 12:16]
        eG = E[:, h0:h0 + 4]
        ekl = E[:, 16 + h0:16 + h0 + 4]
        kb.ts("dve", sc8[:, 32:36], rq, 128 ** -0.5, ALU.mult)
        kb.tt("dve", sc8[:, 36:40], sc8[:, 32:36], eG, ALU.mult)
        kb.tt("dve", sc8[:, 40:44], rk, eG, ALU.mult)
        kb.tt("dve", sc8[:, 40:44], sc8[:, 40:44], sc8[:, h0:h0 + 4], ALU.mult)
        kb.tt("dve", sc8[:, 44:48], rk, ekl, ALU.mult)
        yield
        qv = V(mx.t[:, 0:512].rearrange("p (h d) -> p h d", h=4), mx.res)
        kv = V(mx.t[:, 512:1024].rearrange("p (h d) -> p h d", h=4), mx.res)
        vv = V(mx.t[:, 1024:1536].rearrange("p (h d) -> p h d", h=4), mx.res)
        kb.tt("dve", sv["qn"][:], qv, sc_b(sc8[:, 32:36]), ALU.mult)
        kb.tt("pool", sv["qe"][:], qv, sc_b(sc8[:, 36:40]), ALU.mult)
        yield
        kb.tt("dve", sv["kn"][:], kv, sc_b(rk), ALU.mult)
        kb.tt("pool", sv["kw"][:], kv, sc_b(sc8[:, 40:44]), ALU.mult)
        yield
        kb.tt("dve", sv["kd"][:], kv, sc_b(sc8[:, 44:48]), ALU.mult)
        kb.tt("pool", sv["vb"][:], vv, sc_b(sc8[:, h0:h0 + 4]), ALU.mult)
        yield
        bk = P.next()
        pv = bfv(bk)
        for i, nm in enumerate(("qn", "qe")):
            for h in range(4):
                c0 = (i * 4 + h) * 128
                kb.tr(V(pv.ap[:, c0:c0 + 128], pv.res), sv[nm][:, h, :], idb[:], inc=(i == 1 and h == 3))
        kb.copy("act", qT[:], V(pv.ap.rearrange("p (k t) -> p k t", k=8), pv.res))
        yield
        bk = P.next()
        pv = bfv(bk)
        for h in range(4):
            kb.tr(V(pv.ap[:, h * 128:(h + 1) * 128], pv.res), sv["kn"][:, h, :], idb[:], inc=(h == 3))
        kb.copy("dve", knT[:], V(pv.ap[:, 0:512].rearrange("p (k t) -> p k t", k=4), pv.res))

    def gdn_stage2(ti, hg, ii, is_sample):
        if is_sample:
            sample_prefetch(hg)
        sc8 = sc8_2[ti % 2]
        E = E_2[ti % 2]
        sv, qT, knT = sv_2[ii % 2], qT_2[ii % 2], knT_2[ii % 2]
        sfx = "_s" if is_sample else "_p"
        h0 = hg * 4
        mx = mixed[hg]
        chk(5)
        kb.tt("dve", gm[:], bc(un(sc8[:, 24 + h0:24 + h0 + 4], 2), [128, 4, 128]),
              bc(un(negtriUf[:], 1), [128, 4, 128]), ALU.mult)
        kb.copy("act", gb[:], bc(un(sc8[:, 24 + h0:24 + h0 + 4], 2), [128, 4, 128]))
        yield
        gmf = V(gm.t[:].rearrange("p h j -> p (h j)"), gm.res)
        gbf = V(gb.t[:].rearrange("p h j -> p (h j)"), gb.res)
        chk(51)
        bk = P.next()
        kb.mm(bk[:], mr["triU"][:], gbf, start=True, stop=False)
        kb.mm(bk[:], mr["ones"][:], gmf, start=False, stop=False)
        kb.mm(bk[:], mr["ident"][:], mr["maskneg"][:], start=False, stop=True)
        chk(52)
        kb.act(V(dec.t[:].rearrange("p h j -> p (h j)"), dec.res), bk[:], AF.Exp)
        yield
        chk(53)
        bk = P.next()
        kb.mm(bk[:], mr["negtriU"][:], gbf, start=True, stop=False)
        kb.mm(bk[:], mr["negones"][:], gmf, start=False, stop=False)
        kb.mm(bk[:], mr["ident"][:], mr["masknegT"][:], start=False, stop=True)
        chk(54)
        kb.act(V(decT.t[:].rearrange("p h j -> p (h j)"), decT.res), bk[:], AF.Exp)
        yield
        chk(55)
        chk(6)
        bk = P.next()
        for h in range(4):
            kb.mm(bk[:, h * 128:(h + 1) * 128], knT[:, h, :], knT[:, h, :], inc=(h == 3))
        kb.tt("dve", dec[:], dec[:], bc(un(C("strict" + sfx), 1), [128, 4, 128]), ALU.mult)
        kb.tt("dve", dec[:], dec[:], sc_b(sc8[:, 8 + h0:8 + h0 + 4]), ALU.mult)
        yield
        Pc, PTc, Yc = Mb, MTb, Yb[0]
        kb.tt("dve", Pc[:], v3(bk), dec[:], ALU.mult)
        yield
        chk(61)
        bk = P.next()
        pv = bfv(bk)
        for h in range(4):
            kb.tr(V(pv.ap[:, h * 128:(h + 1) * 128], pv.res), Pc[:, h, :], idb[:], inc=(h == 3))
        pv4 = V(pv.ap[:, 0:512].rearrange("p (k t) -> p k t", k=4), pv.res)
        kb.copy("act", PTc[:], pv4)
        kb.tt("dve", Yc[:], pv4, bc(un(idb[:], 1), [128, 4, 128]), ALU.add)
        yield
        chk(62)
        nsteps = 3 if is_sample else 6
        for stp in range(1, nsteps):
            Pn, PTn, Yn = Pb[stp % 2], PTb[stp % 2], Yb[stp % 2]
            bkA = P.next()
            for h in range(4):
                kb.mm(bkA[:, h * 128:(h + 1) * 128], PTc[:, h, :], Pc[:, h, :], inc=(h == 3))
            last = (stp == nsteps - 1)
            if not last:
                bkB = P.next()
                for h in range(4):
                    kb.mm(bkB[:, h * 128:(h + 1) * 128], Pc[:, h, :], PTc[:, h, :], inc=(h == 3))
            kb.copy("act", Pn[:], v3(bkA))
            if not last:
                kb.copy("dve", PTn[:], v3(bkB))
                yield
            bkC = P.next()
            for h in range(4):
                kb.mm(bkC[:, h * 128:(h + 1) * 128], Pn[:, h, :], Yc[:, h, :], inc=(h == 3))
            kb.tt("dve", Yn[:], v3(bkC), Yc[:], ALU.add)
            yield
            Pc, PTc, Yc = Pn, PTn, Yn
        bk = P.next()
        pv = bfv(bk)
        for h in range(4):
            kb.tr(V(pv.ap[:, h * 128:(h + 1) * 128], pv.res), Yc[:, h, :], idb[:], inc=(h == 3))
        X0b, Rb = PTb[0], Pb[0]
        kb.copy("act", X0b[:], V(pv.ap[:, 0:512].rearrange("p (k t) -> p k t", k=4), pv.res))
        yield
        bk = P.next()
        for h in range(4):
            kb.mm(bk[:, h * 128:(h + 1) * 128], MTb[:, h, :], X0b[:, h, :], start=True, stop=False)
            kb.mm(bk[:, h * 128:(h + 1) * 128], negidb[:], X0b[:, h, :], start=False, stop=True, inc=(h == 3))
        kb.tt("dve", Rb[:], v3(bk), bc(un(idb[:], 1), [128, 4, 128]), ALU.add)
        yield
        bk = P.next()
        for h in range(4):
            kb.mm(bk[:, h * 128:(h + 1) * 128], Rb[:, h, :], Yc[:, h, :], inc=(h == 3))
        Yn = Yb[1] if Yc is Yb[0] else Yb[0]
        kb.tt("dve", Yn[:], v3(bk), Yc[:], ALU.add)
        Yc = Yn
        chk(63)
        bk = P.next()
        for h in range(4):
            kb.mm(bk[:, h * 128:(h + 1) * 128], sv["kw"][:, h, :], Yc[:, h, :], inc=(h == 3))
        kb.act(negWT[:], v3(bk), AF.Copy, scale=-1.0)
        yield
        bk = P.next()
        for h in range(4):
            kb.mm(bk[:, h * 128:(h + 1) * 128], knT[:, h, :], qT[:, h, :], inc=(h == 3))
        kb.tt("dve", qkdT[:], v3(bk), decT[:], ALU.mult)
        yield
        if dbg and ti == dbg_tile and hg == 0:
            kb.dma("sp", ext(dbg_d["d_misc"][:, 0:64]), sc8[:], ch_dbg)
            kb.dma("sp", ext(dbg_d["d_dec"]), V(dec.t[:].rearrange("p h j -> p (h j)"), dec.res), ch_dbg)
        chk(7)
        if not is_sample:
            first = (ti == 0)
            bu = P.next()
            for h in range(4):
                kb.mm(bu[:, h * 128:(h + 1) * 128], Yc[:, h, :], sv["vb"][:, h, :], start=True, stop=first,
                      inc=(first and h == 3))
                if not first:
                    kb.mm(bu[:, h * 128:(h + 1) * 128], negWT[:, h, :], Sbf[:, h0 + h, :], start=False,
                          stop=True, inc=(h == 3))
            kb.copy("act", ub[:], v3(bu))
            yield
            bo = P.next()
            for h in range(4):
                if not first:
                    kb.mm(bo[:, h * 128:(h + 1) * 128], qT[:, 4 + h, :], Sbf[:, h0 + h, :], start=True,
                          stop=False)
                kb.mm(bo[:, h * 128:(h + 1) * 128], qkdT[:, h, :], ub[:, h, :], start=first, stop=True,
                      inc=(h == 3))
            bs = P.next()
            for h in range(4):
                kb.mm(bs[:, h * 128:(h + 1) * 128], sv["kd"][:, h, :], ub[:, h, :], inc=(h == 3))
            for h in range(4):
                if first:
                    kb.copy("dve", S[:, h0 + h, :], bs[:, h * 128:(h + 1) * 128])
                else:
                    kb.stt(S[:, h0 + h, :], S[:, h0 + h, :], E[:, 8 + h0 + h:8 + h0 + h + 1],
                           bs[:, h * 128:(h + 1) * 128], ALU.mult, ALU.add)
            kb.copy("act", Sbf[:, h0:h0 + 4, :], S[:, h0:h0 + 4, :])
            yield
            o_src = bo[:]
        else:
            o_src = gdn_sample_rec(hg, Yc, sv, qT, sc8)
        chk(8)
        o3 = V(o_src.ap.rearrange("p (h d) -> p h d", h=4), o_src.res)
        kb.act(osq[:], o_src, AF.Square)
        kb.reduce_sum(sc8[:, 48:52], V(osq.t[:].rearrange("p (h d) -> p h d", h=4), osq.res))
        yield
        kb.ts("dve", sc8[:, 48:52], sc8[:, 48:52], 1.0 / 128, ALU.mult, EPS, ALU.add)
        kb.tt("pool", sc8[:, 52:56], sc8[:, 48:52], neghalf[:, 0:4], ALU.pow)
        yield
        kb.tt("dve", v3(otmp), o3, sc_b(sc8[:, 52:56]), ALU.mult)
        kb.tt("dve", on[:, hg * 512:(hg + 1) * 512], otmp[:], zs[hg][:], ALU.mult)
        if dbg and ti == dbg_tile and hg == 0:
            kb.dma("pool", ext(dbg_d["d_mixed"]), mx[:], ch_dbg)
            kb.dma("pool", ext(dbg_d["d_Y"]), otmp[:], ch_dbg)

    def gdn_epilogue(ti, is_sample):
        par = ti % 2
        xn = xnT[par]
        src = ext(xs_d[:, :]) if is_sample else ext(xp_d[ti * 128:(ti + 1) * 128, :])
        kb.dma("sp", yt[:], src, ch_yt)
        chk(9)
        bk = P.next()
        pv = bfv(bk)
        for k in range(8):
            kb.tr(V(pv.ap[:, k * 128:(k + 1) * 128], pv.res), on[:, k * 128:(k + 1) * 128], idb[:], inc=(k == 7))
        kb.copy("act", onT[:], V(pv.ap.rearrange("p (k t) -> p k t", k=8), pv.res))
        for n in range(2):
            bk = P.next()
            for k in range(8):
                kb.mm(bk[:], onT[:, k, :], woa[k][:, n * 512:(n + 1) * 512], start=(k == 0), stop=(k == 7))
            kb.tt("dve", yt[:, n * 512:(n + 1) * 512], bk[:], yt[:, n * 512:(n + 1) * 512], ALU.add)
        kb.dma("sp", V(x1_d[ti * 128:(ti + 1) * 128, :], x1_res[ti]), yt[:], ch_yt)
        if dbg:
            dr = n_ptiles if is_sample else ti
            kb.dma("sp", ext(dbg_d["d_y1"][dr * 128:(dr + 1) * 128, :]), yt[:], ch_yt)
        if is_sample or ti == n_ptiles - 1:
            for cb in range(3):
                for half in range(2):
                    bk = P.next()
                    col = cb * 1024 + half * 512
                    for k in range(8):
                        kb.mm(bk[:], xn[:, k, :], wia[k][:, col:col + 512], start=(k == 0), stop=(k == 7))
                    kb.copy("act" if half else "dve", xt[:, half * 512:(half + 1) * 512], bk[:])
                if is_sample:
                    for s_ in range(NSEQ):
                        kb.dma("sp", ext(cas_d[s_ * 3:(s_ + 1) * 3, cb * 1024:(cb + 1) * 1024]),
                               xt[s_ * 8 + 5:s_ * 8 + 8, :], ch_xt)
                else:
                    kb.dma("sp", ext(cap_d[:, cb * 1024:(cb + 1) * 1024]), xt[125:128, :], ch_xt)
        if (not is_sample) and ti == n_ptiles - 1:
            kb.dma("sp", ext(sap_d.rearrange("h k v -> k h v")), S[:], ch_out)


    smpst = {}

    def vi(v, *idx):
        return V(v.ap[idx], v.res)

    def sample_slots():
        if smpst:
            return smpst
        bsl, fsl = [], []
        for i in range(16):
            r = Res(f"smb{i}")
            r.w = diag.res.w
            r.r = dict(diag.res.r)
            bsl.append(V(diag.t[:, i * 4:(i + 1) * 4, :], r))
        for i in range(4):
            r = Res(f"smf{i}")
            r.w = diag.res.w
            r.r = dict(diag.res.r)
            ap = diag.t[:, 64 + i * 8:64 + (i + 1) * 8, :].rearrange("p a b -> p (a b)").bitcast(F32)
            fsl.append(V(ap.rearrange("p (h d) -> p h d", h=4), r))
        smpst["b"] = bsl
        smpst["f"] = fsl
        smpst["chb"] = [kb.chan(f"smb{i}") for i in range(16)]
        smpst["chf"] = [kb.chan(f"smf{i}") for i in range(4)]
        smpst["chfo"] = [kb.chan(f"smfo{i}") for i in range(4)]
        return smpst

    def sample_ld_f(hg, q_):
        if q_ >= NSEQ:
            return
        sm = sample_slots()
        h0 = hg * 4
        kb.dma("sp", sm["f"][q_ % 4], ext(sg_d[q_, h0:h0 + 4].rearrange("h k v -> k h v")), sm["chf"][q_ % 4])

    def sample_prefetch(hg):
        sm = sample_slots()
        h0 = hg * 4
        for q_ in range(NSEQ):
            kb.dma("pool", sm["b"][q_], ext(sg_d[q_, h0:h0 + 4].rearrange("h k v -> k h v")), sm["chb"][q_])
        for q_ in range(3):
            sample_ld_f(hg, q_)

    def gdn_sample_rec(hg, Yc, sv, qT, sc8):
        sm = sample_slots()
        h0 = hg * 4
        kb.tt("dve", gmsk[:], bc(un(sc8[:, 24 + h0:24 + h0 + 4], 1), [128, 16, 4]),
              bc(un(C("seqmask"), 2), [128, 16, 4]), ALU.mult)
        bk = P.next()
        kb.mm(bk[:, 0:64], onesf[:], V(gmsk.t[:].rearrange("p s h -> p (s h)"), gmsk.res))
        kb.act(abc[:], bk[:, 0:64], AF.Exp)
        buT = P.next(hold=True)
        for h in range(4):
            kb.mm(buT[:, h * 128:(h + 1) * 128], sv["vb"][:, h, :], Yc[:, h, :], start=(h == 0), stop=False,
                  inc=False)
        for s_ in range(NSEQ):
            for h in range(4):
                lastmm = (s_ == NSEQ - 1 and h == 3)
                kb.mm(buT[:, h * 128 + s_ * 8:h * 128 + s_ * 8 + 8], vi(sm["b"][s_], slice(None), h, slice(None)),
                      negWT[:, h, s_ * 8:s_ * 8 + 8], start=False, stop=lastmm, inc=(h == 3))
        kb.copy("act", uTb[:], v3(buT))
        P.release(buT)
        bk = P.next()
        pv = bfv(bk)
        for h in range(4):
            kb.tr(V(pv.ap[:, h * 128:(h + 1) * 128], pv.res), uTb[:, h, :], idb[:], inc=(h == 3))
        kb.copy("dve", ub[:], V(pv.ap[:, 0:512].rearrange("p (k t) -> p k t", k=4), pv.res))
        boT = P.next(hold=True)
        for h in range(4):
            kb.mm(boT[:, h * 128:(h + 1) * 128], ub[:, h, :], qkdT[:, h, :], start=(h == 0), stop=False, inc=False)
        for s_ in range(NSEQ):
            sample_ld_f(hg, s_ + 3)
            fs_ = sm["f"][s_ % 4]
            for h in range(4):
                lastmm = (s_ == NSEQ - 1 and h == 3)
                kb.mm(boT[:, h * 128 + s_ * 8:h * 128 + s_ * 8 + 8], vi(sm["b"][s_], slice(None), h, slice(None)),
                      qT[:, 4 + h, s_ * 8:s_ * 8 + 8], start=False, stop=lastmm, inc=(h == 3))
            kdm = Pb[s_ % 2]
            kb.ts("dve", kdm[:], sv["kd"][:], C("seqmask", s_, s_ + 1), ALU.mult)
            bs = P.next()
            for h in range(4):
                kb.mm(bs[:, h * 128:(h + 1) * 128], kdm[:, h, :], ub[:, h, :], inc=(h == 3))
            for h in range(4):
                fh = vi(fs_, slice(None), h, slice(None))
                kb.stt(fh, fh, abc[:, s_ * 4 + h:s_ * 4 + h + 1], bs[:, h * 128:(h + 1) * 128], ALU.mult, ALU.add)
            kb.dma("act", ext(sas_d[s_, h0:h0 + 4].rearrange("h k v -> k h v")), fs_, sm["chfo"][s_ % 4])
        kb.copy("act", uTb[:], v3(boT))
        P.release(boT)
        bk = P.next()
        pv = bfv(bk)
        for h in range(4):
            kb.tr(V(pv.ap[:, h * 128:(h + 1) * 128], pv.res), uTb[:, h, :], idb[:], inc=(h == 3))
        return V(pv.ap[:, 0:512], pv.res)

    def sample_hist_setup():
        for piece in range(3):
            kb.dma("sp", yt[0:48, :], ext(sc_d[:, piece * 1024:(piece + 1) * 1024]), ch_yt)
            bk = P.next()
            for c in range(8):
                kb.tr(bk[:, c * 48:(c + 1) * 48], yt[0:48, c * 128:(c + 1) * 128], V(identf.ap[0:48, 0:48], identf.res),
                      inc=(c == 7))
            kb.copy("dve", histT[:, piece * 8:(piece + 1) * 8, :],
                    V(bk.t[:, 0:384].rearrange("p (c r) -> p c r", c=8), bk.res))

    dbg_tile = 1 if n_ptiles > 1 else 0
    try:
        load_masks("p")
        chk(1)
        if do_sample:
            sample_hist_setup()
        tiles = [(ti, False) for ti in range(n_ptiles)] + ([(NPT, True)] if do_sample else [])
        items = [(ti, hg, smp) for (ti, smp) in tiles for hg in range(2)]
        gdn_prologue(tiles[0][0], tiles[0][1])
        masks_s = False
        def gseq(g, f):
            if g is not None:
                yield from g
            f()
            yield

        def drive(g1, g2, r2=1):
            a1 = g1 is not None
            a2 = g2 is not None
            while a1 or a2:
                for _ in range(r2):
                    if a2:
                        try:
                            next(g2)
                        except StopIteration:
                            a2 = False
                if a1:
                    try:
                        next(g1)
                    except StopIteration:
                        a1 = False
        def gchain(parts):
            for p_ in parts:
                if callable(p_):
                    p_()
                    yield
                else:
                    yield from p_
        n_it = len(items)
        for _ in gdn_stage1a(items[0][0], items[0][1], items[0][2]):
            pass
        for r in range(n_it + 1):
            parts = []
            if r < n_it:
                ti, hg, smp_ = items[r]
                parts.append(gdn_stage1b(ti, hg, r, smp_))
            if r + 1 < n_it:
                ti1, hg1, smp1 = items[r + 1]
                if hg1 == 0:
                    parts.append(lambda a=ti1, b=smp1: gdn_prologue(a, b))
                parts.append(gdn_stage1a(ti1, hg1, smp1))
            g1 = gchain(parts) if parts else None
            g2 = None
            if r >= 1:
                ti2, hg2, smp2 = items[r - 1]
                if smp2 and not masks_s:
                    load_masks("s")
                    masks_s = True
                g2 = gdn_stage2(ti2, hg2, r - 1, smp2)
                if hg2 == 1:
                    g2 = gseq(g2, lambda a=ti2, b=smp2: gdn_epilogue(a, b))
            drive(g1, g2, L0_R2)
    except Cut:
        pass

    if not do_l1:
        kb.final_wait()
        es0.close()
        return nc

    kb.barrier()
    es0.close()
    es1 = ExitStack()
    ch_w1 = kb.chan("w1")
    wib = [kb.sb(f"wib{k}", [128, 6144], BF16, es1) for k in range(8)]
    for k in range(8):
        for (c0, c1) in ((0, 2048), (2048, 4096), (4096, 6144)):
            kb.dma("pool", wib[k][:, c0:c1], ext(wib_d[k * 128:(k + 1) * 128, c0:c1]), ch_w1)
    wob = [kb.sb(f"wob{k}", [128, 1024], BF16, es1) for k in range(16)]
    for k in range(16):
        st, chn = (xt, ch_xt) if k % 2 == 0 else (yt, ch_yt)
        kb.dma("sp", st[:], ext(wob_d[k * 128:(k + 1) * 128, :]), chn)
        kb.ts("dve", wob[k][:], st[:], C("onwbT", k, k + 1), ALU.mult)
    for k in range(8):
        wib[k].res.w = (ch_w1.sem, ch_w1.count, "dma")
    fnw = kb.sb("fnw", [128, D], F32, es1)
    ch_fnw = kb.chan("fnw")
    ch_dm = kb.chan("dm")
    ch_dms = kb.chan("dms")
    kb.dma("sp", fnw[:], ext(fnw_d[0:1, :].broadcast_to([128, D])), ch_fnw)
    dm = kb.sb("dm", [128, 4, 128], F32, es1)
    rot = kb.sb("rot", [128, 2, 128], F32, es1)
    ch_rot = kb.chan("rot")
    tmpA = kb.sb("tmpA", [128, 2, 128], F32, es1)
    tmpB = kb.sb("tmpB", [128, 2, 128], F32, es1)
    tmpC = kb.sb("tmpC", [128, 2, 128], F32, es1)
    tmpD = kb.sb("tmpD", [128, 2, 128], F32, es1)
    qkr = kb.sb("qkr", [128, 2, 2, 128], BF16, es1)
    qd = kb.sb("qd", [128, 256], BF16, es1)
    kdd = kb.sb("kdd", [128, 256], BF16, es1)
    qkT = kb.sb("qkT", [128, 6, 128], BF16, es1)
    vb1 = kb.sb("vb1", [128, 512], BF16, es1)
    gs1 = kb.sb("gs1", [128, 512], BF16, es1)
    qkd1 = kb.sb("qkd1", [128, 128], BF16, es1)
    S1 = kb.sbs("S1", [128, 4, 2, 512], F32, 1, es1)
    S1b = kb.sbs("S1b", [128, 4, 2, 512], BF16, 1, es1)
    on1 = kb.sb("on1", [128, 2048], BF16, es1)
    on1T = kb.sb("on1T", [128, 16, 128], BF16, es1)
    st1 = kb.sb("st1", [128, 8], F32, es1)
    zb = [kb.sb(f"zb{i}", [128, 2, 128], BF16, es1) for i in range(2)]
    kddm = [kb.sb(f"kddm{i}", [128, 256], BF16, es1) for i in range(2)]
    for z_ in zb:
        kb.memset("pool", z_[:], 0.0)
    ch_s1 = [kb.chan(f"s1_{i}") for i in range(4)]
    ch_s1o = [kb.chan(f"s1o_{i}") for i in range(4)]
    gam = [1.0 - 2.0 ** (-5.0 - h) for h in range(4)]

    kdd2 = [kdd, kb.sb("kdd_b", [128, 256], BF16, es1)]
    qkT2 = [qkT, kb.sb("qkT_b", [128, 6, 128], BF16, es1)]
    vb12 = [vb1, kb.sb("vb1_b", [128, 512], BF16, es1)]
    gs12 = [gs1, kb.sb("gs1_b", [128, 512], BF16, es1)]
    rot2 = [rot, kb.sb("rot_b", [128, 2, 128], F32, es1)]
    ch_rot2 = [ch_rot, kb.chan("rot_b")]
    dms = kb.sb("dm_s", [128, 4, 128], F32, es1)

    def ret_prologue(ti, is_sample):
        src = V(x1_d[ti * 128:(ti + 1) * 128, :], x1_res[ti])
        front_end(src, "nwT1", ti % 2)
        kb.dma("sp", rot2[ti % 2][:], ext(rot_d[ti]), ch_rot2[ti % 2])

    def ret_stage1(ti, h, ii, is_sample):
        xn = xnT[ti % 2]
        rt = rot2[ti % 2]
        kdd_, qkT_, vb_, gs_ = kdd2[ii % 2], qkT2[ii % 2], vb12[ii % 2], gs12[ii % 2]
        rd = "rdec_s" if is_sample else "rdec_p"
        bk = P.next()
        for part, c0 in ((0, h * 256), (1, 1024 + h * 256)):
            for k in range(8):
                kb.mm(bk[:, part * 256:(part + 1) * 256], xn[:, k, :], wib[k][:, c0:c0 + 256],
                      start=(k == 0), stop=(k == 7), inc=(k == 7 and part == 1))
        pq = V(bk.t[:].rearrange("p (a b d) -> p a b d", a=2, b=2), bk.res)
        x1v = V(pq.ap[:, :, 0, :], bk.res)
        x2v = V(pq.ap[:, :, 1, :], bk.res)
        cosb = bc(un(rt[:, 0, :], 1), [128, 2, 128])
        sinb = bc(un(rt[:, 1, :], 1), [128, 2, 128])
        kb.tt("dve", tmpA[:], x1v, cosb, ALU.mult)
        kb.tt("dve", tmpB[:], x2v, sinb, ALU.mult)
        kb.tt("dve", V(qkr.t[:, :, 0, :], qkr.res), tmpA[:], tmpB[:], ALU.subtract)
        yield
        kb.tt("dve", tmpC[:], x1v, sinb, ALU.mult)
        kb.tt("dve", tmpD[:], x2v, cosb, ALU.mult)
        kb.tt("dve", V(qkr.t[:, :, 1, :], qkr.res), tmpC[:], tmpD[:], ALU.add)
        qr = V(qkr.t[:, 0, :, :].rearrange("p b d -> p (b d)"), qkr.res)
        kr = V(qkr.t[:, 1, :, :].rearrange("p b d -> p (b d)"), qkr.res)
        kb.ts("dve", qd[:], qr, C(rd, h, h + 1), ALU.mult)
        kb.ts("dve", kdd_[:], kr, C(rd, 4 + h, 5 + h), ALU.mult)
        yield
        bk = P.next()
        for k in range(8):
            kb.mm(bk[:], xn[:, k, :], wib[k][:, 2048 + h * 512:2048 + (h + 1) * 512], start=(k == 0), stop=(k == 7))
        kb.copy("act", vb_[:], bk[:])
        yield
        bk = P.next()
        for k in range(8):
            kb.mm(bk[:], xn[:, k, :], wib[k][:, 4096 + h * 512:4096 + (h + 1) * 512], start=(k == 0), stop=(k == 7))
        kb.act(gs_[:], bk[:], AF.Silu)
        yield
        bk = P.next()
        pv = bfv(bk)
        srcs = [qkr[:, 0, 0, :], qkr[:, 0, 1, :], qkr[:, 1, 0, :], qkr[:, 1, 1, :], qd[:, 0:128], qd[:, 128:256]]
        for i_, sv_ in enumerate(srcs):
            kb.tr(V(pv.ap[:, i_ * 128:(i_ + 1) * 128], pv.res), sv_, idb[:], inc=(i_ == 5))
        kb.copy("act", qkT_[:], V(pv.ap[:, 0:768].rearrange("p (k t) -> p k t", k=6), pv.res))
        yield

    def ret_stage2(ti, h, ii, is_sample, n_tiles_first):
        kdd_, qkT_, vb_, gs_ = kdd2[ii % 2], qkT2[ii % 2], vb12[ii % 2], gs12[ii % 2]
        first = (ti == 0)
        dmk = dms if is_sample else dm
        bk = P.next()
        kb.mm(bk[:, 0:128], qkT_[:, 2, :], qkT_[:, 0, :], start=True, stop=False)
        kb.mm(bk[:, 0:128], qkT_[:, 3, :], qkT_[:, 1, :], start=False, stop=True)
        kb.tt("dve", qkd1[:], bk[:, 0:128], dmk[:, h, :], ALU.mult)
        yield
        bo = P.next(hold=True)
        if not is_sample:
            kb.mm(bo[:], qkd1[:], vb_[:], start=True, stop=first)
            if not first:
                kb.mm(bo[:], qkT_[:, 4, :], S1b[:, h, 0, :], start=False, stop=False)
                kb.mm(bo[:], qkT_[:, 5, :], S1b[:, h, 1, :], start=False, stop=True)
            for c in range(2):
                bs = P.next()
                kb.mm(bs[:], kdd_[:, c * 128:(c + 1) * 128], vb_[:])
                if first:
                    kb.copy("dve", S1[:, h, c, :], bs[:])
                else:
                    kb.stt(S1[:, h, c, :], S1[:, h, c, :], float(gam[h] ** 128), bs[:], ALU.mult, ALU.add)
            kb.copy("act", S1b[:, h, :, :], S1[:, h, :, :])
            yield
        else:
            kb.mm(bo[:], qkd1[:], vb_[:], start=True, stop=False, inc=False)

            def ld1(pi):
                if pi >= 4 * NSEQ:
                    return
                hh, ss = divmod(pi, NSEQ)
                kb.dma("sp", S1[:, pi % 4, :, :], ext(sr_d[ss, hh].rearrange("(c p) v -> p c v", p=128)),
                       ch_s1[pi % 4])
            def cast1(pi_):
                if pi_ < 4 * NSEQ:
                    kb.copy("act", S1b[:, pi_ % 4, :, :], S1[:, pi_ % 4, :, :])
            if h == 0:
                ld1(0)
                ld1(1)
                cast1(0)
            for s_ in range(NSEQ):
                pi = h * NSEQ + s_
                sl = pi % 4
                ld1(pi + 2)
                z_ = zb[s_ % 2]
                kb.copy("dve", z_[:, :, s_ * 8:s_ * 8 + 8], qkT_[:, 4:6, s_ * 8:s_ * 8 + 8])
                kb.mm(bo[:], z_[:, 0, :], S1b[:, sl, 0, :], start=False, stop=False, inc=False)
                kb.mm(bo[:], z_[:, 1, :], S1b[:, sl, 1, :], start=False, stop=(s_ == NSEQ - 1), inc=True)
                kb.memset("dve", z_[:, :, s_ * 8:s_ * 8 + 8], 0.0)
                km = kddm[s_ % 2]
                kb.ts("dve", km[:], kdd_[:], C("seqmask", s_, s_ + 1), ALU.mult)
                bss = []
                for c in range(2):
                    bs = P.next()
                    kb.mm(bs[:], km[:, c * 128:(c + 1) * 128], vb_[:])
                    bss.append(bs)
                cast1(pi + 1)
                for c in range(2):
                    kb.stt(S1[:, sl, c, :], S1[:, sl, c, :], float(gam[h] ** 8), bss[c][:], ALU.mult, ALU.add)
                kb.dma("act", ext(sbs_d[s_, h].rearrange("(c p) v -> p c v", p=128)), S1[:, sl, :, :], ch_s1o[sl])
                yield
        kb.act(xs_b[:, 0:512], bo[:], AF.Square, accum=st1[:, 0:1])
        kb.ts("dve", st1[:, 1:2], st1[:, 0:1], 1.0 / 512, ALU.mult, EPS, ALU.add)
        kb.tt("pool", st1[:, 2:3], st1[:, 1:2], neghalf[:, 0:1], ALU.pow)
        yield
        kb.stt(on1[:, h * 512:(h + 1) * 512], bo[:], st1[:, 2:3], gs_[:], ALU.mult, ALU.mult)
        P.release(bo)
        yield

    def ret_epilogue(ti, is_sample, last_prompt):
        src = V(x1_d[ti * 128:(ti + 1) * 128, :], x1_res[ti])
        kb.dma("sp", yt[:], src, ch_yt)
        for g_ in range(2):
            bk = P.next()
            pv = bfv(bk)
            for k in range(8):
                kk_ = g_ * 8 + k
                kb.tr(V(pv.ap[:, k * 128:(k + 1) * 128], pv.res), on1[:, kk_ * 128:(kk_ + 1) * 128], idb[:], inc=(k == 7))
            kb.copy("act" if g_ else "dve", on1T[:, g_ * 8:(g_ + 1) * 8, :], V(pv.ap.rearrange("p (k t) -> p k t", k=8), pv.res))
        for n in range(2):
            bk = P.next()
            for k in range(16):
                kb.mm(bk[:], on1T[:, k, :], wob[k][:, n * 512:(n + 1) * 512], start=(k == 0), stop=(k == 15))
            kb.tt("dve", yt[:, n * 512:(n + 1) * 512], bk[:], yt[:, n * 512:(n + 1) * 512], ALU.add)
        kb.act(xs_b[:], yt[:], AF.Square, accum=st1[:, 4:5])
        kb.ts("dve", st1[:, 5:6], st1[:, 4:5], 1.0 / D, ALU.mult, EPS, ALU.add)
        kb.tt("pool", st1[:, 6:7], st1[:, 5:6], neghalf[:, 0:1], ALU.pow)
        kb.stt(yt[:], yt[:], st1[:, 6:7], fnw[:], ALU.mult, ALU.mult)
        dst = ext(ys_d[:, :]) if is_sample else ext(yp_d[ti * 128:(ti + 1) * 128, :])
        kb.dma("sp", dst, yt[:], ch_yt)
        if last_prompt:
            kb.dma("sp", ext(sbp_d.rearrange("h (c p) v -> p h c v", p=128)), S1[:], ch_out)

    try:
        kb.dma("sp", dm[:], ext(dmask_d[0]), ch_dm)
        kb.dma("sp", dms[:], ext(dmask_d[1]), ch_dms)
        tiles = [(ti, False) for ti in range(n_ptiles)] + ([(NPT, True)] if do_sample else [])
        items = [(ti, h, smp) for (ti, smp) in tiles for h in range(4)]
        ret_prologue(tiles[0][0], tiles[0][1])
        for ii in range(len(items) + 1):
            g1 = g2 = None
            if ii < len(items):
                ti, h, smp = items[ii]
                g1 = ret_stage1(ti, h, ii, smp)
                if h == 1:
                    nxt = [tt_ for tt_ in tiles if tt_[0] > ti]
                    if nxt:
                        g1 = gseq(g1, lambda a=nxt[0][0], b=nxt[0][1]: ret_prologue(a, b))
            if ii >= 1:
                ti2, h2, smp2 = items[ii - 1]
                g2 = ret_stage2(ti2, h2, ii - 1, smp2, None)
                if h2 == 3:
                    g2 = gseq(g2, lambda a=ti2, b=smp2: ret_epilogue(a, b, (not b) and a == n_ptiles - 1))
            drive(g1, g2)
    except Cut:
        pass
    kb.final_wait()
    es1.close()
    return nc


def _rot_tables():
    half = 128
    inv = 1.0 / (10000.0 ** np.linspace(0.0, 1.0, half))
    gam = 1.0 - 2.0 ** (-5.0 - np.arange(4))
    rot = np.zeros((NPT + 1, 128, 2, 128), np.float32)
    for ti in range(NPT + 1):
        if ti < NPT:
            pos = ti * 128 + np.arange(128, dtype=np.float64)
        else:
            pos = 16384.0 + (np.arange(128) % 8).astype(np.float64)
        ang = pos[:, None] * inv[None, :]
        rot[ti, :, 0, :] = np.cos(ang)
        rot[ti, :, 1, :] = np.sin(ang)
    return rot


def make_in_maps(I):
    f = np.float32
    cst = _build_cst(I["norm_w"], I["conv_w_a"], I["a_log_a"], I["dt_bias_a"], I["onorm_a"], I["onorm_b"])
    idb = np.eye(128, dtype=np.float32).astype(ml_dtypes.bfloat16)
    rot = _rot_tables()
    gam = 1.0 - 2.0 ** (-5.0 - np.arange(4, dtype=np.float64))
    dmask = np.zeros((2, 128, 4, 128), np.float32)
    ii = np.arange(128)
    for bi, blk in enumerate((128, 8)):
        same = (ii[:, None] // blk) == (ii[None, :] // blk)
        dif = ii[None, :] - ii[:, None]
        ok = (dif >= 0) & same
        for h in range(4):
            dmask[bi, :, h, :] = np.where(ok, gam[h] ** np.maximum(dif, 0) * 256.0 ** -0.5, 0.0)
    common = dict(
        wia=np.ascontiguousarray(I["w_in_a"][0], f), woa=np.ascontiguousarray(I["w_out_a"][0], f),
        wib=np.ascontiguousarray(I["w_in_b"][0], f), wob=np.ascontiguousarray(I["w_out_b"][0], f),
        cst=cst, idb=idb, fnw=np.ascontiguousarray(I["final_norm_w"].reshape(1, D), f), rot=rot, dmask=dmask)
    maps = []
    for c in range(NCORES):
        m = dict(common)
        m["xp"] = np.ascontiguousarray(I["x_prompt"][c], f)
        m["xs"] = np.ascontiguousarray(I["x_sample"][c * NSEQ:(c + 1) * NSEQ].reshape(128, D), f)
        m["sg"] = np.ascontiguousarray(I["state_gdn_ssm"][0, c * NSEQ:(c + 1) * NSEQ], f)
        m["sc"] = np.ascontiguousarray(I["state_gdn_conv"][0, c * NSEQ:(c + 1) * NSEQ].reshape(48, 3072), f)
        m["sr"] = np.ascontiguousarray(I["state_ret"][0, c * NSEQ:(c + 1) * NSEQ], f)
        maps.append(m)
    return maps


_NC_CACHE = {}


def kernel(**inputs):
    I = {k: np.asarray(v) for k, v in inputs.items()}
    if "nc" not in _NC_CACHE:
        _NC_CACHE["nc"] = build_program()
    nc = _NC_CACHE["nc"]
    in_maps = make_in_maps(I)
    res = run_bass_kernel_spmd(nc, in_maps, core_ids=list(range(NCORES))).results
    f = np.float32
    yp = np.stack([res[c]["yp"] for c in range(NCORES)]).astype(f)
    ys = np.concatenate([res[c]["ys"].reshape(NSEQ, 8, D) for c in range(NCORES)], 0).astype(f)
    sap = np.stack([res[c]["sap"] for c in range(NCORES)])[None].astype(f)
    cap = np.stack([res[c]["cap"] for c in range(NCORES)])[None].astype(f)
    sbp = np.stack([res[c]["sbp"] for c in range(NCORES)])[None].astype(f)
    sas = np.concatenate([res[c]["sas"] for c in range(NCORES)], 0)[None].astype(f)
    cas = np.concatenate([res[c]["cas"].reshape(NSEQ, 3, 3072) for c in range(NCORES)], 0)[None].astype(f)
    sbs = np.concatenate([res[c]["sbs"] for c in range(NCORES)], 0)[None].astype(f)
    return (yp, ys, sap, cap, sbp, sas, cas, sbs)
```

```python
import numpy as np
import ml_dtypes
from contextlib import ExitStack
import concourse.bass as bass
import concourse.mybir as mybir
from concourse.bass_utils import run_bass_kernel_spmd

F32 = mybir.dt.float32
F32R = mybir.dt.float32r
BF16 = mybir.dt.bfloat16
AF = mybir.ActivationFunctionType
ALU = mybir.AluOpType
AX = mybir.AxisListType

NCORES = 8
D = 1024
LP = 2048
NPT = LP // 128
NSEQ = 16
EPS = 1e-6
NEG = -32768.0


class Res:
    __slots__ = ("name", "w", "r", "excl", "strict")

    def __init__(self, name):
        self.name = name
        self.excl = False
        self.strict = False
        self.w = None
        self.r = {}


class V:
    __slots__ = ("ap", "res")

    def __init__(self, ap, res):
        self.ap = ap
        self.res = res


class TT:
    def __init__(self, t, name, nres=1):
        self.t = t
        self.res = Res(name)

    def __getitem__(self, idx):
        return V(self.t[idx], self.res)

    def v(self, ap):
        return V(ap, self.res)


class STT(TT):
    def __init__(self, t, name, n, slot_size):
        self.t = t
        self.n = n
        self.ss = slot_size
        self.slots = [Res(f"{name}_{i}") for i in range(n // slot_size)]
        self.res = tuple(self.slots)

    def __getitem__(self, idx):
        key = idx[1] if isinstance(idx, tuple) and len(idx) > 1 else slice(None)
        if isinstance(key, int):
            lo = hi = key
        else:
            lo = key.start or 0
            hi = (key.stop if key.stop is not None else self.n) - 1
        rs = tuple(self.slots[lo // self.ss:hi // self.ss + 1])
        return V(self.t[idx], rs if len(rs) > 1 else rs[0])


def _flat(xs):
    out = []
    for x in xs:
        r = x.res if isinstance(x, (V, TT)) else x
        if isinstance(r, tuple):
            out.extend(r)
        else:
            out.append(r)
    return out


class Chan:
    def __init__(self, sem):
        self.sem = sem
        self.count = 0


class EngQ:
    def __init__(self, name, eng, sem):
        self.name = name
        self.eng = eng
        self.sem = sem
        self.count = 0
        self.seen = {}


class KB:
    def __init__(self, nc, es):
        self.nc = nc
        self.es = es
        self.q = {}
        for name, eng in (("pe", nc.tensor), ("act", nc.scalar), ("dve", nc.vector),
                          ("pool", nc.gpsimd), ("sp", nc.sync)):
            sem = es.enter_context(nc.semaphore("sem_" + name))
            self.q[name] = EngQ(name, eng, sem)
        self.chans = []
        self.n_instr = 0

    def sbs(self, name, shape, dt, slot_size, es=None):
        t = (es or self.es).enter_context(self.nc.sbuf_tensor("s_" + name, list(shape), dt))
        return STT(t, name, shape[1], slot_size)

    def sb(self, name, shape, dt, es=None):
        t = (es or self.es).enter_context(self.nc.sbuf_tensor("s_" + name, list(shape), dt))
        return TT(t, name)

    def chan(self, name):
        sem = self.es.enter_context(self.nc.semaphore("ch_" + name))
        c = Chan(sem)
        self.chans.append(c)
        return c

    def _need(self, q, ev):
        sem, val, owner = ev
        if q.seen.get(sem.num, 0) >= val:
            return
        q.eng.wait_ge(sem, val)
        q.seen[sem.num] = val

    def _deps(self, q, reads, writes):
        me = q.name
        for r in reads:
            if r is None:
                continue
            if r.w is not None:
                self._need(q, r.w)
            if r.excl:
                for ev in r.r.values():
                    if ev[2] != me:
                        self._need(q, ev)
        for w in writes:
            if w is None:
                continue
            if w.w is not None:
                if w.w[2] != me or me == "pool" or w.strict:
                    self._need(q, w.w)
            for ev in w.r.values():
                if ev[2] != me or me == "pool":
                    self._need(q, ev)

    def _record(self, ev, reads, writes):
        for r in reads:
            if r is not None:
                r.r[ev[0].num] = ev
        for w in writes:
            if w is not None:
                w.w = ev
                w.r = {}

    def op(self, eng, fn, reads, writes, inc=True):
        q = self.q[eng]
        reads = _flat(reads)
        writes = _flat(writes)
        self._deps(q, reads, writes)
        ins = fn(q.eng)
        self.n_instr += 1
        if inc:
            ins.then_inc(q.sem, 1)
            q.count += 1
            ev = (q.sem, q.count, q.name)
        else:
            ev = (q.sem, q.count + 1, q.name)
        self._record(ev, reads, writes)
        return ins

    def dma(self, qname, out, in_, chan, **kw):
        q = self.q[qname]
        reads = _flat([in_])
        writes = _flat([out])
        self._deps(q, reads, writes)
        ins = q.eng.dma_start(out=out.ap, in_=in_.ap, **kw)
        ins.then_inc(chan.sem, 16)
        chan.count += 16
        ev = (chan.sem, chan.count, "dma")
        self._record(ev, reads, writes)
        self.n_instr += 1
        return ev

    def barrier(self):
        evs = [(q.sem, q.count, q.name) for q in self.q.values() if q.count > 0]
        evs += [(c.sem, c.count, "dma") for c in self.chans if c.count > 0]
        for q in self.q.values():
            for ev in evs:
                if ev[2] == q.name:
                    continue
                self._need(q, ev)

    def final_wait(self):
        q = self.q["sp"]
        for c in self.chans:
            if c.count > 0:
                self._need(q, (c.sem, c.count, "dma"))
        for qq in self.q.values():
            if qq.name != "sp" and qq.count > 0:
                self._need(q, (qq.sem, qq.count, qq.name))

    def mm(self, out, lhsT, rhs, start=True, stop=True, inc=None):
        if inc is None:
            inc = stop
        return self.op("pe", lambda e: e.matmul(out.ap, lhsT.ap, rhs.ap, start=start, stop=stop,
                                                skip_group_check=True),
                       [lhsT, rhs], [out], inc=inc)

    def tr(self, out, in_, ident, inc=True):
        return self.op("pe", lambda e: e.transpose(out.ap, in_.ap, ident.ap), [in_, ident], [out], inc=inc)

    def act(self, out, in_, func, bias=None, scale=None, accum=None, extra_reads=()):
        kw = {}
        reads = [in_] + list(extra_reads)
        writes = [out]
        if bias is not None:
            if isinstance(bias, V):
                kw["bias"] = bias.ap
                reads.append(bias)
            else:
                kw["bias"] = bias
        if scale is not None:
            if isinstance(scale, V):
                kw["scale"] = scale.ap
                reads.append(scale)
            else:
                kw["scale"] = scale
        if accum is not None:
            kw["accum_out"] = accum.ap
            writes.append(accum)
        return self.op("act", lambda e: e.activation(out.ap, in_.ap, func, **kw), reads, writes)

    def tt(self, eng, out, in0, in1, op):
        return self.op(eng, lambda e: e.tensor_tensor(out.ap, in0.ap, in1.ap, op), [in0, in1], [out])

    def ts(self, eng, out, in0, s1, op0, s2=None, op1=None):
        reads = [in0]
        a1 = s1
        a2 = s2
        if isinstance(s1, V):
            reads.append(s1)
            a1 = s1.ap
        if isinstance(s2, V):
            reads.append(s2)
            a2 = s2.ap
        if op1 is None:
            return self.op(eng, lambda e: e.tensor_scalar(out.ap, in0.ap, a1, None, op0), reads, [out])
        return self.op(eng, lambda e: e.tensor_scalar(out.ap, in0.ap, a1, a2, op0, op1), reads, [out])

    def stt(self, out, in0, scalar, in1, op0, op1):
        reads = [in0, in1]
        a = scalar
        if isinstance(scalar, V):
            reads.append(scalar)
            a = scalar.ap
        return self.op("dve", lambda e: e.scalar_tensor_tensor(out.ap, in0.ap, a, in1.ap, op0, op1),
                       reads, [out])

    def copy(self, eng, out, in_):
        if eng == "act":
            return self.op("act", lambda e: e.copy(out.ap, in_.ap), [in_], [out])
        return self.op(eng, lambda e: e.tensor_copy(out.ap, in_.ap), [in_], [out])

    def memset(self, eng, out, val):
        return self.op(eng, lambda e: e.memset(out.ap, val), [], [out])

    def reduce_sum(self, out, in_):
        return self.op("dve", lambda e: e.tensor_reduce(out.ap, in_.ap, AX.X, ALU.add), [in_], [out])


def bc(v, shape):
    return V(v.ap.broadcast_to(list(shape)), v.res)


def un(v, axis):
    return V(v.ap.unsqueeze(axis), v.res)


def _mask_set(blk):
    i = np.arange(128)
    same = (i[:, None] // blk) == (i[None, :] // blk)
    triU = ((i[:, None] <= i[None, :]) & same).astype(np.float32)
    blkm = same.astype(np.float32)
    triSU = ((i[:, None] > i[None, :]) & same).astype(np.float32)
    strict = ((i[:, None] > i[None, :]) & same).astype(np.float32)
    incl = (i[:, None] >= i[None, :]) & same
    maskneg = np.where(incl, 0.0, NEG).astype(np.float32)
    masknegT = np.ascontiguousarray(maskneg.T)
    return dict(triU=triU, blk=blkm, triSU=triSU, strict=strict, maskneg=maskneg, masknegT=masknegT)


def _cst_layout():
    off = {}
    o = 0

    def add(name, n):
        nonlocal o
        off[name] = (o, n)
        o += n
    add("identf", 128)
    for s in ("p", "s"):
        for nm in ("triU", "blk", "triSU", "strict", "maskneg", "masknegT"):
            add(nm + "_" + s, 128)
    add("nwT0", 8)
    add("nwT1", 8)
    add("cwT", 96)
    add("dtb", 8)
    add("alog", 8)
    add("onwa", 1)
    add("onwbT", 16)
    add("seqmask", 16)
    add("rdec_p", 8)
    add("rdec_s", 8)
    return off, o


CST_OFF, CST_N = _cst_layout()


def _build_cst(norm_w, conv_w_a, a_log_a, dt_bias_a, onorm_a, onorm_b):
    c = np.zeros((128, CST_N), np.float32)

    def put(name, arr):
        o, n = CST_OFF[name]
        c[:, o:o + n] = arr
    put("identf", np.eye(128, dtype=np.float32))
    for s, blk in (("p", 128), ("s", 8)):
        m = _mask_set(blk)
        for nm in ("triU", "blk", "triSU", "strict", "maskneg", "masknegT"):
            put(nm + "_" + s, m[nm])
    put("nwT0", norm_w[0].reshape(8, 128).T)
    put("nwT1", norm_w[1].reshape(8, 128).T)
    cw = conv_w_a[0].reshape(4, 24, 128)
    put("cwT", np.transpose(cw, (2, 1, 0)).reshape(128, 96))
    put("dtb", np.broadcast_to(dt_bias_a[0][None, :], (128, 8)))
    put("alog", np.broadcast_to(a_log_a[0][None, :], (128, 8)))
    put("onwa", onorm_a[0].reshape(128, 1))
    put("onwbT", onorm_b[0].reshape(16, 128).T)
    sm = np.zeros((128, 16), np.float32)
    sm[np.arange(128), np.arange(128) // 8] = 1.0
    put("seqmask", sm)
    gam = 1.0 - 2.0 ** (-5.0 - np.arange(4, dtype=np.float64))
    for nm, blk in (("rdec_p", 128), ("rdec_s", 8)):
        t = (np.arange(128) % blk).astype(np.float64)
        qd = gam[None, :] ** (t[:, None] + 1.0)
        kd = gam[None, :] ** (blk - 1.0 - t[:, None]) * 256.0 ** -0.5
        put(nm, np.concatenate([qd, kd], 1))
    return c


def ext(ap):
    return V(ap, None)


class PsumPool:
    def __init__(self, kb, n=8):
        self.banks = [TT(kb.es.enter_context(kb.nc.psum_tensor(f"psb{i}", [128, 512], F32)), f"psb{i}")
                      for i in range(n)]
        for b in self.banks:
            b.res.excl = True
        self.i = 0

        self.held = set()

    def next(self, hold=False):
        while True:
            b = self.banks[self.i % len(self.banks)]
            self.i += 1
            if id(b) not in self.held:
                break
        if hold:
            self.held.add(id(b))
        return b

    def release(self, b):
        self.held.discard(id(b))


def bfv(bank):
    return V(bank.t[:].bitcast(BF16), bank.res)


class Ring:
    def __init__(self, kb, name, n, shape, dt, es=None, chan=True):
        self.slots = [kb.sb(f"{name}{i}", shape, dt, es) for i in range(n)]
        self.chans = [kb.chan(f"{name}{i}") for i in range(n)] if chan else None
        self.i = 0

    def next(self):
        j = self.i % len(self.slots)
        self.i += 1
        return self.slots[j], (self.chans[j] if self.chans else None)


class Cut(Exception):
    pass


def build_program(n_ptiles=NPT, do_sample=True, do_l1=True, dbg=False, cut=None):
    def chk(n):
        if cut is not None and cut == n:
            raise Cut()

    nc = bass.Bass("TRN2", target_bir_lowering=False)
    es = ExitStack()
    kb = KB(nc, es)

    def din(name, shape, dt=F32):
        return nc.dram_tensor(name, list(shape), dt, kind="ExternalInput").ap()

    def dout(name, shape, dt=F32):
        return nc.dram_tensor(name, list(shape), dt, kind="ExternalOutput").ap()

    xp_d = din("xp", [LP, D])
    xs_d = din("xs", [128, D])
    sg_d = din("sg", [NSEQ, 8, 128, 128])
    sc_d = din("sc", [48, 3072])
    sr_d = din("sr", [NSEQ, 4, 256, 512])
    wia_d = din("wia", [D, 4112])
    woa_d = din("woa", [D, D])
    wib_d = din("wib", [D, 6144])
    wob_d = din("wob", [2048, D])
    cst_d = din("cst", [128, CST_N])
    idb_d = din("idb", [128, 128], BF16)
    fnw_d = din("fnw", [1, D])
    rot_d = din("rot", [NPT + 1, 128, 2, 128])
    dmask_d = din("dmask", [2, 128, 4, 128])

    yp_d = dout("yp", [LP, D])
    ys_d = dout("ys", [128, D])
    sap_d = dout("sap", [8, 128, 128])
    cap_d = dout("cap", [3, 3072])
    sbp_d = dout("sbp", [4, 256, 512])
    sas_d = dout("sas", [NSEQ, 8, 128, 128])
    cas_d = dout("cas", [48, 3072])
    sbs_d = dout("sbs", [NSEQ, 4, 256, 512])
    x1_d = nc.dram_tensor("x1s", [LP + 128, D], F32, kind="Internal").ap()
    x1_res = [Res(f"x1_{i}") for i in range(NPT + 1)]
    dbg_d = {}
    if dbg:
        for nm, shp in (("d_y1", [128 * (n_ptiles + 1), D]), ("d_mixed", [128, 1536]), ("d_misc", [128, 64]),
                        ("d_dec", [128, 512]), ("d_Y", [128, 512]), ("d_on", [128, 1024])):
            dbg_d[nm] = dout(nm, shp)

    P = PsumPool(kb)
    ch_c = kb.chan("const")
    ch_dbg = kb.chan("dbg")

    cst = kb.sb("cst", [128, CST_N], F32)
    idb = kb.sb("idb_s", [128, 128], BF16)
    kb.dma("sp", cst[:], ext(cst_d), ch_c)
    kb.dma("sp", idb[:], ext(idb_d), ch_c)
    cst.res.w = (ch_c.sem, ch_c.count, "dma")
    idb.res.w = (ch_c.sem, ch_c.count, "dma")

    def C(name, lo=0, hi=None):
        o, n = CST_OFF[name]
        hi = n if hi is None else hi
        return cst[:, o + lo:o + hi]

    neghalf = kb.sb("neghalf", [128, 16], F32)
    kb.memset("dve", neghalf[:], -0.5)
    identf = C("identf")

    def load_masks(s):
        kb.copy("dve", mr["triU"][:], C("triU_" + s))
        kb.ts("dve", mr["negtriU"][:], C("triU_" + s), -1.0, ALU.mult)
        kb.ts("dve", negtriUf[:], C("triU_" + s), -1.0, ALU.mult)
        kb.copy("dve", mr["ones"][:], onesf[:])
        kb.ts("dve", mr["negones"][:], onesf[:], -1.0, ALU.mult)
        kb.copy("dve", mr["ident"][:], identf)
        for nm in ("maskneg", "masknegT"):
            src = C(nm + "_" + s)
            kb.copy("dve", V(mr[nm].t[:].rearrange("p (h j) -> p h j", h=4), mr[nm].res),
                    bc(un(src, 1), [128, 4, 128]))

    xt = kb.sb("xt", [128, D], F32)
    yt = kb.sb("yt", [128, D], F32)
    ch_xt = kb.chan("xt")
    ch_yt = kb.chan("yt")
    xs_b = kb.sb("xs_b", [128, D], BF16)
    xs_b.res.strict = True
    xnT = [kb.sb(f"xnT{i}", [128, 8, 128], BF16) for i in range(2)]
    st4 = kb.sb("st4", [128, 16], F32)
    es0 = ExitStack()
    mr = {}
    for nm in ("triU", "negtriU", "ones", "negones", "ident", "maskneg", "masknegT"):
        w = 512 if nm.startswith("maskneg") else 128
        mr[nm] = kb.sb("mr_" + nm, [128, w], F32R, es0)
    onesf = kb.sb("onesf", [128, 128], F32, es0)
    kb.memset("dve", onesf[:], 1.0)
    negtriUf = kb.sb("negtriUf", [128, 128], F32, es0)


    ch_w0 = kb.chan("w0")
    wia = [kb.sb(f"wia{k}", [128, 4112], BF16, es0) for k in range(8)]
    pieces = ((0, 1536), (1536, 3072), (3072, 4112))
    for k in range(8):
        for (c0, c1) in pieces:
            kb.dma("pool", wia[k][:, c0:c1], ext(wia_d[k * 128:(k + 1) * 128, c0:c1]), ch_w0)
    woa = [kb.sb(f"woa{k}", [128, 1024], BF16, es0) for k in range(8)]
    for k in range(8):
        st, chn = (xt, ch_xt) if k % 2 == 0 else (yt, ch_yt)
        kb.dma("sp", st[:], ext(woa_d[k * 128:(k + 1) * 128, :]), chn)
        kb.ts("dve", woa[k][:], st[:], C("onwa"), ALU.mult)
    for k in range(8):
        wia[k].res.w = (ch_w0.sem, ch_w0.count, "dma")

    diag = kb.sb("diag", [128, 96, 128], BF16, es0)
    for i in range(96):
        kb.ts("dve", diag[:, i, :], idb[:], C("cwT", i, i + 1), ALU.mult)
    negA = kb.sb("negA", [128, 8], F32, es0)
    kb.act(negA[:], C("alog"), AF.Exp)
    kb.ts("dve", negA[:], negA[:], -1.0, ALU.mult)

    def front_end(src_v, nwname, par):
        kb.dma("sp", xt[:], src_v, ch_xt)
        kb.act(xs_b[:], xt[:], AF.Square, accum=st4[:, 0:1])
        kb.ts("dve", st4[:, 1:2], st4[:, 0:1], 1.0 / D, ALU.mult, EPS, ALU.add)
        kb.tt("pool", st4[:, 2:3], st4[:, 1:2], neghalf[:, 0:1], ALU.pow)
        kb.ts("dve", xs_b[:], xt[:], st4[:, 2:3], ALU.mult)
        bank = P.next()
        pv = bfv(bank)
        for k in range(8):
            kb.tr(V(pv.ap[:, k * 128:(k + 1) * 128], pv.res), xs_b[:, k * 128:(k + 1) * 128], idb[:], inc=(k == 7))
        kb.tt("dve", xnT[par][:], V(pv.ap.rearrange("p (k t) -> p k t", k=8), pv.res),
              bc(un(C(nwname), 2), [128, 8, 128]), ALU.mult)

    pT = [kb.sb(f"pT{i}", [128, 12, 176], BF16, es0) for i in range(2)]
    hist = [kb.sb(f"hist{i}", [128, 12, 3], BF16, es0) for i in range(2)]
    for h_ in hist:
        kb.memset("pool", h_[:], 0.0)
    mixed = [kb.sb(f"mixed{i}", [128, 1536], BF16, es0) for i in range(2)]
    zs = [kb.sb(f"zs{i}", [128, 512], BF16, es0) for i in range(2)]
    ba = kb.sb("ba", [128, 16], F32, es0)
    sc8 = kb.sb("sc8", [128, 64], F32, es0)
    E = kb.sb("E", [128, 24], F32, es0)
    sqb = kb.sb("sqb", [128, 1024], BF16, es0)
    r8 = kb.sb("r8", [128, 16], F32, es0)
    sv = {nm: kb.sb("sv_" + nm, [128, 4, 128], BF16, es0) for nm in ("qn", "qe", "kn", "kw", "kd", "vb")}
    qT = kb.sb("qT", [128, 8, 128], BF16, es0)
    knT = kb.sb("knT", [128, 4, 128], BF16, es0)
    gm = kb.sb("gm", [128, 4, 128], F32R, es0)
    gb = kb.sb("gb", [128, 4, 128], F32R, es0)
    dec = kb.sb("dec", [128, 4, 128], F32, es0)
    decT = kb.sb("decT", [128, 4, 128], BF16, es0)
    Pb = [kb.sb(f"Pb{i}", [128, 4, 128], BF16, es0) for i in range(2)]
    PTb = [kb.sb(f"PTb{i}", [128, 4, 128], BF16, es0) for i in range(2)]
    Yb = [kb.sb(f"Yb{i}", [128, 4, 128], BF16, es0) for i in range(2)]
    Mb = kb.sb("Mb", [128, 4, 128], BF16, es0)
    MTb = kb.sb("MTb", [128, 4, 128], BF16, es0)
    negidb = kb.sb("negidb", [128, 128], BF16, es0)
    kb.ts("dve", negidb[:], idb[:], -1.0, ALU.mult)
    negWT = kb.sb("negWT", [128, 4, 128], BF16, es0)
    qkdT = kb.sb("qkdT", [128, 4, 128], BF16, es0)
    S = kb.sbs("S", [128, 8, 128], F32, 4, es0)
    Sbf = kb.sbs("Sbf", [128, 8, 128], BF16, 4, es0)
    ub = kb.sb("ub", [128, 4, 128], BF16, es0)
    otmp = kb.sb("otmp", [128, 512], BF16, es0)
    on = kb.sb("on", [128, 1024], BF16, es0)
    onT = kb.sb("onT", [128, 8, 128], BF16, es0)
    ch_out = kb.chan("out_small")
    ch_sbf = [kb.chan("sbf0"), kb.chan("sbf1")]
    ch_sf = [kb.chan("sf0"), kb.chan("sf1")]
    ch_sfo = [kb.chan("sfo0"), kb.chan("sfo1")]
    if do_sample:
        cv = kb.sb("cv", [128, 12, 128], BF16, es0)
        histT = kb.sb("histT", [128, 24, 48], BF16, es0)
        abc = kb.sb("abc", [128, 64], F32, es0)
        gmsk = kb.sb("gmsk", [128, 16, 4], F32, es0)
        uTb = kb.sb("uTb", [128, 4, 128], BF16, es0)

    beta = sc8[:, 0:8]
    negbeta = sc8[:, 8:16]
    gv = sc8[:, 24:32]

    def v3(tt_, h=4):
        return V(tt_.t[:].rearrange("p (h d) -> p h d", h=h), tt_.res)

    def sc_b(vw):
        return bc(un(vw, 2), [128, 4, 128])

    sc8_2 = [sc8, kb.sb("sc8_b", [128, 64], F32, es0)]
    E_2 = [E, kb.sb("E_b", [128, 24], F32, es0)]
    sv_2 = [sv, {nm: kb.sb("svb_" + nm, [128, 4, 128], BF16, es0) for nm in ("qn", "qe", "kn", "kw", "kd", "vb")}]
    qT_2 = [qT, kb.sb("qT_b", [128, 8, 128], BF16, es0)]
    knT_2 = [knT, kb.sb("knT_b", [128, 4, 128], BF16, es0)]
    osq = kb.sb("osq", [128, 512], BF16, es0)

    def gdn_prologue(ti, is_sample):
        sc8 = sc8_2[ti % 2]
        E = E_2[ti % 2]
        beta = sc8[:, 0:8]
        negbeta = sc8[:, 8:16]
        gv = sc8[:, 24:32]
        par = ti % 2
        src = ext(xs_d[:, :]) if is_sample else ext(xp_d[ti * 128:(ti + 1) * 128, :])
        front_end(src, "nwT0", par)
        xn = xnT[par]
        chk(2)
        bk = P.next()
        for k in range(8):
            kb.mm(bk[:, 0:16], xn[:, k, :], wia[k][:, 4096:4112], start=(k == 0), stop=(k == 7))
        kb.copy("dve", ba[:], bk[:, 0:16])
        kb.act(sc8[:, 56:64], ba[:, 0:8], AF.Tanh, scale=0.5)
        kb.ts("dve", negbeta, sc8[:, 56:64], -0.5, ALU.mult, -0.5, ALU.add)
        kb.ts("dve", beta, sc8[:, 56:64], 0.5, ALU.mult, 0.5, ALU.add)
        kb.tt("dve", sc8[:, 16:24], ba[:, 8:16], C("dtb"), ALU.add)
        kb.act(sc8[:, 16:24], sc8[:, 16:24], AF.Exp)
        kb.act(sc8[:, 16:24], sc8[:, 16:24], AF.Ln, bias=1.0)
        kb.tt("dve", gv, sc8[:, 16:24], negA[:], ALU.mult)
        sfx = "_s" if is_sample else "_p"
        bk = P.next()
        kb.mm(bk[:, 0:8], C("triU" + sfx), gv)
        kb.mm(bk[:, 8:16], C("blk" + sfx), gv)
        kb.mm(bk[:, 16:24], C("triSU" + sfx), gv)
        kb.act(E[:], bk[:, 0:24], AF.Exp)
        chk(3)


    def gdn_stage1(ti, hg, ii, is_sample):
        par = ti % 2
        xn = xnT[par]
        sc8 = sc8_2[ti % 2]
        E = E_2[ti % 2]
        sv, qT, knT = sv_2[ii % 2], qT_2[ii % 2], knT_2[ii % 2]
        sfx = "_s" if is_sample else "_p"
        h0 = hg * 4
        pt = pT[hg]
        chunks = [h0 + i for i in range(4)] + [8 + h0 + i for i in range(4)] + [16 + h0 + i for i in range(4)]
        if is_sample:
            F4 = V(pt.t[:].rearrange("p c (s r) -> p c s r", r=11), pt.res)
            hT4 = V(histT.t[:].rearrange("p c (s r) -> p c s r", r=3), histT.res)
        else:
            kb.copy("pool", pt[:, :, 0:3], hist[hg][:])
        for grp in range(3):
            bk = P.next()
            for ci in range(4):
                col = chunks[grp * 4 + ci] * 128
                for k in range(8):
                    kb.mm(bk[:, ci * 128:(ci + 1) * 128], wia[k][:, col:col + 128], xn[:, k, :],
                          start=(k == 0), stop=(k == 7), inc=(k == 7 and ci == 3))
            eng = "act" if grp % 2 == 0 else "dve"
            if is_sample:
                c0 = chunks[grp * 4]
                kb.copy(eng, V(F4.ap[:, grp * 4:(grp + 1) * 4, :, 3:11], pt.res),
                        V(bk.t[:].rearrange("p (c s t) -> p c s t", c=4, s=16), bk.res))
                kb.copy("pool", V(F4.ap[:, grp * 4:(grp + 1) * 4, :, 0:3], pt.res),
                        V(hT4.ap[:, c0:c0 + 4, :, :], histT.res))
            else:
                kb.copy(eng, pt[:, grp * 4:(grp + 1) * 4, 3:131], v3(bk))
                yield
        if not is_sample:
            kb.copy("pool", hist[hg][:], pt[:, :, 128:131])
        bk = P.next()
        for k in range(8):
            kb.mm(bk[:], xn[:, k, :], wia[k][:, 3072 + hg * 512:3072 + (hg + 1) * 512],
                  start=(k == 0), stop=(k == 7))
        kb.act(zs[hg][:], bk[:], AF.Silu)
        yield
        mx = mixed[hg]
        if is_sample:
            for c in range(12):
                cg = chunks[c]
                cvv = V(cv.t[:, c, :].rearrange("p (s t) -> p s t", t=8), cv.res)
                kb.ts("dve", cvv, V(F4.ap[:, c, :, 0:8], pt.res), C("cwT", cg * 4, cg * 4 + 1), ALU.mult)
                for j in range(1, 4):
                    kb.stt(cvv, V(F4.ap[:, c, :, j:j + 8], pt.res), C("cwT", cg * 4 + j, cg * 4 + j + 1),
                           cvv, ALU.mult, ALU.add)
        for grp in range(3):
            bk = P.next()
            for ci in range(4):
                cg = chunks[grp * 4 + ci]
                if is_sample:
                    kb.mm(bk[:, ci * 128:(ci + 1) * 128], cv[:, grp * 4 + ci, :], idb[:], inc=(ci == 3))
                else:
                    for j in range(4):
                        kb.mm(bk[:, ci * 128:(ci + 1) * 128], pt[:, grp * 4 + ci, j:j + 128],
                              diag[:, cg * 4 + j, :], start=(j == 0), stop=(j == 3), inc=(j == 3 and ci == 3))
            kb.act(mx[:, grp * 512:(grp + 1) * 512], bk[:], AF.Silu)
            yield
        chk(4)
        kb.act(sqb[:], mx[:, 0:1024], AF.Square)
        kb.reduce_sum(r8[:, 0:8], v3(sqb, 8))
        yield
        kb.ts("dve", r8[:, 0:8], r8[:, 0:8], EPS, ALU.add)
        kb.tt("pool", r8[:, 8:16], r8[:, 0:8], neghalf[:, 0:8], ALU.pow)
        yield
        rq = r8[:, 8:12]
        rk = r8[:, 12:16]
        eG = E[:, h0:h0 + 4]
        ekl = E[:, 16 + h0:16 + h0 + 4]
        kb.ts("dve", sc8[:, 32:36], rq, 128 ** -0.5, ALU.mult)
        kb.tt("dve", sc8[:, 36:40], sc8[:, 32:36], eG, ALU.mult)
        kb.tt("dve", sc8[:, 40:44], rk, eG, ALU.mult)
        kb.tt("dve", sc8[:, 40:44], sc8[:, 40:44], sc8[:, h0:h0 + 4], ALU.mult)
        kb.tt("dve", sc8[:, 44:48], rk, ekl, ALU.mult)
        yield
        qv = V(mx.t[:, 0:512].rearrange("p (h d) -> p h d", h=4), mx.res)
        kv = V(mx.t[:, 512:1024].rearrange("p (h d) -> p h d", h=4), mx.res)
        vv = V(mx.t[:, 1024:1536].rearrange("p (h d) -> p h d", h=4), mx.res)
        kb.tt("dve", sv["qn"][:], qv, sc_b(sc8[:, 32:36]), ALU.mult)
        kb.tt("pool", sv["qe"][:], qv, sc_b(sc8[:, 36:40]), ALU.mult)
        yield
        kb.tt("dve", sv["kn"][:], kv, sc_b(rk), ALU.mult)
        kb.tt("pool", sv["kw"][:], kv, sc_b(sc8[:, 40:44]), ALU.mult)
        yield
        kb.tt("dve", sv["kd"][:], kv, sc_b(sc8[:, 44:48]), ALU.mult)
        kb.tt("pool", sv["vb"][:], vv, sc_b(sc8[:, h0:h0 + 4]), ALU.mult)
        yield
        bk = P.next()
        pv = bfv(bk)
        for i, nm in enumerate(("qn", "qe")):
            for h in range(4):
                c0 = (i * 4 + h) * 128
                kb.tr(V(pv.ap[:, c0:c0 + 128], pv.res), sv[nm][:, h, :], idb[:], inc=(i == 1 and h == 3))
        kb.copy("act", qT[:], V(pv.ap.rearrange("p (k t) -> p k t", k=8), pv.res))
        yield
        bk = P.next()
        pv = bfv(bk)
        for h in range(4):
            kb.tr(V(pv.ap[:, h * 128:(h + 1) * 128], pv.res), sv["kn"][:, h, :], idb[:], inc=(h == 3))
        kb.copy("dve", knT[:], V(pv.ap[:, 0:512].rearrange("p (k t) -> p k t", k=4), pv.res))

    def gdn_stage2(ti, hg, ii, is_sample):
        if is_sample:
            sample_prefetch(hg)
        sc8 = sc8_2[ti % 2]
        E = E_2[ti % 2]
        sv, qT, knT = sv_2[ii % 2], qT_2[ii % 2], knT_2[ii % 2]
        sfx = "_s" if is_sample else "_p"
        h0 = hg * 4
        mx = mixed[hg]
        chk(5)
        kb.tt("dve", gm[:], bc(un(sc8[:, 24 + h0:24 + h0 + 4], 2), [128, 4, 128]),
              bc(un(negtriUf[:], 1), [128, 4, 128]), ALU.mult)
        kb.copy("act", gb[:], bc(un(sc8[:, 24 + h0:24 + h0 + 4], 2), [128, 4, 128]))
        yield
        gmf = V(gm.t[:].rearrange("p h j -> p (h j)"), gm.res)
        gbf = V(gb.t[:].rearrange("p h j -> p (h j)"), gb.res)
        chk(51)
        bk = P.next()
        kb.mm(bk[:], mr["triU"][:], gbf, start=True, stop=False)
        kb.mm(bk[:], mr["ones"][:], gmf, start=False, stop=False)
        kb.mm(bk[:], mr["ident"][:], mr["maskneg"][:], start=False, stop=True)
        chk(52)
        kb.act(V(dec.t[:].rearrange("p h j -> p (h j)"), dec.res), bk[:], AF.Exp)
        yield
        chk(53)
        bk = P.next()
        kb.mm(bk[:], mr["negtriU"][:], gbf, start=True, stop=False)
        kb.mm(bk[:], mr["negones"][:], gmf, start=False, stop=False)
        kb.mm(bk[:], mr["ident"][:], mr["masknegT"][:], start=False, stop=True)
        chk(54)
        kb.act(V(decT.t[:].rearrange("p h j -> p (h j)"), decT.res), bk[:], AF.Exp)
        yield
        chk(55)
        chk(6)
        bk = P.next()
        for h in range(4):
            kb.mm(bk[:, h * 128:(h + 1) * 128], knT[:, h, :], knT[:, h, :], inc=(h == 3))
        kb.tt("dve", dec[:], dec[:], bc(un(C("strict" + sfx), 1), [128, 4, 128]), ALU.mult)
        kb.tt("dve", dec[:], dec[:], sc_b(sc8[:, 8 + h0:8 + h0 + 4]), ALU.mult)
        yield
        Pc, PTc, Yc = Mb, MTb, Yb[0]
        kb.tt("dve", Pc[:], v3(bk), dec[:], ALU.mult)
        yield
        chk(61)
        bk = P.next()
        pv = bfv(bk)
        for h in range(4):
            kb.tr(V(pv.ap[:, h * 128:(h + 1) * 128], pv.res), Pc[:, h, :], idb[:], inc=(h == 3))
        pv4 = V(pv.ap[:, 0:512].rearrange("p (k t) -> p k t", k=4), pv.res)
        kb.copy("act", PTc[:], pv4)
        kb.tt("dve", Yc[:], pv4, bc(un(idb[:], 1), [128, 4, 128]), ALU.add)
        yield
        chk(62)
        nsteps = 3 if is_sample else 6
        for stp in range(1, nsteps):
            Pn, PTn, Yn = Pb[stp % 2], PTb[stp % 2], Yb[stp % 2]
            bkA = P.next()
            for h in range(4):
                kb.mm(bkA[:, h * 128:(h + 1) * 128], PTc[:, h, :], Pc[:, h, :], inc=(h == 3))
            last = (stp == nsteps - 1)
            if not last:
                bkB = P.next()
                for h in range(4):
                    kb.mm(bkB[:, h * 128:(h + 1) * 128], Pc[:, h, :], PTc[:, h, :], inc=(h == 3))
            kb.copy("act", Pn[:], v3(bkA))
            if not last:
                kb.copy("dve", PTn[:], v3(bkB))
                yield
            bkC = P.next()
            for h in range(4):
                kb.mm(bkC[:, h * 128:(h + 1) * 128], Pn[:, h, :], Yc[:, h, :], inc=(h == 3))
            kb.tt("dve", Yn[:], v3(bkC), Yc[:], ALU.add)
            yield
            Pc, PTc, Yc = Pn, PTn, Yn
        bk = P.next()
        pv = bfv(bk)
        for h in range(4):
            kb.tr(V(pv.ap[:, h * 128:(h + 1) * 128], pv.res), Yc[:, h, :], idb[:], inc=(h == 3))
        X0b, Rb = PTb[0], Pb[0]
        kb.copy("act", X0b[:], V(pv.ap[:, 0:512].rearrange("p (k t) -> p k t", k=4), pv.res))
        yield
        bk = P.next()
        for h in range(4):
            kb.mm(bk[:, h * 128:(h + 1) * 128], MTb[:, h, :], X0b[:, h, :], start=True, stop=False)
            kb.mm(bk[:, h * 128:(h + 1) * 128], negidb[:], X0b[:, h, :], start=False, stop=True, inc=(h == 3))
        kb.tt("dve", Rb[:], v3(bk), bc(un(idb[:], 1), [128, 4, 128]), ALU.add)
        yield
        bk = P.next()
        for h in range(4):
            kb.mm(bk[:, h * 128:(h + 1) * 128], Rb[:, h, :], Yc[:, h, :], inc=(h == 3))
        Yn = Yb[1] if Yc is Yb[0] else Yb[0]
        kb.tt("dve", Yn[:], v3(bk), Yc[:], ALU.add)
        Yc = Yn
        chk(63)
        bk = P.next()
        for h in range(4):
            kb.mm(bk[:, h * 128:(h + 1) * 128], sv["kw"][:, h, :], Yc[:, h, :], inc=(h == 3))
        kb.act(negWT[:], v3(bk), AF.Copy, scale=-1.0)
        yield
        bk = P.next()
        for h in range(4):
            kb.mm(bk[:, h * 128:(h + 1) * 128], knT[:, h, :], qT[:, h, :], inc=(h == 3))
        kb.tt("dve", qkdT[:], v3(bk), decT[:], ALU.mult)
        yield
        if dbg and ti == dbg_tile and hg == 0:
            kb.dma("sp", ext(dbg_d["d_misc"][:, 0:64]), sc8[:], ch_dbg)
            kb.dma("sp", ext(dbg_d["d_dec"]), V(dec.t[:].rearrange("p h j -> p (h j)"), dec.res), ch_dbg)
        chk(7)
        if not is_sample:
            first = (ti == 0)
            bu = P.next()
            for h in range(4):
                kb.mm(bu[:, h * 128:(h + 1) * 128], Yc[:, h, :], sv["vb"][:, h, :], start=True, stop=first,
                      inc=(first and h == 3))
                if not first:
                    kb.mm(bu[:, h * 128:(h + 1) * 128], negWT[:, h, :], Sbf[:, h0 + h, :], start=False,
                          stop=True, inc=(h == 3))
            kb.copy("act", ub[:], v3(bu))
            yield
            bo = P.next()
            for h in range(4):
                if not first:
                    kb.mm(bo[:, h * 128:(h + 1) * 128], qT[:, 4 + h, :], Sbf[:, h0 + h, :], start=True,
                          stop=False)
                kb.mm(bo[:, h * 128:(h + 1) * 128], qkdT[:, h, :], ub[:, h, :], start=first, stop=True,
                      inc=(h == 3))
            bs = P.next()
            for h in range(4):
                kb.mm(bs[:, h * 128:(h + 1) * 128], sv["kd"][:, h, :], ub[:, h, :], inc=(h == 3))
            for h in range(4):
                if first:
                    kb.copy("dve", S[:, h0 + h, :], bs[:, h * 128:(h + 1) * 128])
                else:
                    kb.stt(S[:, h0 + h, :], S[:, h0 + h, :], E[:, 8 + h0 + h:8 + h0 + h + 1],
                           bs[:, h * 128:(h + 1) * 128], ALU.mult, ALU.add)
            kb.copy("act", Sbf[:, h0:h0 + 4, :], S[:, h0:h0 + 4, :])
            yield
            o_src = bo[:]
        else:
            o_src = gdn_sample_rec(hg, Yc, sv, qT, sc8)
        chk(8)
        o3 = V(o_src.ap.rearrange("p (h d) -> p h d", h=4), o_src.res)
        kb.act(osq[:], o_src, AF.Square)
        kb.reduce_sum(sc8[:, 48:52], V(osq.t[:].rearrange("p (h d) -> p h d", h=4), osq.res))
        yield
        kb.ts("dve", sc8[:, 48:52], sc8[:, 48:52], 1.0 / 128, ALU.mult, EPS, ALU.add)
        kb.tt("pool", sc8[:, 52:56], sc8[:, 48:52], neghalf[:, 0:4], ALU.pow)
        yield
        kb.tt("dve", v3(otmp), o3, sc_b(sc8[:, 52:56]), ALU.mult)
        kb.tt("dve", on[:, hg * 512:(hg + 1) * 512], otmp[:], zs[hg][:], ALU.mult)
        if dbg and ti == dbg_tile and hg == 0:
            kb.dma("pool", ext(dbg_d["d_mixed"]), mx[:], ch_dbg)
            kb.dma("pool", ext(dbg_d["d_Y"]), otmp[:], ch_dbg)

    def gdn_epilogue(ti, is_sample):
        par = ti % 2
        xn = xnT[par]
        src = ext(xs_d[:, :]) if is_sample else ext(xp_d[ti * 128:(ti + 1) * 128, :])
        kb.dma("sp", yt[:], src, ch_yt)
        chk(9)
        bk = P.next()
        pv = bfv(bk)
        for k in range(8):
            kb.tr(V(pv.ap[:, k * 128:(k + 1) * 128], pv.res), on[:, k * 128:(k + 1) * 128], idb[:], inc=(k == 7))
        kb.copy("act", onT[:], V(pv.ap.rearrange("p (k t) -> p k t", k=8), pv.res))
        for n in range(2):
            bk = P.next()
            for k in range(8):
                kb.mm(bk[:], onT[:, k, :], woa[k][:, n * 512:(n + 1) * 512], start=(k == 0), stop=(k == 7))
            kb.tt("dve", yt[:, n * 512:(n + 1) * 512], bk[:], yt[:, n * 512:(n + 1) * 512], ALU.add)
        kb.dma("sp", V(x1_d[ti * 128:(ti + 1) * 128, :], x1_res[ti]), yt[:], ch_yt)
        if dbg:
            dr = n_ptiles if is_sample else ti
            kb.dma("sp", ext(dbg_d["d_y1"][dr * 128:(dr + 1) * 128, :]), yt[:], ch_yt)
        if is_sample or ti == n_ptiles - 1:
            for cb in range(3):
                for half in range(2):
                    bk = P.next()
                    col = cb * 1024 + half * 512
                    for k in range(8):
                        kb.mm(bk[:], xn[:, k, :], wia[k][:, col:col + 512], start=(k == 0), stop=(k == 7))
                    kb.copy("act" if half else "dve", xt[:, half * 512:(half + 1) * 512], bk[:])
                if is_sample:
                    for s_ in range(NSEQ):
                        kb.dma("sp", ext(cas_d[s_ * 3:(s_ + 1) * 3, cb * 1024:(cb + 1) * 1024]),
                               xt[s_ * 8 + 5:s_ * 8 + 8, :], ch_xt)
                else:
                    kb.dma("sp", ext(cap_d[:, cb * 1024:(cb + 1) * 1024]), xt[125:128, :], ch_xt)
        if (not is_sample) and ti == n_ptiles - 1:
            kb.dma("sp", ext(sap_d.rearrange("h k v -> k h v")), S[:], ch_out)


    smpst = {}

    def vi(v, *idx):
        return V(v.ap[idx], v.res)

    def sample_slots():
        if smpst:
            return smpst
        bsl, fsl = [], []
        for i in range(16):
            r = Res(f"smb{i}")
            r.w = diag.res.w
            r.r = dict(diag.res.r)
            bsl.append(V(diag.t[:, i * 4:(i + 1) * 4, :], r))
        for i in range(4):
            r = Res(f"smf{i}")
            r.w = diag.res.w
            r.r = dict(diag.res.r)
            ap = diag.t[:, 64 + i * 8:64 + (i + 1) * 8, :].rearrange("p a b -> p (a b)").bitcast(F32)
            fsl.append(V(ap.rearrange("p (h d) -> p h d", h=4), r))
        smpst["b"] = bsl
        smpst["f"] = fsl
        smpst["chb"] = [kb.chan(f"smb{i}") for i in range(16)]
        smpst["chf"] = [kb.chan(f"smf{i}") for i in range(4)]
        smpst["chfo"] = [kb.chan(f"smfo{i}") for i in range(4)]
        return smpst

    def sample_ld_f(hg, q_):
        if q_ >= NSEQ:
            return
        sm = sample_slots()
        h0 = hg * 4
        kb.dma("sp", sm["f"][q_ % 4], ext(sg_d[q_, h0:h0 + 4].rearrange("h k v -> k h v")), sm["chf"][q_ % 4])

    def sample_prefetch(hg):
        sm = sample_slots()
        h0 = hg * 4
        for q_ in range(NSEQ):
            kb.dma("pool", sm["b"][q_], ext(sg_d[q_, h0:h0 + 4].rearrange("h k v -> k h v")), sm["chb"][q_])
        for q_ in range(3):
            sample_ld_f(hg, q_)

    def gdn_sample_rec(hg, Yc, sv, qT, sc8):
        sm = sample_slots()
        h0 = hg * 4
        kb.tt("dve", gmsk[:], bc(un(sc8[:, 24 + h0:24 + h0 + 4], 1), [128, 16, 4]),
              bc(un(C("seqmask"), 2), [128, 16, 4]), ALU.mult)
        bk = P.next()
        kb.mm(bk[:, 0:64], onesf[:], V(gmsk.t[:].rearrange("p s h -> p (s h)"), gmsk.res))
        kb.act(abc[:], bk[:, 0:64], AF.Exp)
        buT = P.next(hold=True)
        for h in range(4):
            kb.mm(buT[:, h * 128:(h + 1) * 128], sv["vb"][:, h, :], Yc[:, h, :], start=(h == 0), stop=False,
                  inc=False)
        for s_ in range(NSEQ):
            for h in range(4):
                lastmm = (s_ == NSEQ - 1 and h == 3)
                kb.mm(buT[:, h * 128 + s_ * 8:h * 128 + s_ * 8 + 8], vi(sm["b"][s_], slice(None), h, slice(None)),
                      negWT[:, h, s_ * 8:s_ * 8 + 8], start=False, stop=lastmm, inc=(h == 3))
        kb.copy("act", uTb[:], v3(buT))
        P.release(buT)
        bk = P.next()
        pv = bfv(bk)
        for h in range(4):
            kb.tr(V(pv.ap[:, h * 128:(h + 1) * 128], pv.res), uTb[:, h, :], idb[:], inc=(h == 3))
        kb.copy("dve", ub[:], V(pv.ap[:, 0:512].rearrange("p (k t) -> p k t", k=4), pv.res))
        boT = P.next(hold=True)
        for h in range(4):
            kb.mm(boT[:, h * 128:(h + 1) * 128], ub[:, h, :], qkdT[:, h, :], start=(h == 0), stop=False, inc=False)
        for s_ in range(NSEQ):
            sample_ld_f(hg, s_ + 3)
            fs_ = sm["f"][s_ % 4]
            for h in range(4):
                lastmm = (s_ == NSEQ - 1 and h == 3)
                kb.mm(boT[:, h * 128 + s_ * 8:h * 128 + s_ * 8 + 8], vi(sm["b"][s_], slice(None), h, slice(None)),
                      qT[:, 4 + h, s_ * 8:s_ * 8 + 8], start=False, stop=lastmm, inc=(h == 3))
            kdm = Pb[s_ % 2]
            kb.ts("dve", kdm[:], sv["kd"][:], C("seqmask", s_, s_ + 1), ALU.mult)
            bs = P.next()
            for h in range(4):
                kb.mm(bs[:, h * 128:(h + 1) * 128], kdm[:, h, :], ub[:, h, :], inc=(h == 3))
            for h in range(4):
                fh = vi(fs_, slice(None), h, slice(None))
                kb.stt(fh, fh, abc[:, s_ * 4 + h:s_ * 4 + h + 1], bs[:, h * 128:(h + 1) * 128], ALU.mult, ALU.add)
            kb.dma("act", ext(sas_d[s_, h0:h0 + 4].rearrange("h k v -> k h v")), fs_, sm["chfo"][s_ % 4])
        kb.copy("act", uTb[:], v3(boT))
        P.release(boT)
        bk = P.next()
        pv = bfv(bk)
        for h in range(4):
            kb.tr(V(pv.ap[:, h * 128:(h + 1) * 128], pv.res), uTb[:, h, :], idb[:], inc=(h == 3))
        return V(pv.ap[:, 0:512], pv.res)

    def sample_hist_setup():
        for piece in range(3):
            kb.dma("sp", yt[0:48, :], ext(sc_d[:, piece * 1024:(piece + 1) * 1024]), ch_yt)
            bk = P.next()
            for c in range(8):
                kb.tr(bk[:, c * 48:(c + 1) * 48], yt[0:48, c * 128:(c + 1) * 128], V(identf.ap[0:48, 0:48], identf.res),
                      inc=(c == 7))
            kb.copy("dve", histT[:, piece * 8:(piece + 1) * 8, :],
                    V(bk.t[:, 0:384].rearrange("p (c r) -> p c r", c=8), bk.res))

    dbg_tile = 1 if n_ptiles > 1 else 0
    try:
        load_masks("p")
        chk(1)
        if do_sample:
            sample_hist_setup()
        tiles = [(ti, False) for ti in range(n_ptiles)] + ([(NPT, True)] if do_sample else [])
        items = [(ti, hg, smp) for (ti, smp) in tiles for hg in range(2)]
        gdn_prologue(tiles[0][0], tiles[0][1])
        masks_s = False
        def gseq(g, f):
            if g is not None:
                yield from g
            f()
            yield

        def drive(g1, g2, r2=1):
            a1 = g1 is not None
            a2 = g2 is not None
            while a1 or a2:
                for _ in range(r2):
                    if a2:
                        try:
                            next(g2)
                        except StopIteration:
                            a2 = False
                if a1:
                    try:
                        next(g1)
                    except StopIteration:
                        a1 = False
        for ii in range(len(items) + 1):
            g1 = g2 = None
            if ii < len(items):
                ti, hg, smp = items[ii]
                g1 = gdn_stage1(ti, hg, ii, smp)
            if ii >= 1:
                ti2, hg2, smp2 = items[ii - 1]
                if smp2 and not masks_s:
                    load_masks("s")
                    masks_s = True
                g2 = gdn_stage2(ti2, hg2, ii - 1, smp2)
            if ii >= 1 and hg2 == 1:
                g2 = gseq(g2, lambda a=ti2, b=smp2: gdn_epilogue(a, b))
            if ii < len(items) and hg == 1:
                nxt = [tt_ for tt_ in tiles if tt_[0] > ti]
                if nxt:
                    g1 = gseq(g1, lambda a=nxt[0][0], b=nxt[0][1]: gdn_prologue(a, b))
            drive(g1, g2)
    except Cut:
        pass

    if not do_l1:
        kb.final_wait()
        es0.close()
        return nc

    kb.barrier()
    es0.close()
    es1 = ExitStack()
    ch_w1 = kb.chan("w1")
    wib = [kb.sb(f"wib{k}", [128, 6144], BF16, es1) for k in range(8)]
    for k in range(8):
        for (c0, c1) in ((0, 2048), (2048, 4096), (4096, 6144)):
            kb.dma("pool", wib[k][:, c0:c1], ext(wib_d[k * 128:(k + 1) * 128, c0:c1]), ch_w1)
    wob = [kb.sb(f"wob{k}", [128, 1024], BF16, es1) for k in range(16)]
    for k in range(16):
        st, chn = (xt, ch_xt) if k % 2 == 0 else (yt, ch_yt)
        kb.dma("sp", st[:], ext(wob_d[k * 128:(k + 1) * 128, :]), chn)
        kb.ts("dve", wob[k][:], st[:], C("onwbT", k, k + 1), ALU.mult)
    for k in range(8):
        wib[k].res.w = (ch_w1.sem, ch_w1.count, "dma")
    fnw = kb.sb("fnw", [128, D], F32, es1)
    ch_fnw = kb.chan("fnw")
    ch_dm = kb.chan("dm")
    ch_dms = kb.chan("dms")
    kb.dma("sp", fnw[:], ext(fnw_d[0:1, :].broadcast_to([128, D])), ch_fnw)
    dm = kb.sb("dm", [128, 4, 128], F32, es1)
    rot = kb.sb("rot", [128, 2, 128], F32, es1)
    ch_rot = kb.chan("rot")
    tmpA = kb.sb("tmpA", [128, 2, 128], F32, es1)
    tmpB = kb.sb("tmpB", [128, 2, 128], F32, es1)
    tmpC = kb.sb("tmpC", [128, 2, 128], F32, es1)
    tmpD = kb.sb("tmpD", [128, 2, 128], F32, es1)
    qkr = kb.sb("qkr", [128, 2, 2, 128], BF16, es1)
    qd = kb.sb("qd", [128, 256], BF16, es1)
    kdd = kb.sb("kdd", [128, 256], BF16, es1)
    qkT = kb.sb("qkT", [128, 6, 128], BF16, es1)
    vb1 = kb.sb("vb1", [128, 512], BF16, es1)
    gs1 = kb.sb("gs1", [128, 512], BF16, es1)
    qkd1 = kb.sb("qkd1", [128, 128], BF16, es1)
    S1 = kb.sbs("S1", [128, 4, 2, 512], F32, 1, es1)
    S1b = kb.sbs("S1b", [128, 4, 2, 512], BF16, 1, es1)
    on1 = kb.sb("on1", [128, 2048], BF16, es1)
    on1T = kb.sb("on1T", [128, 16, 128], BF16, es1)
    st1 = kb.sb("st1", [128, 8], F32, es1)
    zb = [kb.sb(f"zb{i}", [128, 2, 128], BF16, es1) for i in range(2)]
    kddm = [kb.sb(f"kddm{i}", [128, 256], BF16, es1) for i in range(2)]
    for z_ in zb:
        kb.memset("pool", z_[:], 0.0)
    ch_s1 = [kb.chan(f"s1_{i}") for i in range(4)]
    ch_s1o = [kb.chan(f"s1o_{i}") for i in range(4)]
    gam = [1.0 - 2.0 ** (-5.0 - h) for h in range(4)]

    kdd2 = [kdd, kb.sb("kdd_b", [128, 256], BF16, es1)]
    qkT2 = [qkT, kb.sb("qkT_b", [128, 6, 128], BF16, es1)]
    vb12 = [vb1, kb.sb("vb1_b", [128, 512], BF16, es1)]
    gs12 = [gs1, kb.sb("gs1_b", [128, 512], BF16, es1)]
    rot2 = [rot, kb.sb("rot_b", [128, 2, 128], F32, es1)]
    ch_rot2 = [ch_rot, kb.chan("rot_b")]
    dms = kb.sb("dm_s", [128, 4, 128], F32, es1)

    def ret_prologue(ti, is_sample):
        src = V(x1_d[ti * 128:(ti + 1) * 128, :], x1_res[ti])
        front_end(src, "nwT1", ti % 2)
        kb.dma("sp", rot2[ti % 2][:], ext(rot_d[ti]), ch_rot2[ti % 2])

    def ret_stage1(ti, h, ii, is_sample):
        xn = xnT[ti % 2]
        rt = rot2[ti % 2]
        kdd_, qkT_, vb_, gs_ = kdd2[ii % 2], qkT2[ii % 2], vb12[ii % 2], gs12[ii % 2]
        rd = "rdec_s" if is_sample else "rdec_p"
        bk = P.next()
        for part, c0 in ((0, h * 256), (1, 1024 + h * 256)):
            for k in range(8):
                kb.mm(bk[:, part * 256:(part + 1) * 256], xn[:, k, :], wib[k][:, c0:c0 + 256],
                      start=(k == 0), stop=(k == 7), inc=(k == 7 and part == 1))
        pq = V(bk.t[:].rearrange("p (a b d) -> p a b d", a=2, b=2), bk.res)
        x1v = V(pq.ap[:, :, 0, :], bk.res)
        x2v = V(pq.ap[:, :, 1, :], bk.res)
        cosb = bc(un(rt[:, 0, :], 1), [128, 2, 128])
        sinb = bc(un(rt[:, 1, :], 1), [128, 2, 128])
        kb.tt("dve", tmpA[:], x1v, cosb, ALU.mult)
        kb.tt("dve", tmpB[:], x2v, sinb, ALU.mult)
        kb.tt("dve", V(qkr.t[:, :, 0, :], qkr.res), tmpA[:], tmpB[:], ALU.subtract)
        yield
        kb.tt("dve", tmpC[:], x1v, sinb, ALU.mult)
        kb.tt("dve", tmpD[:], x2v, cosb, ALU.mult)
        kb.tt("dve", V(qkr.t[:, :, 1, :], qkr.res), tmpC[:], tmpD[:], ALU.add)
        qr = V(qkr.t[:, 0, :, :].rearrange("p b d -> p (b d)"), qkr.res)
        kr = V(qkr.t[:, 1, :, :].rearrange("p b d -> p (b d)"), qkr.res)
        kb.ts("dve", qd[:], qr, C(rd, h, h + 1), ALU.mult)
        kb.ts("dve", kdd_[:], kr, C(rd, 4 + h, 5 + h), ALU.mult)
        yield
        bk = P.next()
        for k in range(8):
            kb.mm(bk[:], xn[:, k, :], wib[k][:, 2048 + h * 512:2048 + (h + 1) * 512], start=(k == 0), stop=(k == 7))
        kb.copy("act", vb_[:], bk[:])
        yield
        bk = P.next()
        for k in range(8):
            kb.mm(bk[:], xn[:, k, :], wib[k][:, 4096 + h * 512:4096 + (h + 1) * 512], start=(k == 0), stop=(k == 7))
        kb.act(gs_[:], bk[:], AF.Silu)
        yield
        bk = P.next()
        pv = bfv(bk)
        srcs = [qkr[:, 0, 0, :], qkr[:, 0, 1, :], qkr[:, 1, 0, :], qkr[:, 1, 1, :], qd[:, 0:128], qd[:, 128:256]]
        for i_, sv_ in enumerate(srcs):
            kb.tr(V(pv.ap[:, i_ * 128:(i_ + 1) * 128], pv.res), sv_, idb[:], inc=(i_ == 5))
        kb.copy("act", qkT_[:], V(pv.ap[:, 0:768].rearrange("p (k t) -> p k t", k=6), pv.res))
        yield

    def ret_stage2(ti, h, ii, is_sample, n_tiles_first):
        kdd_, qkT_, vb_, gs_ = kdd2[ii % 2], qkT2[ii % 2], vb12[ii % 2], gs12[ii % 2]
        first = (ti == 0)
        dmk = dms if is_sample else dm
        bk = P.next()
        kb.mm(bk[:, 0:128], qkT_[:, 2, :], qkT_[:, 0, :], start=True, stop=False)
        kb.mm(bk[:, 0:128], qkT_[:, 3, :], qkT_[:, 1, :], start=False, stop=True)
        kb.tt("dve", qkd1[:], bk[:, 0:128], dmk[:, h, :], ALU.mult)
        yield
        bo = P.next(hold=True)
        if not is_sample:
            kb.mm(bo[:], qkd1[:], vb_[:], start=True, stop=first)
            if not first:
                kb.mm(bo[:], qkT_[:, 4, :], S1b[:, h, 0, :], start=False, stop=False)
                kb.mm(bo[:], qkT_[:, 5, :], S1b[:, h, 1, :], start=False, stop=True)
            for c in range(2):
                bs = P.next()
                kb.mm(bs[:], kdd_[:, c * 128:(c + 1) * 128], vb_[:])
                if first:
                    kb.copy("dve", S1[:, h, c, :], bs[:])
                else:
                    kb.stt(S1[:, h, c, :], S1[:, h, c, :], float(gam[h] ** 128), bs[:], ALU.mult, ALU.add)
            kb.copy("act", S1b[:, h, :, :], S1[:, h, :, :])
            yield
        else:
            kb.mm(bo[:], qkd1[:], vb_[:], start=True, stop=False, inc=False)

            def ld1(pi):
                if pi >= 4 * NSEQ:
                    return
                hh, ss = divmod(pi, NSEQ)
                kb.dma("sp", S1[:, pi % 4, :, :], ext(sr_d[ss, hh].rearrange("(c p) v -> p c v", p=128)),
                       ch_s1[pi % 4])
            def cast1(pi_):
                if pi_ < 4 * NSEQ:
                    kb.copy("act", S1b[:, pi_ % 4, :, :], S1[:, pi_ % 4, :, :])
            if h == 0:
                ld1(0)
                ld1(1)
                cast1(0)
            for s_ in range(NSEQ):
                pi = h * NSEQ + s_
                sl = pi % 4
                ld1(pi + 2)
                z_ = zb[s_ % 2]
                kb.copy("dve", z_[:, :, s_ * 8:s_ * 8 + 8], qkT_[:, 4:6, s_ * 8:s_ * 8 + 8])
                kb.mm(bo[:], z_[:, 0, :], S1b[:, sl, 0, :], start=False, stop=False, inc=False)
                kb.mm(bo[:], z_[:, 1, :], S1b[:, sl, 1, :], start=False, stop=(s_ == NSEQ - 1), inc=True)
                kb.memset("dve", z_[:, :, s_ * 8:s_ * 8 + 8], 0.0)
                km = kddm[s_ % 2]
                kb.ts("dve", km[:], kdd_[:], C("seqmask", s_, s_ + 1), ALU.mult)
                bss = []
                for c in range(2):
                    bs = P.next()
                    kb.mm(bs[:], km[:, c * 128:(c + 1) * 128], vb_[:])
                    bss.append(bs)
                cast1(pi + 1)
                for c in range(2):
                    kb.stt(S1[:, sl, c, :], S1[:, sl, c, :], float(gam[h] ** 8), bss[c][:], ALU.mult, ALU.add)
                kb.dma("act", ext(sbs_d[s_, h].rearrange("(c p) v -> p c v", p=128)), S1[:, sl, :, :], ch_s1o[sl])
                yield
        kb.act(xs_b[:, 0:512], bo[:], AF.Square, accum=st1[:, 0:1])
        kb.ts("dve", st1[:, 1:2], st1[:, 0:1], 1.0 / 512, ALU.mult, EPS, ALU.add)
        kb.tt("pool", st1[:, 2:3], st1[:, 1:2], neghalf[:, 0:1], ALU.pow)
        yield
        kb.stt(on1[:, h * 512:(h + 1) * 512], bo[:], st1[:, 2:3], gs_[:], ALU.mult, ALU.mult)
        P.release(bo)
        yield

    def ret_epilogue(ti, is_sample, last_prompt):
        src = V(x1_d[ti * 128:(ti + 1) * 128, :], x1_res[ti])
        kb.dma("sp", yt[:], src, ch_yt)
        for g_ in range(2):
            bk = P.next()
            pv = bfv(bk)
            for k in range(8):
                kk_ = g_ * 8 + k
                kb.tr(V(pv.ap[:, k * 128:(k + 1) * 128], pv.res), on1[:, kk_ * 128:(kk_ + 1) * 128], idb[:], inc=(k == 7))
            kb.copy("act" if g_ else "dve", on1T[:, g_ * 8:(g_ + 1) * 8, :], V(pv.ap.rearrange("p (k t) -> p k t", k=8), pv.res))
        for n in range(2):
            bk = P.next()
            for k in range(16):
                kb.mm(bk[:], on1T[:, k, :], wob[k][:, n * 512:(n + 1) * 512], start=(k == 0), stop=(k == 15))
            kb.tt("dve", yt[:, n * 512:(n + 1) * 512], bk[:], yt[:, n * 512:(n + 1) * 512], ALU.add)
        kb.act(xs_b[:], yt[:], AF.Square, accum=st1[:, 4:5])
        kb.ts("dve", st1[:, 5:6], st1[:, 4:5], 1.0 / D, ALU.mult, EPS, ALU.add)
        kb.tt("pool", st1[:, 6:7], st1[:, 5:6], neghalf[:, 0:1], ALU.pow)
        kb.stt(yt[:], yt[:], st1[:, 6:7], fnw[:], ALU.mult, ALU.mult)
        dst = ext(ys_d[:, :]) if is_sample else ext(yp_d[ti * 128:(ti + 1) * 128, :])
        kb.dma("sp", dst, yt[:], ch_yt)
        if last_prompt:
            kb.dma("sp", ext(sbp_d.rearrange("h (c p) v -> p h c v", p=128)), S1[:], ch_out)

    try:
        kb.dma("sp", dm[:], ext(dmask_d[0]), ch_dm)
        kb.dma("sp", dms[:], ext(dmask_d[1]), ch_dms)
        tiles = [(ti, False) for ti in range(n_ptiles)] + ([(NPT, True)] if do_sample else [])
        items = [(ti, h, smp) for (ti, smp) in tiles for h in range(4)]
        ret_prologue(tiles[0][0], tiles[0][1])
        for ii in range(len(items) + 1):
            g1 = g2 = None
            if ii < len(items):
                ti, h, smp = items[ii]
                g1 = ret_stage1(ti, h, ii, smp)
                if h == 1:
                    nxt = [tt_ for tt_ in tiles if tt_[0] > ti]
                    if nxt:
                        g1 = gseq(g1, lambda a=nxt[0][0], b=nxt[0][1]: ret_prologue(a, b))
            if ii >= 1:
                ti2, h2, smp2 = items[ii - 1]
                g2 = ret_stage2(ti2, h2, ii - 1, smp2, None)
                if h2 == 3:
                    g2 = gseq(g2, lambda a=ti2, b=smp2: ret_epilogue(a, b, (not b) and a == n_ptiles - 1))
            drive(g1, g2)
    except Cut:
        pass
    kb.final_wait()
    es1.close()
    return nc


def _rot_tables():
    half = 128
    inv = 1.0 / (10000.0 ** np.linspace(0.0, 1.0, half))
    gam = 1.0 - 2.0 ** (-5.0 - np.arange(4))
    rot = np.zeros((NPT + 1, 128, 2, 128), np.float32)
    for ti in range(NPT + 1):
        if ti < NPT:
            pos = ti * 128 + np.arange(128, dtype=np.float64)
        else:
            pos = 16384.0 + (np.arange(128) % 8).astype(np.float64)
        ang = pos[:, None] * inv[None, :]
        rot[ti, :, 0, :] = np.cos(ang)
        rot[ti, :, 1, :] = np.sin(ang)
    return rot


def make_in_maps(I):
    f = np.float32
    cst = _build_cst(I["norm_w"], I["conv_w_a"], I["a_log_a"], I["dt_bias_a"], I["onorm_a"], I["onorm_b"])
    idb = np.eye(128, dtype=np.float32).astype(ml_dtypes.bfloat16)
    rot = _rot_tables()
    gam = 1.0 - 2.0 ** (-5.0 - np.arange(4, dtype=np.float64))
    dmask = np.zeros((2, 128, 4, 128), np.float32)
    ii = np.arange(128)
    for bi, blk in enumerate((128, 8)):
        same = (ii[:, None] // blk) == (ii[None, :] // blk)
        dif = ii[None, :] - ii[:, None]
        ok = (dif >= 0) & same
        for h in range(4):
            dmask[bi, :, h, :] = np.where(ok, gam[h] ** np.maximum(dif, 0) * 256.0 ** -0.5, 0.0)
    common = dict(
        wia=np.ascontiguousarray(I["w_in_a"][0], f), woa=np.ascontiguousarray(I["w_out_a"][0], f),
        wib=np.ascontiguousarray(I["w_in_b"][0], f), wob=np.ascontiguousarray(I["w_out_b"][0], f),
        cst=cst, idb=idb, fnw=np.ascontiguousarray(I["final_norm_w"].reshape(1, D), f), rot=rot, dmask=dmask)
    maps = []
    for c in range(NCORES):
        m = dict(common)
        m["xp"] = np.ascontiguousarray(I["x_prompt"][c], f)
        m["xs"] = np.ascontiguousarray(I["x_sample"][c * NSEQ:(c + 1) * NSEQ].reshape(128, D), f)
        m["sg"] = np.ascontiguousarray(I["state_gdn_ssm"][0, c * NSEQ:(c + 1) * NSEQ], f)
        m["sc"] = np.ascontiguousarray(I["state_gdn_conv"][0, c * NSEQ:(c + 1) * NSEQ].reshape(48, 3072), f)
        m["sr"] = np.ascontiguousarray(I["state_ret"][0, c * NSEQ:(c + 1) * NSEQ], f)
        maps.append(m)
    return maps


_NC_CACHE = {}


def kernel(**inputs):
    I = {k: np.asarray(v) for k, v in inputs.items()}
    if "nc" not in _NC_CACHE:
        _NC_CACHE["nc"] = build_program()
    nc = _NC_CACHE["nc"]
    in_maps = make_in_maps(I)
    res = run_bass_kernel_spmd(nc, in_maps, core_ids=list(range(NCORES))).results
    f = np.float32
    yp = np.stack([res[c]["yp"] for c in range(NCORES)]).astype(f)
    ys = np.concatenate([res[c]["ys"].reshape(NSEQ, 8, D) for c in range(NCORES)], 0).astype(f)
    sap = np.stack([res[c]["sap"] for c in range(NCORES)])[None].astype(f)
    cap = np.stack([res[c]["cap"] for c in range(NCORES)])[None].astype(f)
    sbp = np.stack([res[c]["sbp"] for c in range(NCORES)])[None].astype(f)
    sas = np.concatenate([res[c]["sas"] for c in range(NCORES)], 0)[None].astype(f)
    cas = np.concatenate([res[c]["cas"].reshape(NSEQ, 3, 3072) for c in range(NCORES)], 0)[None].astype(f)
    sbs = np.concatenate([res[c]["sbs"] for c in range(NCORES)], 0)[None].astype(f)
    return (yp, ys, sap, cap, sbp, sas, cas, sbs)
```

```python
import numpy as np
import ml_dtypes
from contextlib import ExitStack
import concourse.bass as bass
import concourse.mybir as mybir
from concourse.bass_utils import run_bass_kernel_spmd

F32 = mybir.dt.float32
F32R = mybir.dt.float32r
BF16 = mybir.dt.bfloat16
AF = mybir.ActivationFunctionType
ALU = mybir.AluOpType
AX = mybir.AxisListType

NCORES = 8
D = 1024
LP = 2048
NPT = LP // 128
NSEQ = 16
EPS = 1e-6
NEG = -32768.0


class Res:
    __slots__ = ("name", "w", "r", "excl", "strict")

    def __init__(self, name):
        self.name = name
        self.excl = False
        self.strict = False
        self.w = None
        self.r = {}


class V:
    __slots__ = ("ap", "res")

    def __init__(self, ap, res):
        self.ap = ap
        self.res = res


class TT:
    def __init__(self, t, name, nres=1):
        self.t = t
        self.res = Res(name)

    def __getitem__(self, idx):
        return V(self.t[idx], self.res)

    def v(self, ap):
        return V(ap, self.res)


class STT(TT):
    def __init__(self, t, name, n, slot_size):
        self.t = t
        self.n = n
        self.ss = slot_size
        self.slots = [Res(f"{name}_{i}") for i in range(n // slot_size)]
        self.res = tuple(self.slots)

    def __getitem__(self, idx):
        key = idx[1] if isinstance(idx, tuple) and len(idx) > 1 else slice(None)
        if isinstance(key, int):
            lo = hi = key
        else:
            lo = key.start or 0
            hi = (key.stop if key.stop is not None else self.n) - 1
        rs = tuple(self.slots[lo // self.ss:hi // self.ss + 1])
        return V(self.t[idx], rs if len(rs) > 1 else rs[0])


def _flat(xs):
    out = []
    for x in xs:
        r = x.res if isinstance(x, (V, TT)) else x
        if isinstance(r, tuple):
            out.extend(r)
        else:
            out.append(r)
    return out


class Chan:
    def __init__(self, sem):
        self.sem = sem
        self.count = 0


class EngQ:
    def __init__(self, name, eng, sem):
        self.name = name
        self.eng = eng
        self.sem = sem
        self.count = 0
        self.seen = {}


class KB:
    def __init__(self, nc, es):
        self.nc = nc
        self.es = es
        self.q = {}
        for name, eng in (("pe", nc.tensor), ("act", nc.scalar), ("dve", nc.vector),
                          ("pool", nc.gpsimd), ("sp", nc.sync)):
            sem = es.enter_context(nc.semaphore("sem_" + name))
            self.q[name] = EngQ(name, eng, sem)
        self.chans = []
        self.n_instr = 0

    def sbs(self, name, shape, dt, slot_size, es=None):
        t = (es or self.es).enter_context(self.nc.sbuf_tensor("s_" + name, list(shape), dt))
        return STT(t, name, shape[1], slot_size)

    def sb(self, name, shape, dt, es=None):
        t = (es or self.es).enter_context(self.nc.sbuf_tensor("s_" + name, list(shape), dt))
        return TT(t, name)

    def chan(self, name):
        sem = self.es.enter_context(self.nc.semaphore("ch_" + name))
        c = Chan(sem)
        self.chans.append(c)
        return c

    def _need(self, q, ev):
        sem, val, owner = ev
        if q.seen.get(sem.num, 0) >= val:
            return
        q.eng.wait_ge(sem, val)
        q.seen[sem.num] = val

    def _deps(self, q, reads, writes):
        me = q.name
        for r in reads:
            if r is None:
                continue
            if r.w is not None:
                self._need(q, r.w)
            if r.excl:
                for ev in r.r.values():
                    if ev[2] != me:
                        self._need(q, ev)
        for w in writes:
            if w is None:
                continue
            if w.w is not None:
                if w.w[2] != me or me == "pool" or w.strict:
                    self._need(q, w.w)
            for ev in w.r.values():
                if ev[2] != me or me == "pool":
                    self._need(q, ev)

    def _record(self, ev, reads, writes):
        for r in reads:
            if r is not None:
                r.r[ev[0].num] = ev
        for w in writes:
            if w is not None:
                w.w = ev
                w.r = {}

    def op(self, eng, fn, reads, writes, inc=True):
        q = self.q[eng]
        reads = _flat(reads)
        writes = _flat(writes)
        self._deps(q, reads, writes)
        ins = fn(q.eng)
        self.n_instr += 1
        if inc:
            ins.then_inc(q.sem, 1)
            q.count += 1
            ev = (q.sem, q.count, q.name)
        else:
            ev = (q.sem, q.count + 1, q.name)
        self._record(ev, reads, writes)
        return ins

    def dma(self, qname, out, in_, chan, **kw):
        q = self.q[qname]
        reads = _flat([in_])
        writes = _flat([out])
        self._deps(q, reads, writes)
        ins = q.eng.dma_start(out=out.ap, in_=in_.ap, **kw)
        ins.then_inc(chan.sem, 16)
        chan.count += 16
        ev = (chan.sem, chan.count, "dma")
        self._record(ev, reads, writes)
        self.n_instr += 1
        return ev

    def barrier(self):
        evs = [(q.sem, q.count, q.name) for q in self.q.values() if q.count > 0]
        evs += [(c.sem, c.count, "dma") for c in self.chans if c.count > 0]
        for q in self.q.values():
            for ev in evs:
                if ev[2] == q.name:
                    continue
                self._need(q, ev)

    def final_wait(self):
        q = self.q["sp"]
        for c in self.chans:
            if c.count > 0:
                self._need(q, (c.sem, c.count, "dma"))
        for qq in self.q.values():
            if qq.name != "sp" and qq.count > 0:
                self._need(q, (qq.sem, qq.count, qq.name))

    def mm(self, out, lhsT, rhs, start=True, stop=True, inc=None):
        if inc is None:
            inc = stop
        return self.op("pe", lambda e: e.matmul(out.ap, lhsT.ap, rhs.ap, start=start, stop=stop,
                                                skip_group_check=True),
                       [lhsT, rhs], [out], inc=inc)

    def tr(self, out, in_, ident, inc=True):
        return self.op("pe", lambda e: e.transpose(out.ap, in_.ap, ident.ap), [in_, ident], [out], inc=inc)

    def act(self, out, in_, func, bias=None, scale=None, accum=None, extra_reads=()):
        kw = {}
        reads = [in_] + list(extra_reads)
        writes = [out]
        if bias is not None:
            if isinstance(bias, V):
                kw["bias"] = bias.ap
                reads.append(bias)
            else:
                kw["bias"] = bias
        if scale is not None:
            if isinstance(scale, V):
                kw["scale"] = scale.ap
                reads.append(scale)
            else:
                kw["scale"] = scale
        if accum is not None:
            kw["accum_out"] = accum.ap
            writes.append(accum)
        return self.op("act", lambda e: e.activation(out.ap, in_.ap, func, **kw), reads, writes)

    def tt(self, eng, out, in0, in1, op):
        return self.op(eng, lambda e: e.tensor_tensor(out.ap, in0.ap, in1.ap, op), [in0, in1], [out])

    def ts(self, eng, out, in0, s1, op0, s2=None, op1=None):
        reads = [in0]
        a1 = s1
        a2 = s2
        if isinstance(s1, V):
            reads.append(s1)
            a1 = s1.ap
        if isinstance(s2, V):
            reads.append(s2)
            a2 = s2.ap
        if op1 is None:
            return self.op(eng, lambda e: e.tensor_scalar(out.ap, in0.ap, a1, None, op0), reads, [out])
        return self.op(eng, lambda e: e.tensor_scalar(out.ap, in0.ap, a1, a2, op0, op1), reads, [out])

    def stt(self, out, in0, scalar, in1, op0, op1):
        reads = [in0, in1]
        a = scalar
        if isinstance(scalar, V):
            reads.append(scalar)
            a = scalar.ap
        return self.op("dve", lambda e: e.scalar_tensor_tensor(out.ap, in0.ap, a, in1.ap, op0, op1),
                       reads, [out])

    def copy(self, eng, out, in_):
        if eng == "act":
            return self.op("act", lambda e: e.copy(out.ap, in_.ap), [in_], [out])
        return self.op(eng, lambda e: e.tensor_copy(out.ap, in_.ap), [in_], [out])

    def memset(self, eng, out, val):
        return self.op(eng, lambda e: e.memset(out.ap, val), [], [out])

    def reduce_sum(self, out, in_):
        return self.op("dve", lambda e: e.tensor_reduce(out.ap, in_.ap, AX.X, ALU.add), [in_], [out])


def bc(v, shape):
    return V(v.ap.broadcast_to(list(shape)), v.res)


def un(v, axis):
    return V(v.ap.unsqueeze(axis), v.res)


def _mask_set(blk):
    i = np.arange(128)
    same = (i[:, None] // blk) == (i[None, :] // blk)
    triU = ((i[:, None] <= i[None, :]) & same).astype(np.float32)
    blkm = same.astype(np.float32)
    triSU = ((i[:, None] > i[None, :]) & same).astype(np.float32)
    strict = ((i[:, None] > i[None, :]) & same).astype(np.float32)
    incl = (i[:, None] >= i[None, :]) & same
    maskneg = np.where(incl, 0.0, NEG).astype(np.float32)
    masknegT = np.ascontiguousarray(maskneg.T)
    return dict(triU=triU, blk=blkm, triSU=triSU, strict=strict, maskneg=maskneg, masknegT=masknegT)


def _cst_layout():
    off = {}
    o = 0

    def add(name, n):
        nonlocal o
        off[name] = (o, n)
        o += n
    add("identf", 128)
    for s in ("p", "s"):
        for nm in ("triU", "blk", "triSU", "strict", "maskneg", "masknegT"):
            add(nm + "_" + s, 128)
    add("nwT0", 8)
    add("nwT1", 8)
    add("cwT", 96)
    add("dtb", 8)
    add("alog", 8)
    add("onwa", 1)
    add("onwbT", 16)
    add("seqmask", 16)
    add("rdec_p", 8)
    add("rdec_s", 8)
    return off, o


CST_OFF, CST_N = _cst_layout()


def _build_cst(norm_w, conv_w_a, a_log_a, dt_bias_a, onorm_a, onorm_b):
    c = np.zeros((128, CST_N), np.float32)

    def put(name, arr):
        o, n = CST_OFF[name]
        c[:, o:o + n] = arr
    put("identf", np.eye(128, dtype=np.float32))
    for s, blk in (("p", 128), ("s", 8)):
        m = _mask_set(blk)
        for nm in ("triU", "blk", "triSU", "strict", "maskneg", "masknegT"):
            put(nm + "_" + s, m[nm])
    put("nwT0", norm_w[0].reshape(8, 128).T)
    put("nwT1", norm_w[1].reshape(8, 128).T)
    cw = conv_w_a[0].reshape(4, 24, 128)
    put("cwT", np.transpose(cw, (2, 1, 0)).reshape(128, 96))
    put("dtb", np.broadcast_to(dt_bias_a[0][None, :], (128, 8)))
    put("alog", np.broadcast_to(a_log_a[0][None, :], (128, 8)))
    put("onwa", onorm_a[0].reshape(128, 1))
    put("onwbT", onorm_b[0].reshape(16, 128).T)
    sm = np.zeros((128, 16), np.float32)
    sm[np.arange(128), np.arange(128) // 8] = 1.0
    put("seqmask", sm)
    gam = 1.0 - 2.0 ** (-5.0 - np.arange(4, dtype=np.float64))
    for nm, blk in (("rdec_p", 128), ("rdec_s", 8)):
        t = (np.arange(128) % blk).astype(np.float64)
        qd = gam[None, :] ** (t[:, None] + 1.0)
        kd = gam[None, :] ** (blk - 1.0 - t[:, None]) * 256.0 ** -0.5
        put(nm, np.concatenate([qd, kd], 1))
    return c


def ext(ap):
    return V(ap, None)


class PsumPool:
    def __init__(self, kb, n=8):
        self.banks = [TT(kb.es.enter_context(kb.nc.psum_tensor(f"psb{i}", [128, 512], F32)), f"psb{i}")
                      for i in range(n)]
        for b in self.banks:
            b.res.excl = True
        self.i = 0

        self.held = set()

    def next(self, hold=False):
        while True:
            b = self.banks[self.i % len(self.banks)]
            self.i += 1
            if id(b) not in self.held:
                break
        if hold:
            self.held.add(id(b))
        return b

    def release(self, b):
        self.held.discard(id(b))


def bfv(bank):
    return V(bank.t[:].bitcast(BF16), bank.res)


class Ring:
    def __init__(self, kb, name, n, shape, dt, es=None, chan=True):
        self.slots = [kb.sb(f"{name}{i}", shape, dt, es) for i in range(n)]
        self.chans = [kb.chan(f"{name}{i}") for i in range(n)] if chan else None
        self.i = 0

    def next(self):
        j = self.i % len(self.slots)
        self.i += 1
        return self.slots[j], (self.chans[j] if self.chans else None)


class Cut(Exception):
    pass


def build_program(n_ptiles=NPT, do_sample=True, do_l1=True, dbg=False, cut=None):
    def chk(n):
        if cut is not None and cut == n:
            raise Cut()

    nc = bass.Bass("TRN2", target_bir_lowering=False)
    es = ExitStack()
    kb = KB(nc, es)

    def din(name, shape, dt=F32):
        return nc.dram_tensor(name, list(shape), dt, kind="ExternalInput").ap()

    def dout(name, shape, dt=F32):
        return nc.dram_tensor(name, list(shape), dt, kind="ExternalOutput").ap()

    xp_d = din("xp", [LP, D])
    xs_d = din("xs", [128, D])
    sg_d = din("sg", [NSEQ, 8, 128, 128])
    sc_d = din("sc", [48, 3072])
    sr_d = din("sr", [NSEQ, 4, 256, 512])
    wia_d = din("wia", [D, 4112])
    woa_d = din("woa", [D, D])
    wib_d = din("wib", [D, 6144])
    wob_d = din("wob", [2048, D])
    cst_d = din("cst", [128, CST_N])
    idb_d = din("idb", [128, 128], BF16)
    fnw_d = din("fnw", [1, D])
    rot_d = din("rot", [NPT + 1, 128, 2, 128])
    dmask_d = din("dmask", [2, 128, 4, 128])

    yp_d = dout("yp", [LP, D])
    ys_d = dout("ys", [128, D])
    sap_d = dout("sap", [8, 128, 128])
    cap_d = dout("cap", [3, 3072])
    sbp_d = dout("sbp", [4, 256, 512])
    sas_d = dout("sas", [NSEQ, 8, 128, 128])
    cas_d = dout("cas", [48, 3072])
    sbs_d = dout("sbs", [NSEQ, 4, 256, 512])
    x1_d = nc.dram_tensor("x1s", [LP + 128, D], F32, kind="Internal").ap()
    x1_res = [Res(f"x1_{i}") for i in range(NPT + 1)]
    dbg_d = {}
    if dbg:
        for nm, shp in (("d_y1", [128 * (n_ptiles + 1), D]), ("d_mixed", [128, 1536]), ("d_misc", [128, 64]),
                        ("d_dec", [128, 512]), ("d_Y", [128, 512]), ("d_on", [128, 1024])):
            dbg_d[nm] = dout(nm, shp)

    P = PsumPool(kb)
    ch_c = kb.chan("const")
    ch_dbg = kb.chan("dbg")

    cst = kb.sb("cst", [128, CST_N], F32)
    idb = kb.sb("idb_s", [128, 128], BF16)
    kb.dma("sp", cst[:], ext(cst_d), ch_c)
    kb.dma("sp", idb[:], ext(idb_d), ch_c)
    cst.res.w = (ch_c.sem, ch_c.count, "dma")
    idb.res.w = (ch_c.sem, ch_c.count, "dma")

    def C(name, lo=0, hi=None):
        o, n = CST_OFF[name]
        hi = n if hi is None else hi
        return cst[:, o + lo:o + hi]

    neghalf = kb.sb("neghalf", [128, 16], F32)
    kb.memset("dve", neghalf[:], -0.5)
    identf = C("identf")

    def load_masks(s):
        kb.copy("dve", mr["triU"][:], C("triU_" + s))
        kb.ts("dve", mr["negtriU"][:], C("triU_" + s), -1.0, ALU.mult)
        kb.ts("dve", negtriUf[:], C("triU_" + s), -1.0, ALU.mult)
        kb.copy("dve", mr["ones"][:], onesf[:])
        kb.ts("dve", mr["negones"][:], onesf[:], -1.0, ALU.mult)
        kb.copy("dve", mr["ident"][:], identf)
        for nm in ("maskneg", "masknegT"):
            src = C(nm + "_" + s)
            kb.copy("dve", V(mr[nm].t[:].rearrange("p (h j) -> p h j", h=4), mr[nm].res),
                    bc(un(src, 1), [128, 4, 128]))

    xt = kb.sb("xt", [128, D], F32)
    yt = kb.sb("yt", [128, D], F32)
    ch_xt = kb.chan("xt")
    ch_yt = kb.chan("yt")
    xs_b = kb.sb("xs_b", [128, D], BF16)
    xs_b.res.strict = True
    xnT = [kb.sb(f"xnT{i}", [128, 8, 128], BF16) for i in range(2)]
    st4 = kb.sb("st4", [128, 16], F32)
    es0 = ExitStack()
    mr = {}
    for nm in ("triU", "negtriU", "ones", "negones", "ident", "maskneg", "masknegT"):
        w = 512 if nm.startswith("maskneg") else 128
        mr[nm] = kb.sb("mr_" + nm, [128, w], F32R, es0)
    onesf = kb.sb("onesf", [128, 128], F32, es0)
    kb.memset("dve", onesf[:], 1.0)
    negtriUf = kb.sb("negtriUf", [128, 128], F32, es0)


    ch_w0 = kb.chan("w0")
    wia = [kb.sb(f"wia{k}", [128, 4112], BF16, es0) for k in range(8)]
    pieces = ((0, 1536), (1536, 3072), (3072, 4112))
    for k in range(8):
        for (c0, c1) in pieces:
            kb.dma("pool", wia[k][:, c0:c1], ext(wia_d[k * 128:(k + 1) * 128, c0:c1]), ch_w0)
    woa = [kb.sb(f"woa{k}", [128, 1024], BF16, es0) for k in range(8)]
    for k in range(8):
        st, chn = (xt, ch_xt) if k % 2 == 0 else (yt, ch_yt)
        kb.dma("sp", st[:], ext(woa_d[k * 128:(k + 1) * 128, :]), chn)
        kb.ts("dve", woa[k][:], st[:], C("onwa"), ALU.mult)
    for k in range(8):
        wia[k].res.w = (ch_w0.sem, ch_w0.count, "dma")

    diag = kb.sb("diag", [128, 96, 128], BF16, es0)
    for i in range(96):
        kb.ts("dve", diag[:, i, :], idb[:], C("cwT", i, i + 1), ALU.mult)
    negA = kb.sb("negA", [128, 8], F32, es0)
    kb.act(negA[:], C("alog"), AF.Exp)
    kb.ts("dve", negA[:], negA[:], -1.0, ALU.mult)

    def front_end(src_v, nwname, par):
        kb.dma("sp", xt[:], src_v, ch_xt)
        kb.act(xs_b[:], xt[:], AF.Square, accum=st4[:, 0:1])
        yield
        kb.ts("dve", st4[:, 1:2], st4[:, 0:1], 1.0 / D, ALU.mult, EPS, ALU.add)
        kb.tt("pool", st4[:, 2:3], st4[:, 1:2], neghalf[:, 0:1], ALU.pow)
        yield
        kb.ts("dve", xs_b[:], xt[:], st4[:, 2:3], ALU.mult)
        yield
        bank = P.next()
        pv = bfv(bank)
        for k in range(8):
            kb.tr(V(pv.ap[:, k * 128:(k + 1) * 128], pv.res), xs_b[:, k * 128:(k + 1) * 128], idb[:], inc=(k == 7))
        kb.tt("dve", xnT[par][:], V(pv.ap.rearrange("p (k t) -> p k t", k=8), pv.res),
              bc(un(C(nwname), 2), [128, 8, 128]), ALU.mult)
        yield

    pT = [kb.sb(f"pT{i}", [128, 12, 176], BF16, es0) for i in range(2)]
    hist = [kb.sb(f"hist{i}", [128, 12, 3], BF16, es0) for i in range(2)]
    for h_ in hist:
        kb.memset("pool", h_[:], 0.0)
    mixed = [kb.sb(f"mixed{i}", [128, 1536], BF16, es0) for i in range(2)]
    zs = [kb.sb(f"zs{i}", [128, 512], BF16, es0) for i in range(2)]
    ba = kb.sb("ba", [128, 16], F32, es0)
    sc8 = kb.sb("sc8", [128, 64], F32, es0)
    E = kb.sb("E", [128, 24], F32, es0)
    sqb = kb.sb("sqb", [128, 1024], BF16, es0)
    r8 = kb.sb("r8", [128, 16], F32, es0)
    sv = {nm: kb.sb("sv_" + nm, [128, 4, 128], BF16, es0) for nm in ("qn", "qe", "kn", "kw", "kd", "vb")}
    qT = kb.sb("qT", [128, 8, 128], BF16, es0)
    knT = kb.sb("knT", [128, 4, 128], BF16, es0)
    gm = kb.sb("gm", [128, 4, 128], F32R, es0)
    gb = kb.sb("gb", [128, 4, 128], F32R, es0)
    dec = kb.sb("dec", [128, 4, 128], F32, es0)
    decT = kb.sb("decT", [128, 4, 128], BF16, es0)
    Pb = [kb.sb(f"Pb{i}", [128, 4, 128], BF16, es0) for i in range(2)]
    PTb = [kb.sb(f"PTb{i}", [128, 4, 128], BF16, es0) for i in range(2)]
    Yb = [kb.sb(f"Yb{i}", [128, 4, 128], BF16, es0) for i in range(2)]
    Mb = kb.sb("Mb", [128, 4, 128], BF16, es0)
    MTb = kb.sb("MTb", [128, 4, 128], BF16, es0)
    negidb = kb.sb("negidb", [128, 128], BF16, es0)
    kb.ts("dve", negidb[:], idb[:], -1.0, ALU.mult)
    negWT = kb.sb("negWT", [128, 4, 128], BF16, es0)
    qkdT = kb.sb("qkdT", [128, 4, 128], BF16, es0)
    S = kb.sbs("S", [128, 8, 128], F32, 4, es0)
    Sbf = kb.sbs("Sbf", [128, 8, 128], BF16, 4, es0)
    ub = kb.sb("ub", [128, 4, 128], BF16, es0)
    otmp = kb.sb("otmp", [128, 512], BF16, es0)
    on = kb.sb("on", [128, 1024], BF16, es0)
    onT = kb.sb("onT", [128, 8, 128], BF16, es0)
    ch_out = kb.chan("out_small")
    ch_sbf = [kb.chan("sbf0"), kb.chan("sbf1")]
    ch_sf = [kb.chan("sf0"), kb.chan("sf1")]
    ch_sfo = [kb.chan("sfo0"), kb.chan("sfo1")]
    if do_sample:
        cv = kb.sb("cv", [128, 12, 128], BF16, es0)
        histT = kb.sb("histT", [128, 24, 48], BF16, es0)
        abc = kb.sb("abc", [128, 64], F32, es0)
        gmsk = kb.sb("gmsk", [128, 16, 4], F32, es0)
        uTb = kb.sb("uTb", [128, 4, 128], BF16, es0)

    beta = sc8[:, 0:8]
    negbeta = sc8[:, 8:16]
    gv = sc8[:, 24:32]

    def v3(tt_, h=4):
        return V(tt_.t[:].rearrange("p (h d) -> p h d", h=h), tt_.res)

    def sc_b(vw):
        return bc(un(vw, 2), [128, 4, 128])

    sc8_2 = [sc8, kb.sb("sc8_b", [128, 64], F32, es0)]
    E_2 = [E, kb.sb("E_b", [128, 24], F32, es0)]
    sv_2 = [sv, {nm: kb.sb("svb_" + nm, [128, 4, 128], BF16, es0) for nm in ("qn", "qe", "kn", "kw", "kd", "vb")}]
    qT_2 = [qT, kb.sb("qT_b", [128, 8, 128], BF16, es0)]
    knT_2 = [knT, kb.sb("knT_b", [128, 4, 128], BF16, es0)]
    osq = kb.sb("osq", [128, 512], BF16, es0)

    def gdn_prologue(ti, is_sample):
        sc8 = sc8_2[ti % 2]
        E = E_2[ti % 2]
        beta = sc8[:, 0:8]
        negbeta = sc8[:, 8:16]
        gv = sc8[:, 24:32]
        par = ti % 2
        src = ext(xs_d[:, :]) if is_sample else ext(xp_d[ti * 128:(ti + 1) * 128, :])
        yield from front_end(src, "nwT0", par)
        xn = xnT[par]
        chk(2)
        bk = P.next()
        for k in range(8):
            kb.mm(bk[:, 0:16], xn[:, k, :], wia[k][:, 4096:4112], start=(k == 0), stop=(k == 7))
        kb.copy("dve", ba[:], bk[:, 0:16])
        yield
        kb.act(sc8[:, 56:64], ba[:, 0:8], AF.Tanh, scale=0.5)
        kb.ts("dve", negbeta, sc8[:, 56:64], -0.5, ALU.mult, -0.5, ALU.add)
        kb.ts("dve", beta, sc8[:, 56:64], 0.5, ALU.mult, 0.5, ALU.add)
        yield
        kb.tt("dve", sc8[:, 16:24], ba[:, 8:16], C("dtb"), ALU.add)
        kb.act(sc8[:, 16:24], sc8[:, 16:24], AF.Exp)
        kb.act(sc8[:, 16:24], sc8[:, 16:24], AF.Ln, bias=1.0)
        yield
        kb.tt("dve", gv, sc8[:, 16:24], negA[:], ALU.mult)
        yield
        sfx = "_s" if is_sample else "_p"
        bk = P.next()
        kb.mm(bk[:, 0:8], C("triU" + sfx), gv)
        kb.mm(bk[:, 8:16], C("blk" + sfx), gv)
        kb.mm(bk[:, 16:24], C("triSU" + sfx), gv)
        kb.act(E[:], bk[:, 0:24], AF.Exp)
        chk(3)
        yield

    def gdn_stage1(ti, hg, ii, is_sample):
        par = ti % 2
        xn = xnT[par]
        sc8 = sc8_2[ti % 2]
        E = E_2[ti % 2]
        sv, qT, knT = sv_2[ii % 2], qT_2[ii % 2], knT_2[ii % 2]
        sfx = "_s" if is_sample else "_p"
        h0 = hg * 4
        pt = pT[hg]
        chunks = [h0 + i for i in range(4)] + [8 + h0 + i for i in range(4)] + [16 + h0 + i for i in range(4)]
        if is_sample:
            F4 = V(pt.t[:].rearrange("p c (s r) -> p c s r", r=11), pt.res)
            hT4 = V(histT.t[:].rearrange("p c (s r) -> p c s r", r=3), histT.res)
        else:
            kb.copy("pool", pt[:, :, 0:3], hist[hg][:])
        for grp in range(3):
            bk = P.next()
            for ci in range(4):
                col = chunks[grp * 4 + ci] * 128
                for k in range(8):
                    kb.mm(bk[:, ci * 128:(ci + 1) * 128], wia[k][:, col:col + 128], xn[:, k, :],
                          start=(k == 0), stop=(k == 7), inc=(k == 7 and ci == 3))
            eng = "act" if grp % 2 == 0 else "dve"
            if is_sample:
                c0 = chunks[grp * 4]
                kb.copy(eng, V(F4.ap[:, grp * 4:(grp + 1) * 4, :, 3:11], pt.res),
                        V(bk.t[:].rearrange("p (c s t) -> p c s t", c=4, s=16), bk.res))
                kb.copy("pool", V(F4.ap[:, grp * 4:(grp + 1) * 4, :, 0:3], pt.res),
                        V(hT4.ap[:, c0:c0 + 4, :, :], histT.res))
            else:
                kb.copy(eng, pt[:, grp * 4:(grp + 1) * 4, 3:131], v3(bk))
                yield
        if not is_sample:
            kb.copy("pool", hist[hg][:], pt[:, :, 128:131])
        bk = P.next()
        for k in range(8):
            kb.mm(bk[:], xn[:, k, :], wia[k][:, 3072 + hg * 512:3072 + (hg + 1) * 512],
                  start=(k == 0), stop=(k == 7))
        kb.act(zs[hg][:], bk[:], AF.Silu)
        yield
        mx = mixed[hg]
        if is_sample:
            for c in range(12):
                cg = chunks[c]
                cvv = V(cv.t[:, c, :].rearrange("p (s t) -> p s t", t=8), cv.res)
                kb.ts("dve", cvv, V(F4.ap[:, c, :, 0:8], pt.res), C("cwT", cg * 4, cg * 4 + 1), ALU.mult)
                for j in range(1, 4):
                    kb.stt(cvv, V(F4.ap[:, c, :, j:j + 8], pt.res), C("cwT", cg * 4 + j, cg * 4 + j + 1),
                           cvv, ALU.mult, ALU.add)
        for grp in range(3):
            bk = P.next()
            for ci in range(4):
                cg = chunks[grp * 4 + ci]
                if is_sample:
                    kb.mm(bk[:, ci * 128:(ci + 1) * 128], cv[:, grp * 4 + ci, :], idb[:], inc=(ci == 3))
                else:
                    for j in range(4):
                        kb.mm(bk[:, ci * 128:(ci + 1) * 128], pt[:, grp * 4 + ci, j:j + 128],
                              diag[:, cg * 4 + j, :], start=(j == 0), stop=(j == 3), inc=(j == 3 and ci == 3))
            kb.act(mx[:, grp * 512:(grp + 1) * 512], bk[:], AF.Silu)
            yield
        chk(4)
        kb.act(sqb[:], mx[:, 0:1024], AF.Square)
        kb.reduce_sum(r8[:, 0:8], v3(sqb, 8))
        yield
        kb.ts("dve", r8[:, 0:8], r8[:, 0:8], EPS, ALU.add)
        kb.tt("pool", r8[:, 8:16], r8[:, 0:8], neghalf[:, 0:8], ALU.pow)
        yield
        rq = r8[:, 8:12]
        rk = r8[:, 12:16]
        eG = E[:, h0:h0 + 4]
        ekl = E[:, 16 + h0:16 + h0 + 4]
        kb.ts("dve", sc8[:, 32:36], rq, 128 ** -0.5, ALU.mult)
        kb.tt("dve", sc8[:, 36:40], sc8[:, 32:36], eG, ALU.mult)
        kb.tt("dve", sc8[:, 40:44], rk, eG, ALU.mult)
        kb.tt("dve", sc8[:, 40:44], sc8[:, 40:44], sc8[:, h0:h0 + 4], ALU.mult)
        kb.tt("dve", sc8[:, 44:48], rk, ekl, ALU.mult)
        yield
        qv = V(mx.t[:, 0:512].rearrange("p (h d) -> p h d", h=4), mx.res)
        kv = V(mx.t[:, 512:1024].rearrange("p (h d) -> p h d", h=4), mx.res)
        vv = V(mx.t[:, 1024:1536].rearrange("p (h d) -> p h d", h=4), mx.res)
        kb.tt("dve", sv["qn"][:], qv, sc_b(sc8[:, 32:36]), ALU.mult)
        kb.tt("pool", sv["qe"][:], qv, sc_b(sc8[:, 36:40]), ALU.mult)
        yield
        kb.tt("dve", sv["kn"][:], kv, sc_b(rk), ALU.mult)
        kb.tt("pool", sv["kw"][:], kv, sc_b(sc8[:, 40:44]), ALU.mult)
        yield
        kb.tt("dve", sv["kd"][:], kv, sc_b(sc8[:, 44:48]), ALU.mult)
        kb.tt("pool", sv["vb"][:], vv, sc_b(sc8[:, h0:h0 + 4]), ALU.mult)
        yield
        bk = P.next()
        pv = bfv(bk)
        for i, nm in enumerate(("qn", "qe")):
            for h in range(4):
                c0 = (i * 4 + h) * 128
                kb.tr(V(pv.ap[:, c0:c0 + 128], pv.res), sv[nm][:, h, :], idb[:], inc=(i == 1 and h == 3))
        kb.copy("act", qT[:], V(pv.ap.rearrange("p (k t) -> p k t", k=8), pv.res))
        yield
        bk = P.next()
        pv = bfv(bk)
        for h in range(4):
            kb.tr(V(pv.ap[:, h * 128:(h + 1) * 128], pv.res), sv["kn"][:, h, :], idb[:], inc=(h == 3))
        kb.copy("dve", knT[:], V(pv.ap[:, 0:512].rearrange("p (k t) -> p k t", k=4), pv.res))

    def gdn_stage2(ti, hg, ii, is_sample):
        if is_sample:
            sample_prefetch(hg)
        sc8 = sc8_2[ti % 2]
        E = E_2[ti % 2]
        sv, qT, knT = sv_2[ii % 2], qT_2[ii % 2], knT_2[ii % 2]
        sfx = "_s" if is_sample else "_p"
        h0 = hg * 4
        mx = mixed[hg]
        chk(5)
        kb.tt("dve", gm[:], bc(un(sc8[:, 24 + h0:24 + h0 + 4], 2), [128, 4, 128]),
              bc(un(negtriUf[:], 1), [128, 4, 128]), ALU.mult)
        kb.copy("act", gb[:], bc(un(sc8[:, 24 + h0:24 + h0 + 4], 2), [128, 4, 128]))
        yield
        gmf = V(gm.t[:].rearrange("p h j -> p (h j)"), gm.res)
        gbf = V(gb.t[:].rearrange("p h j -> p (h j)"), gb.res)
        chk(51)
        bk = P.next()
        kb.mm(bk[:], mr["triU"][:], gbf, start=True, stop=False)
        kb.mm(bk[:], mr["ones"][:], gmf, start=False, stop=False)
        kb.mm(bk[:], mr["ident"][:], mr["maskneg"][:], start=False, stop=True)
        chk(52)
        kb.act(V(dec.t[:].rearrange("p h j -> p (h j)"), dec.res), bk[:], AF.Exp)
        yield
        chk(53)
        bk = P.next()
        kb.mm(bk[:], mr["negtriU"][:], gbf, start=True, stop=False)
        kb.mm(bk[:], mr["negones"][:], gmf, start=False, stop=False)
        kb.mm(bk[:], mr["ident"][:], mr["masknegT"][:], start=False, stop=True)
        chk(54)
        kb.act(V(decT.t[:].rearrange("p h j -> p (h j)"), decT.res), bk[:], AF.Exp)
        yield
        chk(55)
        chk(6)
        bk = P.next()
        for h in range(4):
            kb.mm(bk[:, h * 128:(h + 1) * 128], knT[:, h, :], knT[:, h, :], inc=(h == 3))
        kb.tt("dve", dec[:], dec[:], bc(un(C("strict" + sfx), 1), [128, 4, 128]), ALU.mult)
        kb.tt("dve", dec[:], dec[:], sc_b(sc8[:, 8 + h0:8 + h0 + 4]), ALU.mult)
        yield
        Pc, PTc, Yc = Mb, MTb, Yb[0]
        kb.tt("dve", Pc[:], v3(bk), dec[:], ALU.mult)
        yield
        chk(61)
        bk = P.next()
        pv = bfv(bk)
        for h in range(4):
            kb.tr(V(pv.ap[:, h * 128:(h + 1) * 128], pv.res), Pc[:, h, :], idb[:], inc=(h == 3))
        pv4 = V(pv.ap[:, 0:512].rearrange("p (k t) -> p k t", k=4), pv.res)
        kb.copy("act", PTc[:], pv4)
        kb.tt("dve", Yc[:], pv4, bc(un(idb[:], 1), [128, 4, 128]), ALU.add)
        yield
        chk(62)
        nsteps = 3 if is_sample else 6
        for stp in range(1, nsteps):
            Pn, PTn, Yn = Pb[stp % 2], PTb[stp % 2], Yb[stp % 2]
            bkA = P.next()
            for h in range(4):
                kb.mm(bkA[:, h * 128:(h + 1) * 128], PTc[:, h, :], Pc[:, h, :], inc=(h == 3))
            last = (stp == nsteps - 1)
            if not last:
                bkB = P.next()
                for h in range(4):
                    kb.mm(bkB[:, h * 128:(h + 1) * 128], Pc[:, h, :], PTc[:, h, :], inc=(h == 3))
            kb.copy("act", Pn[:], v3(bkA))
            if not last:
                kb.copy("dve", PTn[:], v3(bkB))
                yield
            bkC = P.next()
            for h in range(4):
                kb.mm(bkC[:, h * 128:(h + 1) * 128], Pn[:, h, :], Yc[:, h, :], inc=(h == 3))
            kb.tt("dve", Yn[:], v3(bkC), Yc[:], ALU.add)
            yield
            Pc, PTc, Yc = Pn, PTn, Yn
        bk = P.next()
        pv = bfv(bk)
        for h in range(4):
            kb.tr(V(pv.ap[:, h * 128:(h + 1) * 128], pv.res), Yc[:, h, :], idb[:], inc=(h == 3))
        X0b, Rb = PTb[0], Pb[0]
        kb.copy("act", X0b[:], V(pv.ap[:, 0:512].rearrange("p (k t) -> p k t", k=4), pv.res))
        yield
        bk = P.next()
        for h in range(4):
            kb.mm(bk[:, h * 128:(h + 1) * 128], MTb[:, h, :], X0b[:, h, :], start=True, stop=False)
            kb.mm(bk[:, h * 128:(h + 1) * 128], negidb[:], X0b[:, h, :], start=False, stop=True, inc=(h == 3))
        kb.tt("dve", Rb[:], v3(bk), bc(un(idb[:], 1), [128, 4, 128]), ALU.add)
        yield
        bk = P.next()
        for h in range(4):
            kb.mm(bk[:, h * 128:(h + 1) * 128], Rb[:, h, :], Yc[:, h, :], inc=(h == 3))
        Yn = Yb[1] if Yc is Yb[0] else Yb[0]
        kb.tt("dve", Yn[:], v3(bk), Yc[:], ALU.add)
        Yc = Yn
        chk(63)
        bk = P.next()
        for h in range(4):
            kb.mm(bk[:, h * 128:(h + 1) * 128], sv["kw"][:, h, :], Yc[:, h, :], inc=(h == 3))
        kb.act(negWT[:], v3(bk), AF.Copy, scale=-1.0)
        yield
        bk = P.next()
        for h in range(4):
            kb.mm(bk[:, h * 128:(h + 1) * 128], knT[:, h, :], qT[:, h, :], inc=(h == 3))
        kb.tt("dve", qkdT[:], v3(bk), decT[:], ALU.mult)
        yield
        if dbg and ti == dbg_tile and hg == 0:
            kb.dma("sp", ext(dbg_d["d_misc"][:, 0:64]), sc8[:], ch_dbg)
            kb.dma("sp", ext(dbg_d["d_dec"]), V(dec.t[:].rearrange("p h j -> p (h j)"), dec.res), ch_dbg)
        chk(7)
        if not is_sample:
            first = (ti == 0)
            bu = P.next()
            for h in range(4):
                kb.mm(bu[:, h * 128:(h + 1) * 128], Yc[:, h, :], sv["vb"][:, h, :], start=True, stop=first,
                      inc=(first and h == 3))
                if not first:
                    kb.mm(bu[:, h * 128:(h + 1) * 128], negWT[:, h, :], Sbf[:, h0 + h, :], start=False,
                          stop=True, inc=(h == 3))
            kb.copy("act", ub[:], v3(bu))
            yield
            bo = P.next()
            for h in range(4):
                if not first:
                    kb.mm(bo[:, h * 128:(h + 1) * 128], qT[:, 4 + h, :], Sbf[:, h0 + h, :], start=True,
                          stop=False)
                kb.mm(bo[:, h * 128:(h + 1) * 128], qkdT[:, h, :], ub[:, h, :], start=first, stop=True,
                      inc=(h == 3))
            bs = P.next()
            for h in range(4):
                kb.mm(bs[:, h * 128:(h + 1) * 128], sv["kd"][:, h, :], ub[:, h, :], inc=(h == 3))
            for h in range(4):
                if first:
                    kb.copy("dve", S[:, h0 + h, :], bs[:, h * 128:(h + 1) * 128])
                else:
                    kb.stt(S[:, h0 + h, :], S[:, h0 + h, :], E[:, 8 + h0 + h:8 + h0 + h + 1],
                           bs[:, h * 128:(h + 1) * 128], ALU.mult, ALU.add)
            kb.copy("act", Sbf[:, h0:h0 + 4, :], S[:, h0:h0 + 4, :])
            yield
            o_src = bo[:]
        else:
            o_src = gdn_sample_rec(hg, Yc, sv, qT, sc8)
        chk(8)
        o3 = V(o_src.ap.rearrange("p (h d) -> p h d", h=4), o_src.res)
        kb.act(osq[:], o_src, AF.Square)
        kb.reduce_sum(sc8[:, 48:52], V(osq.t[:].rearrange("p (h d) -> p h d", h=4), osq.res))
        yield
        kb.ts("dve", sc8[:, 48:52], sc8[:, 48:52], 1.0 / 128, ALU.mult, EPS, ALU.add)
        kb.tt("pool", sc8[:, 52:56], sc8[:, 48:52], neghalf[:, 0:4], ALU.pow)
        yield
        kb.tt("dve", v3(otmp), o3, sc_b(sc8[:, 52:56]), ALU.mult)
        kb.tt("dve", on[:, hg * 512:(hg + 1) * 512], otmp[:], zs[hg][:], ALU.mult)
        if dbg and ti == dbg_tile and hg == 0:
            kb.dma("pool", ext(dbg_d["d_mixed"]), mx[:], ch_dbg)
            kb.dma("pool", ext(dbg_d["d_Y"]), otmp[:], ch_dbg)

    def gdn_epilogue(ti, is_sample):
        par = ti % 2
        xn = xnT[par]
        src = ext(xs_d[:, :]) if is_sample else ext(xp_d[ti * 128:(ti + 1) * 128, :])
        kb.dma("sp", yt[:], src, ch_yt)
        chk(9)
        bk = P.next()
        pv = bfv(bk)
        for k in range(8):
            kb.tr(V(pv.ap[:, k * 128:(k + 1) * 128], pv.res), on[:, k * 128:(k + 1) * 128], idb[:], inc=(k == 7))
        kb.copy("act", onT[:], V(pv.ap.rearrange("p (k t) -> p k t", k=8), pv.res))
        yield
        for n in range(2):
            bk = P.next()
            for k in range(8):
                kb.mm(bk[:], onT[:, k, :], woa[k][:, n * 512:(n + 1) * 512], start=(k == 0), stop=(k == 7))
            kb.tt("dve", yt[:, n * 512:(n + 1) * 512], bk[:], yt[:, n * 512:(n + 1) * 512], ALU.add)
            yield
        kb.dma("sp", V(x1_d[ti * 128:(ti + 1) * 128, :], x1_res[ti]), yt[:], ch_yt)
        if dbg:
            dr = n_ptiles if is_sample else ti
            kb.dma("sp", ext(dbg_d["d_y1"][dr * 128:(dr + 1) * 128, :]), yt[:], ch_yt)
        if is_sample or ti == n_ptiles - 1:
            for cb in range(3):
                for half in range(2):
                    bk = P.next()
                    col = cb * 1024 + half * 512
                    for k in range(8):
                        kb.mm(bk[:], xn[:, k, :], wia[k][:, col:col + 512], start=(k == 0), stop=(k == 7))
                    kb.copy("act" if half else "dve", xt[:, half * 512:(half + 1) * 512], bk[:])
                    yield
                if is_sample:
                    for s_ in range(NSEQ):
                        kb.dma("sp", ext(cas_d[s_ * 3:(s_ + 1) * 3, cb * 1024:(cb + 1) * 1024]),
                               xt[s_ * 8 + 5:s_ * 8 + 8, :], ch_xt)
                else:
                    kb.dma("sp", ext(cap_d[:, cb * 1024:(cb + 1) * 1024]), xt[125:128, :], ch_xt)
        if (not is_sample) and ti == n_ptiles - 1:
            kb.dma("sp", ext(sap_d.rearrange("h k v -> k h v")), S[:], ch_out)
        yield


    smpst = {}

    def vi(v, *idx):
        return V(v.ap[idx], v.res)

    def sample_slots():
        if smpst:
            return smpst
        bsl, fsl = [], []
        for i in range(16):
            r = Res(f"smb{i}")
            r.w = diag.res.w
            r.r = dict(diag.res.r)
            bsl.append(V(diag.t[:, i * 4:(i + 1) * 4, :], r))
        for i in range(4):
            r = Res(f"smf{i}")
            r.w = diag.res.w
            r.r = dict(diag.res.r)
            ap = diag.t[:, 64 + i * 8:64 + (i + 1) * 8, :].rearrange("p a b -> p (a b)").bitcast(F32)
            fsl.append(V(ap.rearrange("p (h d) -> p h d", h=4), r))
        smpst["b"] = bsl
        smpst["f"] = fsl
        smpst["chb"] = [kb.chan(f"smb{i}") for i in range(16)]
        smpst["chf"] = [kb.chan(f"smf{i}") for i in range(4)]
        smpst["chfo"] = [kb.chan(f"smfo{i}") for i in range(4)]
        return smpst

    def sample_ld_f(hg, q_):
        if q_ >= NSEQ:
            return
        sm = sample_slots()
        h0 = hg * 4
        kb.dma("sp", sm["f"][q_ % 4], ext(sg_d[q_, h0:h0 + 4].rearrange("h k v -> k h v")), sm["chf"][q_ % 4])

    def sample_prefetch(hg):
        sm = sample_slots()
        h0 = hg * 4
        for q_ in range(NSEQ):
            kb.dma("pool", sm["b"][q_], ext(sg_d[q_, h0:h0 + 4].rearrange("h k v -> k h v")), sm["chb"][q_])
        for q_ in range(3):
            sample_ld_f(hg, q_)

    def gdn_sample_rec(hg, Yc, sv, qT, sc8):
        sm = sample_slots()
        h0 = hg * 4
        kb.tt("dve", gmsk[:], bc(un(sc8[:, 24 + h0:24 + h0 + 4], 1), [128, 16, 4]),
              bc(un(C("seqmask"), 2), [128, 16, 4]), ALU.mult)
        bk = P.next()
        kb.mm(bk[:, 0:64], onesf[:], V(gmsk.t[:].rearrange("p s h -> p (s h)"), gmsk.res))
        kb.act(abc[:], bk[:, 0:64], AF.Exp)
        buT = P.next(hold=True)
        for h in range(4):
            kb.mm(buT[:, h * 128:(h + 1) * 128], sv["vb"][:, h, :], Yc[:, h, :], start=(h == 0), stop=False,
                  inc=False)
        for s_ in range(NSEQ):
            for h in range(4):
                lastmm = (s_ == NSEQ - 1 and h == 3)
                kb.mm(buT[:, h * 128 + s_ * 8:h * 128 + s_ * 8 + 8], vi(sm["b"][s_], slice(None), h, slice(None)),
                      negWT[:, h, s_ * 8:s_ * 8 + 8], start=False, stop=lastmm, inc=(h == 3))
        kb.copy("act", uTb[:], v3(buT))
        P.release(buT)
        bk = P.next()
        pv = bfv(bk)
        for h in range(4):
            kb.tr(V(pv.ap[:, h * 128:(h + 1) * 128], pv.res), uTb[:, h, :], idb[:], inc=(h == 3))
        kb.copy("dve", ub[:], V(pv.ap[:, 0:512].rearrange("p (k t) -> p k t", k=4), pv.res))
        boT = P.next(hold=True)
        for h in range(4):
            kb.mm(boT[:, h * 128:(h + 1) * 128], ub[:, h, :], qkdT[:, h, :], start=(h == 0), stop=False, inc=False)
        for s_ in range(NSEQ):
            sample_ld_f(hg, s_ + 3)
            fs_ = sm["f"][s_ % 4]
            for h in range(4):
                lastmm = (s_ == NSEQ - 1 and h == 3)
                kb.mm(boT[:, h * 128 + s_ * 8:h * 128 + s_ * 8 + 8], vi(sm["b"][s_], slice(None), h, slice(None)),
                      qT[:, 4 + h, s_ * 8:s_ * 8 + 8], start=False, stop=lastmm, inc=(h == 3))
            kdm = Pb[s_ % 2]
            kb.ts("dve", kdm[:], sv["kd"][:], C("seqmask", s_, s_ + 1), ALU.mult)
            bs = P.next()
            for h in range(4):
                kb.mm(bs[:, h * 128:(h + 1) * 128], kdm[:, h, :], ub[:, h, :], inc=(h == 3))
            for h in range(4):
                fh = vi(fs_, slice(None), h, slice(None))
                kb.stt(fh, fh, abc[:, s_ * 4 + h:s_ * 4 + h + 1], bs[:, h * 128:(h + 1) * 128], ALU.mult, ALU.add)
            kb.dma("act", ext(sas_d[s_, h0:h0 + 4].rearrange("h k v -> k h v")), fs_, sm["chfo"][s_ % 4])
        kb.copy("act", uTb[:], v3(boT))
        P.release(boT)
        bk = P.next()
        pv = bfv(bk)
        for h in range(4):
            kb.tr(V(pv.ap[:, h * 128:(h + 1) * 128], pv.res), uTb[:, h, :], idb[:], inc=(h == 3))
        return V(pv.ap[:, 0:512], pv.res)

    def sample_hist_setup():
        for piece in range(3):
            kb.dma("sp", yt[0:48, :], ext(sc_d[:, piece * 1024:(piece + 1) * 1024]), ch_yt)
            bk = P.next()
            for c in range(8):
                kb.tr(bk[:, c * 48:(c + 1) * 48], yt[0:48, c * 128:(c + 1) * 128], V(identf.ap[0:48, 0:48], identf.res),
                      inc=(c == 7))
            kb.copy("dve", histT[:, piece * 8:(piece + 1) * 8, :],
                    V(bk.t[:, 0:384].rearrange("p (c r) -> p c r", c=8), bk.res))

    dbg_tile = 1 if n_ptiles > 1 else 0
    try:
        load_masks("p")
        chk(1)
        if do_sample:
            sample_hist_setup()
        tiles = [(ti, False) for ti in range(n_ptiles)] + ([(NPT, True)] if do_sample else [])
        items = [(ti, hg, smp) for (ti, smp) in tiles for hg in range(2)]
        for _ in gdn_prologue(tiles[0][0], tiles[0][1]):
            pass
        masks_s = False
        def drive(gens):
            gens = [g_ for g_ in gens if g_ is not None]
            while gens:
                for g_ in list(gens):
                    try:
                        next(g_)
                    except StopIteration:
                        gens.remove(g_)
        carry = None
        for ii in range(len(items) + 1):
            gens = []
            if carry is not None:
                gens.append(carry)
                carry = None
            hg2 = None
            if ii >= 1:
                ti2, hg2, smp2 = items[ii - 1]
                if smp2 and not masks_s:
                    load_masks("s")
                    masks_s = True
                gens.append(gdn_stage2(ti2, hg2, ii - 1, smp2))
            if ii < len(items):
                ti, hg, smp_ = items[ii]
                gens.append(gdn_stage1(ti, hg, ii, smp_))
                if hg == 1:
                    nxt = [tt_ for tt_ in tiles if tt_[0] > ti]
                    if nxt:
                        gens.append(gdn_prologue(nxt[0][0], nxt[0][1]))
            drive(gens)
            if hg2 == 1:
                carry = gdn_epilogue(ti2, smp2)
        if carry is not None:
            drive([carry])
    except Cut:
        pass

    if not do_l1:
        kb.final_wait()
        es0.close()
        return nc

    kb.barrier()
    es0.close()
    es1 = ExitStack()
    ch_w1 = kb.chan("w1")
    wib = [kb.sb(f"wib{k}", [128, 6144], BF16, es1) for k in range(8)]
    for k in range(8):
        for (c0, c1) in ((0, 2048), (2048, 4096), (4096, 6144)):
            kb.dma("pool", wib[k][:, c0:c1], ext(wib_d[k * 128:(k + 1) * 128, c0:c1]), ch_w1)
    wob = [kb.sb(f"wob{k}", [128, 1024], BF16, es1) for k in range(16)]
    for k in range(16):
        st, chn = (xt, ch_xt) if k % 2 == 0 else (yt, ch_yt)
        kb.dma("sp", st[:], ext(wob_d[k * 128:(k + 1) * 128, :]), chn)
        kb.ts("dve", wob[k][:], st[:], C("onwbT", k, k + 1), ALU.mult)
    for k in range(8):
        wib[k].res.w = (ch_w1.sem, ch_w1.count, "dma")
    fnw = kb.sb("fnw", [128, D], F32, es1)
    ch_fnw = kb.chan("fnw")
    ch_dm = kb.chan("dm")
    ch_dms = kb.chan("dms")
    kb.dma("sp", fnw[:], ext(fnw_d[0:1, :].broadcast_to([128, D])), ch_fnw)
    dm = kb.sb("dm", [128, 4, 128], F32, es1)
    rot = kb.sb("rot", [128, 2, 128], F32, es1)
    ch_rot = kb.chan("rot")
    tmpA = kb.sb("tmpA", [128, 2, 128], F32, es1)
    tmpB = kb.sb("tmpB", [128, 2, 128], F32, es1)
    tmpC = kb.sb("tmpC", [128, 2, 128], F32, es1)
    tmpD = kb.sb("tmpD", [128, 2, 128], F32, es1)
    qkr = kb.sb("qkr", [128, 2, 2, 128], BF16, es1)
    qd = kb.sb("qd", [128, 256], BF16, es1)
    kdd = kb.sb("kdd", [128, 256], BF16, es1)
    qkT = kb.sb("qkT", [128, 6, 128], BF16, es1)
    vb1 = kb.sb("vb1", [128, 512], BF16, es1)
    gs1 = kb.sb("gs1", [128, 512], BF16, es1)
    qkd1 = kb.sb("qkd1", [128, 128], BF16, es1)
    S1 = kb.sbs("S1", [128, 4, 2, 512], F32, 1, es1)
    S1b = kb.sbs("S1b", [128, 4, 2, 512], BF16, 1, es1)
    on1 = kb.sb("on1", [128, 2048], BF16, es1)
    on1T = kb.sb("on1T", [128, 16, 128], BF16, es1)
    st1 = kb.sb("st1", [128, 8], F32, es1)
    zb = [kb.sb(f"zb{i}", [128, 2, 128], BF16, es1) for i in range(2)]
    kddm = [kb.sb(f"kddm{i}", [128, 256], BF16, es1) for i in range(2)]
    for z_ in zb:
        kb.memset("pool", z_[:], 0.0)
    ch_s1 = [kb.chan(f"s1_{i}") for i in range(4)]
    ch_s1o = [kb.chan(f"s1o_{i}") for i in range(4)]
    gam = [1.0 - 2.0 ** (-5.0 - h) for h in range(4)]

    kdd2 = [kdd, kb.sb("kdd_b", [128, 256], BF16, es1)]
    qkT2 = [qkT, kb.sb("qkT_b", [128, 6, 128], BF16, es1)]
    vb12 = [vb1, kb.sb("vb1_b", [128, 512], BF16, es1)]
    gs12 = [gs1, kb.sb("gs1_b", [128, 512], BF16, es1)]
    rot2 = [rot, kb.sb("rot_b", [128, 2, 128], F32, es1)]
    ch_rot2 = [ch_rot, kb.chan("rot_b")]
    junk1 = kb.sb("junk1", [128, 512], BF16, es1)

    def ret_prologue(ti, is_sample):
        src = V(x1_d[ti * 128:(ti + 1) * 128, :], x1_res[ti])
        yield from front_end(src, "nwT1", ti % 2)
        kb.dma("sp", rot2[ti % 2][:], ext(rot_d[ti]), ch_rot2[ti % 2])
        yield

    def ret_stage1(ti, h, ii, is_sample):
        xn = xnT[ti % 2]
        rt = rot2[ti % 2]
        kdd_, qkT_, vb_, gs_ = kdd2[ii % 2], qkT2[ii % 2], vb12[ii % 2], gs12[ii % 2]
        rd = "rdec_s" if is_sample else "rdec_p"
        bk = P.next()
        for part, c0 in ((0, h * 256), (1, 1024 + h * 256)):
            for k in range(8):
                kb.mm(bk[:, part * 256:(part + 1) * 256], xn[:, k, :], wib[k][:, c0:c0 + 256],
                      start=(k == 0), stop=(k == 7), inc=(k == 7 and part == 1))
        pq = V(bk.t[:].rearrange("p (a b d) -> p a b d", a=2, b=2), bk.res)
        x1v = V(pq.ap[:, :, 0, :], bk.res)
        x2v = V(pq.ap[:, :, 1, :], bk.res)
        cosb = bc(un(rt[:, 0, :], 1), [128, 2, 128])
        sinb = bc(un(rt[:, 1, :], 1), [128, 2, 128])
        kb.tt("dve", tmpA[:], x1v, cosb, ALU.mult)
        kb.tt("dve", tmpB[:], x2v, sinb, ALU.mult)
        kb.tt("dve", V(qkr.t[:, :, 0, :], qkr.res), tmpA[:], tmpB[:], ALU.subtract)
        yield
        kb.tt("dve", tmpC[:], x1v, sinb, ALU.mult)
        kb.tt("dve", tmpD[:], x2v, cosb, ALU.mult)
        kb.tt("dve", V(qkr.t[:, :, 1, :], qkr.res), tmpC[:], tmpD[:], ALU.add)
        qr = V(qkr.t[:, 0, :, :].rearrange("p b d -> p (b d)"), qkr.res)
        kr = V(qkr.t[:, 1, :, :].rearrange("p b d -> p (b d)"), qkr.res)
        kb.ts("dve", qd[:], qr, C(rd, h, h + 1), ALU.mult)
        kb.ts("dve", kdd_[:], kr, C(rd, 4 + h, 5 + h), ALU.mult)
        yield
        bk = P.next()
        for k in range(8):
            kb.mm(bk[:], xn[:, k, :], wib[k][:, 2048 + h * 512:2048 + (h + 1) * 512], start=(k == 0), stop=(k == 7))
        kb.copy("act", vb_[:], bk[:])
        yield
        bk = P.next()
        for k in range(8):
            kb.mm(bk[:], xn[:, k, :], wib[k][:, 4096 + h * 512:4096 + (h + 1) * 512], start=(k == 0), stop=(k == 7))
        kb.act(gs_[:], bk[:], AF.Silu)
        yield
        bk = P.next()
        pv = bfv(bk)
        srcs = [qkr[:, 0, 0, :], qkr[:, 0, 1, :], qkr[:, 1, 0, :], qkr[:, 1, 1, :], qd[:, 0:128], qd[:, 128:256]]
        for i_, sv_ in enumerate(srcs):
            kb.tr(V(pv.ap[:, i_ * 128:(i_ + 1) * 128], pv.res), sv_, idb[:], inc=(i_ == 5))
        kb.copy("act", qkT_[:], V(pv.ap[:, 0:768].rearrange("p (k t) -> p k t", k=6), pv.res))
        yield

    def ret_stage2(ti, h, ii, is_sample, n_tiles_first):
        kdd_, qkT_, vb_, gs_ = kdd2[ii % 2], qkT2[ii % 2], vb12[ii % 2], gs12[ii % 2]
        first = (ti == 0)
        dmk = dm
        bk = P.next()
        kb.mm(bk[:, 0:128], qkT_[:, 2, :], qkT_[:, 0, :], start=True, stop=False)
        kb.mm(bk[:, 0:128], qkT_[:, 3, :], qkT_[:, 1, :], start=False, stop=True)
        kb.tt("dve", qkd1[:], bk[:, 0:128], dmk[:, h, :], ALU.mult)
        yield
        bo = P.next(hold=True)
        if not is_sample:
            kb.mm(bo[:], qkd1[:], vb_[:], start=True, stop=first)
            if not first:
                kb.mm(bo[:], qkT_[:, 4, :], S1b[:, h, 0, :], start=False, stop=False)
                kb.mm(bo[:], qkT_[:, 5, :], S1b[:, h, 1, :], start=False, stop=True)
            for c in range(2):
                bs = P.next()
                kb.mm(bs[:], kdd_[:, c * 128:(c + 1) * 128], vb_[:])
                if first:
                    kb.copy("dve", S1[:, h, c, :], bs[:])
                else:
                    kb.stt(S1[:, h, c, :], S1[:, h, c, :], float(gam[h] ** 128), bs[:], ALU.mult, ALU.add)
            kb.copy("act", S1b[:, h, :, :], S1[:, h, :, :])
            yield
        else:
            kb.mm(bo[:], qkd1[:], vb_[:], start=True, stop=False, inc=False)

            def ld1(pi):
                if pi >= 4 * NSEQ:
                    return
                hh, ss = divmod(pi, NSEQ)
                kb.dma("sp", S1[:, pi % 4, :, :], ext(sr_d[ss, hh].rearrange("(c p) v -> p c v", p=128)),
                       ch_s1[pi % 4])
            def cast1(pi_):
                if pi_ < 4 * NSEQ:
                    kb.copy("act", S1b[:, pi_ % 4, :, :], S1[:, pi_ % 4, :, :])
            if h == 0:
                ld1(0)
                ld1(1)
                cast1(0)
            for s_ in range(NSEQ):
                pi = h * NSEQ + s_
                sl = pi % 4
                ld1(pi + 2)
                z_ = zb[s_ % 2]
                kb.copy("dve", z_[:, :, s_ * 8:s_ * 8 + 8], qkT_[:, 4:6, s_ * 8:s_ * 8 + 8])
                kb.mm(bo[:], z_[:, 0, :], S1b[:, sl, 0, :], start=False, stop=False, inc=False)
                kb.mm(bo[:], z_[:, 1, :], S1b[:, sl, 1, :], start=False, stop=(s_ == NSEQ - 1), inc=True)
                kb.memset("dve", z_[:, :, s_ * 8:s_ * 8 + 8], 0.0)
                km = kddm[s_ % 2]
                kb.ts("dve", km[:], kdd_[:], C("seqmask", s_, s_ + 1), ALU.mult)
                bss = []
                for c in range(2):
                    bs = P.next()
                    kb.mm(bs[:], km[:, c * 128:(c + 1) * 128], vb_[:])
                    bss.append(bs)
                cast1(pi + 1)
                for c in range(2):
                    kb.stt(S1[:, sl, c, :], S1[:, sl, c, :], float(gam[h] ** 8), bss[c][:], ALU.mult, ALU.add)
                kb.dma("act", ext(sbs_d[s_, h].rearrange("(c p) v -> p c v", p=128)), S1[:, sl, :, :], ch_s1o[sl])
                yield
        kb.act(junk1[:], bo[:], AF.Square, accum=st1[:, 0:1])
        kb.ts("dve", st1[:, 1:2], st1[:, 0:1], 1.0 / 512, ALU.mult, EPS, ALU.add)
        kb.tt("pool", st1[:, 2:3], st1[:, 1:2], neghalf[:, 0:1], ALU.pow)
        yield
        kb.stt(on1[:, h * 512:(h + 1) * 512], bo[:], st1[:, 2:3], gs_[:], ALU.mult, ALU.mult)
        P.release(bo)
        yield

    def ret_epilogue(ti, is_sample, last_prompt):
        src = V(x1_d[ti * 128:(ti + 1) * 128, :], x1_res[ti])
        kb.dma("sp", yt[:], src, ch_yt)
        for g_ in range(2):
            bk = P.next()
            pv = bfv(bk)
            for k in range(8):
                kk_ = g_ * 8 + k
                kb.tr(V(pv.ap[:, k * 128:(k + 1) * 128], pv.res), on1[:, kk_ * 128:(kk_ + 1) * 128], idb[:], inc=(k == 7))
            kb.copy("act" if g_ else "dve", on1T[:, g_ * 8:(g_ + 1) * 8, :], V(pv.ap.rearrange("p (k t) -> p k t", k=8), pv.res))
            yield
        for n in range(2):
            bk = P.next()
            for k in range(16):
                kb.mm(bk[:], on1T[:, k, :], wob[k][:, n * 512:(n + 1) * 512], start=(k == 0), stop=(k == 15))
            kb.tt("dve", yt[:, n * 512:(n + 1) * 512], bk[:], yt[:, n * 512:(n + 1) * 512], ALU.add)
            yield
        kb.act(V(on1T.t[:].rearrange("p k t -> p (k t)")[:, 0:1024], on1T.res), yt[:], AF.Square, accum=st1[:, 4:5])
        kb.ts("dve", st1[:, 5:6], st1[:, 4:5], 1.0 / D, ALU.mult, EPS, ALU.add)
        kb.tt("pool", st1[:, 6:7], st1[:, 5:6], neghalf[:, 0:1], ALU.pow)
        yield
        kb.stt(yt[:], yt[:], st1[:, 6:7], fnw[:], ALU.mult, ALU.mult)
        dst = ext(ys_d[:, :]) if is_sample else ext(yp_d[ti * 128:(ti + 1) * 128, :])
        kb.dma("sp", dst, yt[:], ch_yt)
        yield

    try:
        kb.dma("sp", dm[:], ext(dmask_d[0]), ch_dm)
        tiles = [(ti, False) for ti in range(n_ptiles)] + ([(NPT, True)] if do_sample else [])
        items = [(ti, h, smp) for (ti, smp) in tiles for h in range(4)]
        for _ in ret_prologue(tiles[0][0], tiles[0][1]):
            pass
        carry = None
        dm_s_loaded = False
        for ii in range(len(items) + 1):
            gens = []
            if carry is not None:
                gens.append(carry)
                carry = None
            h2 = None
            if ii >= 1:
                ti2, h2, smp2 = items[ii - 1]
                if smp2 and not dm_s_loaded:
                    kb.dma("sp", dm[:], ext(dmask_d[1]), ch_dms)
                    dm_s_loaded = True
                gens.append(ret_stage2(ti2, h2, ii - 1, smp2, None))
            if ii < len(items):
                ti, h, smp_ = items[ii]
                gens.append(ret_stage1(ti, h, ii, smp_))
                if h == 1:
                    nxt = [tt_ for tt_ in tiles if tt_[0] > ti]
                    if nxt:
                        gens.append(ret_prologue(nxt[0][0], nxt[0][1]))
            drive(gens)
            if h2 == 3:
                if (not smp2) and ti2 == n_ptiles - 1:
                    kb.dma("sp", ext(sbp_d.rearrange("h (c p) v -> p h c v", p=128)), S1[:], ch_out)
                carry = ret_epilogue(ti2, smp2, False)
        if carry is not None:
            drive([carry])
    except Cut:
        pass
    kb.final_wait()
    es1.close()
    return nc


def _rot_tables():
    half = 128
    inv = 1.0 / (10000.0 ** np.linspace(0.0, 1.0, half))
    gam = 1.0 - 2.0 ** (-5.0 - np.arange(4))
    rot = np.zeros((NPT + 1, 128, 2, 128), np.float32)
    for ti in range(NPT + 1):
        if ti < NPT:
            pos = ti * 128 + np.arange(128, dtype=np.float64)
        else:
            pos = 16384.0 + (np.arange(128) % 8).astype(np.float64)
        ang = pos[:, None] * inv[None, :]
        rot[ti, :, 0, :] = np.cos(ang)
        rot[ti, :, 1, :] = np.sin(ang)
    return rot


def make_in_maps(I):
    f = np.float32
    cst = _build_cst(I["norm_w"], I["conv_w_a"], I["a_log_a"], I["dt_bias_a"], I["onorm_a"], I["onorm_b"])
    idb = np.eye(128, dtype=np.float32).astype(ml_dtypes.bfloat16)
    rot = _rot_tables()
    gam = 1.0 - 2.0 ** (-5.0 - np.arange(4, dtype=np.float64))
    dmask = np.zeros((2, 128, 4, 128), np.float32)
    ii = np.arange(128)
    for bi, blk in enumerate((128, 8)):
        same = (ii[:, None] // blk) == (ii[None, :] // blk)
        dif = ii[None, :] - ii[:, None]
        ok = (dif >= 0) & same
        for h in range(4):
            dmask[bi, :, h, :] = np.where(ok, gam[h] ** np.maximum(dif, 0) * 256.0 ** -0.5, 0.0)
    common = dict(
        wia=np.ascontiguousarray(I["w_in_a"][0], f), woa=np.ascontiguousarray(I["w_out_a"][0], f),
        wib=np.ascontiguousarray(I["w_in_b"][0], f), wob=np.ascontiguousarray(I["w_out_b"][0], f),
        cst=cst, idb=idb, fnw=np.ascontiguousarray(I["final_norm_w"].reshape(1, D), f), rot=rot, dmask=dmask)
    maps = []
    for c in range(NCORES):
        m = dict(common)
        m["xp"] = np.ascontiguousarray(I["x_prompt"][c], f)
        m["xs"] = np.ascontiguousarray(I["x_sample"][c * NSEQ:(c + 1) * NSEQ].reshape(128, D), f)
        m["sg"] = np.ascontiguousarray(I["state_gdn_ssm"][0, c * NSEQ:(c + 1) * NSEQ], f)
        m["sc"] = np.ascontiguousarray(I["state_gdn_conv"][0, c * NSEQ:(c + 1) * NSEQ].reshape(48, 3072), f)
        m["sr"] = np.ascontiguousarray(I["state_ret"][0, c * NSEQ:(c + 1) * NSEQ], f)
        maps.append(m)
    return maps


_NC_CACHE = {}


def kernel(**inputs):
    I = {k: np.asarray(v) for k, v in inputs.items()}
    if "nc" not in _NC_CACHE:
        _NC_CACHE["nc"] = build_program()
    nc = _NC_CACHE["nc"]
    in_maps = make_in_maps(I)
    res = run_bass_kernel_spmd(nc, in_maps, core_ids=list(range(NCORES))).results
    f = np.float32
    yp = np.stack([res[c]["yp"] for c in range(NCORES)]).astype(f)
    ys = np.concatenate([res[c]["ys"].reshape(NSEQ, 8, D) for c in range(NCORES)], 0).astype(f)
    sap = np.stack([res[c]["sap"] for c in range(NCORES)])[None].astype(f)
    cap = np.stack([res[c]["cap"] for c in range(NCORES)])[None].astype(f)
    sbp = np.stack([res[c]["sbp"] for c in range(NCORES)])[None].astype(f)
    sas = np.concatenate([res[c]["sas"] for c in range(NCORES)], 0)[None].astype(f)
    cas = np.concatenate([res[c]["cas"].reshape(NSEQ, 3, 3072) for c in range(NCORES)], 0)[None].astype(f)
    sbs = np.concatenate([res[c]["sbs"] for c in range(NCORES)], 0)[None].astype(f)
    return (yp, ys, sap, cap, sbp, sas, cas, sbs)
```

```python
import numpy as np
import ml_dtypes
from contextlib import ExitStack
import concourse.bass as bass
import concourse.mybir as mybir
from concourse.bass_utils import run_bass_kernel_spmd

F32 = mybir.dt.float32
F32R = mybir.dt.float32r
BF16 = mybir.dt.bfloat16
AF = mybir.ActivationFunctionType
ALU = mybir.AluOpType
AX = mybir.AxisListType

NCORES = 8
D = 1024
LP = 2048
NPT = LP // 128
NSEQ = 16
EPS = 1e-6
NEG = -32768.0


class Res:
    __slots__ = ("name", "w", "r", "excl", "strict")

    def __init__(self, name):
        self.name = name
        self.excl = False
        self.strict = False
        self.w = None
        self.r = {}


class V:
    __slots__ = ("ap", "res")

    def __init__(self, ap, res):
        self.ap = ap
        self.res = res


class TT:
    def __init__(self, t, name, nres=1):
        self.t = t
        self.res = Res(name)

    def __getitem__(self, idx):
        return V(self.t[idx], self.res)

    def v(self, ap):
        return V(ap, self.res)


class STT(TT):
    def __init__(self, t, name, n, slot_size):
        self.t = t
        self.n = n
        self.ss = slot_size
        self.slots = [Res(f"{name}_{i}") for i in range(n // slot_size)]
        self.res = tuple(self.slots)

    def __getitem__(self, idx):
        key = idx[1] if isinstance(idx, tuple) and len(idx) > 1 else slice(None)
        if isinstance(key, int):
            lo = hi = key
        else:
            lo = key.start or 0
            hi = (key.stop if key.stop is not None else self.n) - 1
        rs = tuple(self.slots[lo // self.ss:hi // self.ss + 1])
        return V(self.t[idx], rs if len(rs) > 1 else rs[0])


def _flat(xs):
    out = []
    for x in xs:
        r = x.res if isinstance(x, (V, TT)) else x
        if isinstance(r, tuple):
            out.extend(r)
        else:
            out.append(r)
    return out


class Chan:
    def __init__(self, sem):
        self.sem = sem
        self.count = 0


class EngQ:
    def __init__(self, name, eng, sem):
        self.name = name
        self.eng = eng
        self.sem = sem
        self.count = 0
        self.seen = {}


class KB:
    def __init__(self, nc, es):
        self.nc = nc
        self.es = es
        self.q = {}
        for name, eng in (("pe", nc.tensor), ("act", nc.scalar), ("dve", nc.vector),
                          ("pool", nc.gpsimd), ("sp", nc.sync)):
            sem = es.enter_context(nc.semaphore("sem_" + name))
            self.q[name] = EngQ(name, eng, sem)
        self.chans = []
        self.n_instr = 0

    def sbs(self, name, shape, dt, slot_size, es=None):
        t = (es or self.es).enter_context(self.nc.sbuf_tensor("s_" + name, list(shape), dt))
        return STT(t, name, shape[1], slot_size)

    def sb(self, name, shape, dt, es=None):
        t = (es or self.es).enter_context(self.nc.sbuf_tensor("s_" + name, list(shape), dt))
        return TT(t, name)

    def chan(self, name):
        sem = self.es.enter_context(self.nc.semaphore("ch_" + name))
        c = Chan(sem)
        self.chans.append(c)
        return c

    def _need(self, q, ev):
        sem, val, owner = ev
        if q.seen.get(sem.num, 0) >= val:
            return
        q.eng.wait_ge(sem, val)
        q.seen[sem.num] = val

    def _deps(self, q, reads, writes):
        me = q.name
        for r in reads:
            if r is None:
                continue
            if r.w is not None:
                self._need(q, r.w)
            if r.excl:
                for ev in r.r.values():
                    if ev[2] != me:
                        self._need(q, ev)
        for w in writes:
            if w is None:
                continue
            if w.w is not None:
                if w.w[2] != me or me == "pool" or w.strict:
                    self._need(q, w.w)
            for ev in w.r.values():
                if ev[2] != me or me == "pool":
                    self._need(q, ev)

    def _record(self, ev, reads, writes):
        for r in reads:
            if r is not None:
                r.r[ev[0].num] = ev
        for w in writes:
            if w is not None:
                w.w = ev
                w.r = {}

    def op(self, eng, fn, reads, writes, inc=True):
        q = self.q[eng]
        reads = _flat(reads)
        writes = _flat(writes)
        self._deps(q, reads, writes)
        ins = fn(q.eng)
        self.n_instr += 1
        if inc:
            ins.then_inc(q.sem, 1)
            q.count += 1
            ev = (q.sem, q.count, q.name)
        else:
            ev = (q.sem, q.count + 1, q.name)
        self._record(ev, reads, writes)
        return ins

    def dma(self, qname, out, in_, chan, **kw):
        q = self.q[qname]
        reads = _flat([in_])
        writes = _flat([out])
        self._deps(q, reads, writes)
        ins = q.eng.dma_start(out=out.ap, in_=in_.ap, **kw)
        ins.then_inc(chan.sem, 16)
        chan.count += 16
        ev = (chan.sem, chan.count, "dma")
        self._record(ev, reads, writes)
        self.n_instr += 1
        return ev

    def barrier(self):
        evs = [(q.sem, q.count, q.name) for q in self.q.values() if q.count > 0]
        evs += [(c.sem, c.count, "dma") for c in self.chans if c.count > 0]
        for q in self.q.values():
            for ev in evs:
                if ev[2] == q.name:
                    continue
                self._need(q, ev)

    def final_wait(self):
        q = self.q["sp"]
        for c in self.chans:
            if c.count > 0:
                self._need(q, (c.sem, c.count, "dma"))
        for qq in self.q.values():
            if qq.name != "sp" and qq.count > 0:
                self._need(q, (qq.sem, qq.count, qq.name))

    def mm(self, out, lhsT, rhs, start=True, stop=True, inc=None):
        if inc is None:
            inc = stop
        return self.op("pe", lambda e: e.matmul(out.ap, lhsT.ap, rhs.ap, start=start, stop=stop,
                                                skip_group_check=True),
                       [lhsT, rhs], [out], inc=inc)

    def tr(self, out, in_, ident, inc=True):
        return self.op("pe", lambda e: e.transpose(out.ap, in_.ap, ident.ap), [in_, ident], [out], inc=inc)

    def act(self, out, in_, func, bias=None, scale=None, accum=None, extra_reads=()):
        kw = {}
        reads = [in_] + list(extra_reads)
        writes = [out]
        if bias is not None:
            if isinstance(bias, V):
                kw["bias"] = bias.ap
                reads.append(bias)
            else:
                kw["bias"] = bias
        if scale is not None:
            if isinstance(scale, V):
                kw["scale"] = scale.ap
                reads.append(scale)
            else:
                kw["scale"] = scale
        if accum is not None:
            kw["accum_out"] = accum.ap
            writes.append(accum)
        return self.op("act", lambda e: e.activation(out.ap, in_.ap, func, **kw), reads, writes)

    def tt(self, eng, out, in0, in1, op):
        return self.op(eng, lambda e: e.tensor_tensor(out.ap, in0.ap, in1.ap, op), [in0, in1], [out])

    def ts(self, eng, out, in0, s1, op0, s2=None, op1=None):
        reads = [in0]
        a1 = s1
        a2 = s2
        if isinstance(s1, V):
            reads.append(s1)
            a1 = s1.ap
        if isinstance(s2, V):
            reads.append(s2)
            a2 = s2.ap
        if op1 is None:
            return self.op(eng, lambda e: e.tensor_scalar(out.ap, in0.ap, a1, None, op0), reads, [out])
        return self.op(eng, lambda e: e.tensor_scalar(out.ap, in0.ap, a1, a2, op0, op1), reads, [out])

    def stt(self, out, in0, scalar, in1, op0, op1):
        reads = [in0, in1]
        a = scalar
        if isinstance(scalar, V):
            reads.append(scalar)
            a = scalar.ap
        return self.op("dve", lambda e: e.scalar_tensor_tensor(out.ap, in0.ap, a, in1.ap, op0, op1),
                       reads, [out])

    def copy(self, eng, out, in_):
        if eng == "act":
            return self.op("act", lambda e: e.copy(out.ap, in_.ap), [in_], [out])
        return self.op(eng, lambda e: e.tensor_copy(out.ap, in_.ap), [in_], [out])

    def memset(self, eng, out, val):
        return self.op(eng, lambda e: e.memset(out.ap, val), [], [out])

    def reduce_sum(self, out, in_):
        return self.op("dve", lambda e: e.tensor_reduce(out.ap, in_.ap, AX.X, ALU.add), [in_], [out])


def bc(v, shape):
    return V(v.ap.broadcast_to(list(shape)), v.res)


def un(v, axis):
    return V(v.ap.unsqueeze(axis), v.res)


def _mask_set(blk):
    i = np.arange(128)
    same = (i[:, None] // blk) == (i[None, :] // blk)
    triU = ((i[:, None] <= i[None, :]) & same).astype(np.float32)
    blkm = same.astype(np.float32)
    triSU = ((i[:, None] > i[None, :]) & same).astype(np.float32)
    strict = ((i[:, None] > i[None, :]) & same).astype(np.float32)
    incl = (i[:, None] >= i[None, :]) & same
    maskneg = np.where(incl, 0.0, NEG).astype(np.float32)
    masknegT = np.ascontiguousarray(maskneg.T)
    return dict(triU=triU, blk=blkm, triSU=triSU, strict=strict, maskneg=maskneg, masknegT=masknegT)


def _cst_layout():
    off = {}
    o = 0

    def add(name, n):
        nonlocal o
        off[name] = (o, n)
        o += n
    add("identf", 128)
    for s in ("p", "s"):
        for nm in ("triU", "blk", "triSU", "strict", "maskneg", "masknegT"):
            add(nm + "_" + s, 128)
    add("nwT0", 8)
    add("nwT1", 8)
    add("cwT", 96)
    add("dtb", 8)
    add("alog", 8)
    add("onwa", 1)
    add("onwbT", 16)
    add("seqmask", 16)
    add("rdec_p", 8)
    add("rdec_s", 8)
    return off, o


CST_OFF, CST_N = _cst_layout()


def _build_cst(norm_w, conv_w_a, a_log_a, dt_bias_a, onorm_a, onorm_b):
    c = np.zeros((128, CST_N), np.float32)

    def put(name, arr):
        o, n = CST_OFF[name]
        c[:, o:o + n] = arr
    put("identf", np.eye(128, dtype=np.float32))
    for s, blk in (("p", 128), ("s", 8)):
        m = _mask_set(blk)
        for nm in ("triU", "blk", "triSU", "strict", "maskneg", "masknegT"):
            put(nm + "_" + s, m[nm])
    put("nwT0", norm_w[0].reshape(8, 128).T)
    put("nwT1", norm_w[1].reshape(8, 128).T)
    cw = conv_w_a[0].reshape(4, 24, 128)
    put("cwT", np.transpose(cw, (2, 1, 0)).reshape(128, 96))
    put("dtb", np.broadcast_to(dt_bias_a[0][None, :], (128, 8)))
    put("alog", np.broadcast_to(a_log_a[0][None, :], (128, 8)))
    put("onwa", onorm_a[0].reshape(128, 1))
    put("onwbT", onorm_b[0].reshape(16, 128).T)
    sm = np.zeros((128, 16), np.float32)
    sm[np.arange(128), np.arange(128) // 8] = 1.0
    put("seqmask", sm)
    gam = 1.0 - 2.0 ** (-5.0 - np.arange(4, dtype=np.float64))
    for nm, blk in (("rdec_p", 128), ("rdec_s", 8)):
        t = (np.arange(128) % blk).astype(np.float64)
        qd = gam[None, :] ** (t[:, None] + 1.0)
        kd = gam[None, :] ** (blk - 1.0 - t[:, None]) * 256.0 ** -0.5
        put(nm, np.concatenate([qd, kd], 1))
    return c


def ext(ap):
    return V(ap, None)


class PsumPool:
    def __init__(self, kb, n=8):
        self.banks = [TT(kb.es.enter_context(kb.nc.psum_tensor(f"psb{i}", [128, 512], F32)), f"psb{i}")
                      for i in range(n)]
        for b in self.banks:
            b.res.excl = True
        self.i = 0

        self.held = set()

    def next(self, hold=False):
        while True:
            b = self.banks[self.i % len(self.banks)]
            self.i += 1
            if id(b) not in self.held:
                break
        if hold:
            self.held.add(id(b))
        return b

    def release(self, b):
        self.held.discard(id(b))


def bfv(bank):
    return V(bank.t[:].bitcast(BF16), bank.res)


class Ring:
    def __init__(self, kb, name, n, shape, dt, es=None, chan=True):
        self.slots = [kb.sb(f"{name}{i}", shape, dt, es) for i in range(n)]
        self.chans = [kb.chan(f"{name}{i}") for i in range(n)] if chan else None
        self.i = 0

    def next(self):
        j = self.i % len(self.slots)
        self.i += 1
        return self.slots[j], (self.chans[j] if self.chans else None)


class Cut(Exception):
    pass


def build_program(n_ptiles=NPT, do_sample=True, do_l1=True, dbg=False, cut=None):
    def chk(n):
        if cut is not None and cut == n:
            raise Cut()

    nc = bass.Bass("TRN2", target_bir_lowering=False)
    es = ExitStack()
    kb = KB(nc, es)

    def din(name, shape, dt=F32):
        return nc.dram_tensor(name, list(shape), dt, kind="ExternalInput").ap()

    def dout(name, shape, dt=F32):
        return nc.dram_tensor(name, list(shape), dt, kind="ExternalOutput").ap()

    xp_d = din("xp", [LP, D])
    xs_d = din("xs", [128, D])
    sg_d = din("sg", [NSEQ, 8, 128, 128])
    sc_d = din("sc", [48, 3072])
    sr_d = din("sr", [NSEQ, 4, 256, 512])
    wia_d = din("wia", [D, 4112])
    woa_d = din("woa", [D, D])
    wib_d = din("wib", [D, 6144])
    wob_d = din("wob", [2048, D])
    cst_d = din("cst", [128, CST_N])
    idb_d = din("idb", [128, 128], BF16)
    fnw_d = din("fnw", [1, D])
    rot_d = din("rot", [NPT + 1, 128, 2, 128])
    dmask_d = din("dmask", [2, 128, 4, 128])

    yp_d = dout("yp", [LP, D])
    ys_d = dout("ys", [128, D])
    sap_d = dout("sap", [8, 128, 128])
    cap_d = dout("cap", [3, 3072])
    sbp_d = dout("sbp", [4, 256, 512])
    sas_d = dout("sas", [NSEQ, 8, 128, 128])
    cas_d = dout("cas", [48, 3072])
    sbs_d = dout("sbs", [NSEQ, 4, 256, 512])
    x1_d = nc.dram_tensor("x1s", [LP + 128, D], F32, kind="Internal").ap()
    x1_res = [Res(f"x1_{i}") for i in range(NPT + 1)]
    wibb_d = nc.dram_tensor("wibb", [D, 6144], BF16, kind="Internal").ap()
    wibb_res = [[Res(f"wibb{k}_{p}") for p in range(3)] for k in range(8)]
    dbg_d = {}
    if dbg:
        for nm, shp in (("d_y1", [128 * (n_ptiles + 1), D]), ("d_mixed", [128, 1536]), ("d_misc", [128, 64]),
                        ("d_dec", [128, 512]), ("d_Y", [128, 512]), ("d_on", [128, 1024])):
            dbg_d[nm] = dout(nm, shp)

    P = PsumPool(kb)
    ch_c = kb.chan("const")
    ch_dbg = kb.chan("dbg")

    cst = kb.sb("cst", [128, CST_N], F32)
    idb = kb.sb("idb_s", [128, 128], BF16)
    kb.dma("sp", cst[:], ext(cst_d), ch_c)
    kb.dma("sp", idb[:], ext(idb_d), ch_c)
    cst.res.w = (ch_c.sem, ch_c.count, "dma")
    idb.res.w = (ch_c.sem, ch_c.count, "dma")

    def C(name, lo=0, hi=None):
        o, n = CST_OFF[name]
        hi = n if hi is None else hi
        return cst[:, o + lo:o + hi]

    neghalf = kb.sb("neghalf", [128, 16], F32)
    kb.memset("dve", neghalf[:], -0.5)
    identf = C("identf")

    def load_masks(s):
        kb.copy("dve", mr["triU"][:], C("triU_" + s))
        kb.ts("dve", mr["negtriU"][:], C("triU_" + s), -1.0, ALU.mult)
        kb.ts("dve", negtriUf[:], C("triU_" + s), -1.0, ALU.mult)
        kb.copy("dve", mr["ones"][:], onesf[:])
        kb.ts("dve", mr["negones"][:], onesf[:], -1.0, ALU.mult)
        kb.copy("dve", mr["ident"][:], identf)
        for nm in ("maskneg", "masknegT"):
            src = C(nm + "_" + s)
            kb.copy("dve", V(mr[nm].t[:].rearrange("p (h j) -> p h j", h=4), mr[nm].res),
                    bc(un(src, 1), [128, 4, 128]))

    xt = kb.sb("xt", [128, D], F32)
    yt = kb.sb("yt", [128, D], F32)
    ch_xt = kb.chan("xt")
    ch_yt = kb.chan("yt")
    xs_b = kb.sb("xs_b", [128, D], BF16)
    xs_b.res.strict = True
    xnT = [kb.sb(f"xnT{i}", [128, 8, 128], BF16) for i in range(2)]
    st4 = kb.sb("st4", [128, 16], F32)
    es0 = ExitStack()
    mr = {}
    for nm in ("triU", "negtriU", "ones", "negones", "ident", "maskneg", "masknegT"):
        w = 512 if nm.startswith("maskneg") else 128
        mr[nm] = kb.sb("mr_" + nm, [128, w], F32R, es0)
    onesf = kb.sb("onesf", [128, 128], F32, es0)
    kb.memset("dve", onesf[:], 1.0)
    negtriUf = kb.sb("negtriUf", [128, 128], F32, es0)


    ch_w0 = kb.chan("w0")
    wia = [kb.sb(f"wia{k}", [128, 4112], BF16, es0) for k in range(8)]
    pieces = ((0, 1536), (1536, 3072), (3072, 4112))
    for k in range(8):
        for (c0, c1) in pieces:
            kb.dma("pool", wia[k][:, c0:c1], ext(wia_d[k * 128:(k + 1) * 128, c0:c1]), ch_w0)
    woa = [kb.sb(f"woa{k}", [128, 1024], BF16, es0) for k in range(8)]
    for k in range(8):
        st, chn = (xt, ch_xt) if k % 2 == 0 else (yt, ch_yt)
        kb.dma("sp", st[:], ext(woa_d[k * 128:(k + 1) * 128, :]), chn)
        kb.ts("dve", woa[k][:], st[:], C("onwa"), ALU.mult)
    for k in range(8):
        wia[k].res.w = (ch_w0.sem, ch_w0.count, "dma")

    diag = kb.sb("diag", [128, 96, 128], BF16, es0)
    for i in range(96):
        kb.ts("dve", diag[:, i, :], idb[:], C("cwT", i, i + 1), ALU.mult)
    negA = kb.sb("negA", [128, 8], F32, es0)
    kb.act(negA[:], C("alog"), AF.Exp)
    kb.ts("dve", negA[:], negA[:], -1.0, ALU.mult)

    def front_end(src_v, nwname, par):
        kb.dma("sp", xt[:], src_v, ch_xt)
        kb.act(xs_b[:], xt[:], AF.Square, accum=st4[:, 0:1])
        yield
        kb.ts("dve", st4[:, 1:2], st4[:, 0:1], 1.0 / D, ALU.mult, EPS, ALU.add)
        kb.tt("pool", st4[:, 2:3], st4[:, 1:2], neghalf[:, 0:1], ALU.pow)
        yield
        kb.ts("dve", xs_b[:], xt[:], st4[:, 2:3], ALU.mult)
        yield
        bank = P.next()
        pv = bfv(bank)
        for k in range(8):
            kb.tr(V(pv.ap[:, k * 128:(k + 1) * 128], pv.res), xs_b[:, k * 128:(k + 1) * 128], idb[:], inc=(k == 7))
        kb.tt("dve", xnT[par][:], V(pv.ap.rearrange("p (k t) -> p k t", k=8), pv.res),
              bc(un(C(nwname), 2), [128, 8, 128]), ALU.mult)
        yield

    pT = [kb.sb(f"pT{i}", [128, 12, 176], BF16, es0) for i in range(2)]
    hist = [kb.sb(f"hist{i}", [128, 12, 3], BF16, es0) for i in range(2)]
    for h_ in hist:
        kb.memset("pool", h_[:], 0.0)
    mixed = [kb.sb(f"mixed{i}", [128, 1536], BF16, es0) for i in range(2)]
    zs = [kb.sb(f"zs{i}", [128, 512], BF16, es0) for i in range(2)]
    ba = kb.sb("ba", [128, 16], F32, es0)
    sc8 = kb.sb("sc8", [128, 64], F32, es0)
    E = kb.sb("E", [128, 24], F32, es0)
    sqb = kb.sb("sqb", [128, 1024], BF16, es0)
    r8 = kb.sb("r8", [128, 16], F32, es0)
    sv = {nm: kb.sb("sv_" + nm, [128, 4, 128], BF16, es0) for nm in ("qn", "qe", "kn", "kw", "kd", "vb")}
    qT = kb.sb("qT", [128, 8, 128], BF16, es0)
    knT = kb.sb("knT", [128, 4, 128], BF16, es0)
    gm = kb.sb("gm", [128, 4, 128], F32R, es0)
    gb = kb.sb("gb", [128, 4, 128], F32R, es0)
    dec = kb.sb("dec", [128, 4, 128], F32, es0)
    decT = kb.sb("decT", [128, 4, 128], BF16, es0)
    Pb = [kb.sb(f"Pb{i}", [128, 4, 128], BF16, es0) for i in range(2)]
    PTb = [kb.sb(f"PTb{i}", [128, 4, 128], BF16, es0) for i in range(2)]
    Yb = [kb.sb(f"Yb{i}", [128, 4, 128], BF16, es0) for i in range(2)]
    Mb = kb.sb("Mb", [128, 4, 128], BF16, es0)
    MTb = kb.sb("MTb", [128, 4, 128], BF16, es0)
    negidb = kb.sb("negidb", [128, 128], BF16, es0)
    kb.ts("dve", negidb[:], idb[:], -1.0, ALU.mult)
    negWT = kb.sb("negWT", [128, 4, 128], BF16, es0)
    qkdT = kb.sb("qkdT", [128, 4, 128], BF16, es0)
    S = kb.sbs("S", [128, 8, 128], F32, 4, es0)
    Sbf = kb.sbs("Sbf", [128, 8, 128], BF16, 4, es0)
    ub = kb.sb("ub", [128, 4, 128], BF16, es0)
    otmp = kb.sb("otmp", [128, 512], BF16, es0)
    on = kb.sb("on", [128, 1024], BF16, es0)
    onT = kb.sb("onT", [128, 8, 128], BF16, es0)
    ch_out = kb.chan("out_small")
    ch_sbf = [kb.chan("sbf0"), kb.chan("sbf1")]
    ch_sf = [kb.chan("sf0"), kb.chan("sf1")]
    ch_sfo = [kb.chan("sfo0"), kb.chan("sfo1")]
    if do_sample:
        cv = kb.sb("cv", [128, 12, 128], BF16, es0)
        histT = kb.sb("histT", [128, 24, 48], BF16, es0)
        abc = kb.sb("abc", [128, 64], F32, es0)
        gmsk = kb.sb("gmsk", [128, 16, 4], F32, es0)
        uTb = kb.sb("uTb", [128, 4, 128], BF16, es0)

    beta = sc8[:, 0:8]
    negbeta = sc8[:, 8:16]
    gv = sc8[:, 24:32]

    def v3(tt_, h=4):
        return V(tt_.t[:].rearrange("p (h d) -> p h d", h=h), tt_.res)

    def sc_b(vw):
        return bc(un(vw, 2), [128, 4, 128])

    sc8_2 = [sc8, kb.sb("sc8_b", [128, 64], F32, es0)]
    E_2 = [E, kb.sb("E_b", [128, 24], F32, es0)]
    sv_2 = [sv, {nm: kb.sb("svb_" + nm, [128, 4, 128], BF16, es0) for nm in ("qn", "qe", "kn", "kw", "kd", "vb")}]
    qT_2 = [qT, kb.sb("qT_b", [128, 8, 128], BF16, es0)]
    knT_2 = [knT, kb.sb("knT_b", [128, 4, 128], BF16, es0)]
    osq = kb.sb("osq", [128, 512], BF16, es0)

    def gdn_prologue(ti, is_sample):
        sc8 = sc8_2[ti % 2]
        E = E_2[ti % 2]
        beta = sc8[:, 0:8]
        negbeta = sc8[:, 8:16]
        gv = sc8[:, 24:32]
        par = ti % 2
        src = ext(xs_d[:, :]) if is_sample else ext(xp_d[ti * 128:(ti + 1) * 128, :])
        yield from front_end(src, "nwT0", par)
        xn = xnT[par]
        chk(2)
        bk = P.next()
        for k in range(8):
            kb.mm(bk[:, 0:16], xn[:, k, :], wia[k][:, 4096:4112], start=(k == 0), stop=(k == 7))
        kb.copy("dve", ba[:], bk[:, 0:16])
        yield
        kb.act(sc8[:, 56:64], ba[:, 0:8], AF.Tanh, scale=0.5)
        kb.ts("dve", negbeta, sc8[:, 56:64], -0.5, ALU.mult, -0.5, ALU.add)
        kb.ts("dve", beta, sc8[:, 56:64], 0.5, ALU.mult, 0.5, ALU.add)
        yield
        kb.tt("dve", sc8[:, 16:24], ba[:, 8:16], C("dtb"), ALU.add)
        kb.act(sc8[:, 16:24], sc8[:, 16:24], AF.Exp)
        kb.act(sc8[:, 16:24], sc8[:, 16:24], AF.Ln, bias=1.0)
        yield
        kb.tt("dve", gv, sc8[:, 16:24], negA[:], ALU.mult)
        yield
        sfx = "_s" if is_sample else "_p"
        bk = P.next()
        kb.mm(bk[:, 0:8], C("triU" + sfx), gv)
        kb.mm(bk[:, 8:16], C("blk" + sfx), gv)
        kb.mm(bk[:, 16:24], C("triSU" + sfx), gv)
        kb.act(E[:], bk[:, 0:24], AF.Exp)
        chk(3)
        yield

    def gdn_stage1(ti, hg, ii, is_sample):
        par = ti % 2
        xn = xnT[par]
        sc8 = sc8_2[ti % 2]
        E = E_2[ti % 2]
        sv, qT, knT = sv_2[ii % 2], qT_2[ii % 2], knT_2[ii % 2]
        sfx = "_s" if is_sample else "_p"
        h0 = hg * 4
        pt = pT[hg]
        chunks = [h0 + i for i in range(4)] + [8 + h0 + i for i in range(4)] + [16 + h0 + i for i in range(4)]
        if is_sample:
            F4 = V(pt.t[:].rearrange("p c (s r) -> p c s r", r=11), pt.res)
            hT4 = V(histT.t[:].rearrange("p c (s r) -> p c s r", r=3), histT.res)
        else:
            kb.copy("pool", pt[:, :, 0:3], hist[hg][:])
        for grp in range(3):
            bk = P.next()
            for ci in range(4):
                col = chunks[grp * 4 + ci] * 128
                for k in range(8):
                    kb.mm(bk[:, ci * 128:(ci + 1) * 128], wia[k][:, col:col + 128], xn[:, k, :],
                          start=(k == 0), stop=(k == 7), inc=(k == 7 and ci == 3))
            eng = "act" if grp % 2 == 0 else "dve"
            if is_sample:
                c0 = chunks[grp * 4]
                kb.copy(eng, V(F4.ap[:, grp * 4:(grp + 1) * 4, :, 3:11], pt.res),
                        V(bk.t[:].rearrange("p (c s t) -> p c s t", c=4, s=16), bk.res))
                kb.copy("pool", V(F4.ap[:, grp * 4:(grp + 1) * 4, :, 0:3], pt.res),
                        V(hT4.ap[:, c0:c0 + 4, :, :], histT.res))
            else:
                kb.copy(eng, pt[:, grp * 4:(grp + 1) * 4, 3:131], v3(bk))
                yield
        if not is_sample:
            kb.copy("pool", hist[hg][:], pt[:, :, 128:131])
        bk = P.next()
        for k in range(8):
            kb.mm(bk[:], xn[:, k, :], wia[k][:, 3072 + hg * 512:3072 + (hg + 1) * 512],
                  start=(k == 0), stop=(k == 7))
        kb.act(zs[hg][:], bk[:], AF.Silu)
        yield
        mx = mixed[hg]
        if is_sample:
            for c in range(12):
                cg = chunks[c]
                cvv = V(cv.t[:, c, :].rearrange("p (s t) -> p s t", t=8), cv.res)
                kb.ts("dve", cvv, V(F4.ap[:, c, :, 0:8], pt.res), C("cwT", cg * 4, cg * 4 + 1), ALU.mult)
                for j in range(1, 4):
                    kb.stt(cvv, V(F4.ap[:, c, :, j:j + 8], pt.res), C("cwT", cg * 4 + j, cg * 4 + j + 1),
                           cvv, ALU.mult, ALU.add)
        for grp in range(3):
            bk = P.next()
            for ci in range(4):
                cg = chunks[grp * 4 + ci]
                if is_sample:
                    kb.mm(bk[:, ci * 128:(ci + 1) * 128], cv[:, grp * 4 + ci, :], idb[:], inc=(ci == 3))
                else:
                    for j in range(4):
                        kb.mm(bk[:, ci * 128:(ci + 1) * 128], pt[:, grp * 4 + ci, j:j + 128],
                              diag[:, cg * 4 + j, :], start=(j == 0), stop=(j == 3), inc=(j == 3 and ci == 3))
            kb.act(mx[:, grp * 512:(grp + 1) * 512], bk[:], AF.Silu)
            yield
        chk(4)
        kb.act(sqb[:], mx[:, 0:1024], AF.Square)
        kb.reduce_sum(r8[:, 0:8], v3(sqb, 8))
        yield
        kb.ts("dve", r8[:, 0:8], r8[:, 0:8], EPS, ALU.add)
        kb.tt("pool", r8[:, 8:16], r8[:, 0:8], neghalf[:, 0:8], ALU.pow)
        yield
        rq = r8[:, 8:12]
        rk = r8[:, 12:16]
        eG = E[:, h0:h0 + 4]
        ekl = E[:, 16 + h0:16 + h0 + 4]
        kb.ts("dve", sc8[:, 32:36], rq, 128 ** -0.5, ALU.mult)
        kb.tt("dve", sc8[:, 36:40], sc8[:, 32:36], eG, ALU.mult)
        kb.tt("dve", sc8[:, 40:44], rk, eG, ALU.mult)
        kb.tt("dve", sc8[:, 40:44], sc8[:, 40:44], sc8[:, h0:h0 + 4], ALU.mult)
        kb.tt("dve", sc8[:, 44:48], rk, ekl, ALU.mult)
        yield
        qv = V(mx.t[:, 0:512].rearrange("p (h d) -> p h d", h=4), mx.res)
        kv = V(mx.t[:, 512:1024].rearrange("p (h d) -> p h d", h=4), mx.res)
        vv = V(mx.t[:, 1024:1536].rearrange("p (h d) -> p h d", h=4), mx.res)
        kb.tt("dve", sv["qn"][:], qv, sc_b(sc8[:, 32:36]), ALU.mult)
        kb.tt("pool", sv["qe"][:], qv, sc_b(sc8[:, 36:40]), ALU.mult)
        yield
        kb.tt("dve", sv["kn"][:], kv, sc_b(rk), ALU.mult)
        kb.tt("pool", sv["kw"][:], kv, sc_b(sc8[:, 40:44]), ALU.mult)
        yield
        kb.tt("dve", sv["kd"][:], kv, sc_b(sc8[:, 44:48]), ALU.mult)
        kb.tt("pool", sv["vb"][:], vv, sc_b(sc8[:, h0:h0 + 4]), ALU.mult)
        yield
        bk = P.next()
        pv = bfv(bk)
        for i, nm in enumerate(("qn", "qe")):
            for h in range(4):
                c0 = (i * 4 + h) * 128
                kb.tr(V(pv.ap[:, c0:c0 + 128], pv.res), sv[nm][:, h, :], idb[:], inc=(i == 1 and h == 3))
        kb.copy("act", qT[:], V(pv.ap.rearrange("p (k t) -> p k t", k=8), pv.res))
        yield
        bk = P.next()
        pv = bfv(bk)
        for h in range(4):
            kb.tr(V(pv.ap[:, h * 128:(h + 1) * 128], pv.res), sv["kn"][:, h, :], idb[:], inc=(h == 3))
        kb.copy("dve", knT[:], V(pv.ap[:, 0:512].rearrange("p (k t) -> p k t", k=4), pv.res))

    def gdn_stage2(ti, hg, ii, is_sample):
        if is_sample:
            sample_prefetch(hg)
        sc8 = sc8_2[ti % 2]
        E = E_2[ti % 2]
        sv, qT, knT = sv_2[ii % 2], qT_2[ii % 2], knT_2[ii % 2]
        sfx = "_s" if is_sample else "_p"
        h0 = hg * 4
        mx = mixed[hg]
        chk(5)
        kb.tt("dve", gm[:], bc(un(sc8[:, 24 + h0:24 + h0 + 4], 2), [128, 4, 128]),
              bc(un(negtriUf[:], 1), [128, 4, 128]), ALU.mult)
        kb.copy("act", gb[:], bc(un(sc8[:, 24 + h0:24 + h0 + 4], 2), [128, 4, 128]))
        yield
        gmf = V(gm.t[:].rearrange("p h j -> p (h j)"), gm.res)
        gbf = V(gb.t[:].rearrange("p h j -> p (h j)"), gb.res)
        chk(51)
        bk = P.next()
        kb.mm(bk[:], mr["triU"][:], gbf, start=True, stop=False)
        kb.mm(bk[:], mr["ones"][:], gmf, start=False, stop=False)
        kb.mm(bk[:], mr["ident"][:], mr["maskneg"][:], start=False, stop=True)
        chk(52)
        kb.act(V(dec.t[:].rearrange("p h j -> p (h j)"), dec.res), bk[:], AF.Exp)
        yield
        chk(53)
        bk = P.next()
        kb.mm(bk[:], mr["negtriU"][:], gbf, start=True, stop=False)
        kb.mm(bk[:], mr["negones"][:], gmf, start=False, stop=False)
        kb.mm(bk[:], mr["ident"][:], mr["masknegT"][:], start=False, stop=True)
        chk(54)
        kb.act(V(decT.t[:].rearrange("p h j -> p (h j)"), decT.res), bk[:], AF.Exp)
        yield
        chk(55)
        chk(6)
        bk = P.next()
        for h in range(4):
            kb.mm(bk[:, h * 128:(h + 1) * 128], knT[:, h, :], knT[:, h, :], inc=(h == 3))
        kb.tt("dve", dec[:], dec[:], bc(un(C("strict" + sfx), 1), [128, 4, 128]), ALU.mult)
        kb.tt("dve", dec[:], dec[:], sc_b(sc8[:, 8 + h0:8 + h0 + 4]), ALU.mult)
        yield
        Pc, PTc, Yc = Mb, MTb, Yb[0]
        kb.tt("dve", Pc[:], v3(bk), dec[:], ALU.mult)
        yield
        chk(61)
        bk = P.next()
        pv = bfv(bk)
        for h in range(4):
            kb.tr(V(pv.ap[:, h * 128:(h + 1) * 128], pv.res), Pc[:, h, :], idb[:], inc=(h == 3))
        pv4 = V(pv.ap[:, 0:512].rearrange("p (k t) -> p k t", k=4), pv.res)
        kb.copy("act", PTc[:], pv4)
        kb.tt("dve", Yc[:], pv4, bc(un(idb[:], 1), [128, 4, 128]), ALU.add)
        yield
        chk(62)
        nsteps = 3 if is_sample else 6
        for stp in range(1, nsteps):
            Pn, PTn, Yn = Pb[stp % 2], PTb[stp % 2], Yb[stp % 2]
            bkA = P.next()
            for h in range(4):
                kb.mm(bkA[:, h * 128:(h + 1) * 128], PTc[:, h, :], Pc[:, h, :], inc=(h == 3))
            last = (stp == nsteps - 1)
            if not last:
                bkB = P.next()
                for h in range(4):
                    kb.mm(bkB[:, h * 128:(h + 1) * 128], Pc[:, h, :], PTc[:, h, :], inc=(h == 3))
            kb.copy("act", Pn[:], v3(bkA))
            if not last:
                kb.copy("dve", PTn[:], v3(bkB))
                yield
            bkC = P.next()
            for h in range(4):
                kb.mm(bkC[:, h * 128:(h + 1) * 128], Pn[:, h, :], Yc[:, h, :], inc=(h == 3))
            kb.tt("dve", Yn[:], v3(bkC), Yc[:], ALU.add)
            yield
            Pc, PTc, Yc = Pn, PTn, Yn
        bk = P.next()
        pv = bfv(bk)
        for h in range(4):
            kb.tr(V(pv.ap[:, h * 128:(h + 1) * 128], pv.res), Yc[:, h, :], idb[:], inc=(h == 3))
        X0b, Rb = PTb[0], Pb[0]
        kb.copy("act", X0b[:], V(pv.ap[:, 0:512].rearrange("p (k t) -> p k t", k=4), pv.res))
        yield
        bk = P.next()
        for h in range(4):
            kb.mm(bk[:, h * 128:(h + 1) * 128], MTb[:, h, :], X0b[:, h, :], start=True, stop=False)
            kb.mm(bk[:, h * 128:(h + 1) * 128], negidb[:], X0b[:, h, :], start=False, stop=True, inc=(h == 3))
        kb.tt("dve", Rb[:], v3(bk), bc(un(idb[:], 1), [128, 4, 128]), ALU.add)
        yield
        bk = P.next()
        for h in range(4):
            kb.mm(bk[:, h * 128:(h + 1) * 128], Rb[:, h, :], Yc[:, h, :], inc=(h == 3))
        Yn = Yb[1] if Yc is Yb[0] else Yb[0]
        kb.tt("dve", Yn[:], v3(bk), Yc[:], ALU.add)
        Yc = Yn
        chk(63)
        bk = P.next()
        for h in range(4):
            kb.mm(bk[:, h * 128:(h + 1) * 128], sv["kw"][:, h, :], Yc[:, h, :], inc=(h == 3))
        kb.act(negWT[:], v3(bk), AF.Copy, scale=-1.0)
        yield
        bk = P.next()
        for h in range(4):
            kb.mm(bk[:, h * 128:(h + 1) * 128], knT[:, h, :], qT[:, h, :], inc=(h == 3))
        kb.tt("dve", qkdT[:], v3(bk), decT[:], ALU.mult)
        yield
        if dbg and ti == dbg_tile and hg == 0:
            kb.dma("sp", ext(dbg_d["d_misc"][:, 0:64]), sc8[:], ch_dbg)
            kb.dma("sp", ext(dbg_d["d_dec"]), V(dec.t[:].rearrange("p h j -> p (h j)"), dec.res), ch_dbg)
        chk(7)
        if not is_sample:
            first = (ti == 0)
            bu = P.next()
            for h in range(4):
                kb.mm(bu[:, h * 128:(h + 1) * 128], Yc[:, h, :], sv["vb"][:, h, :], start=True, stop=first,
                      inc=(first and h == 3))
                if not first:
                    kb.mm(bu[:, h * 128:(h + 1) * 128], negWT[:, h, :], Sbf[:, h0 + h, :], start=False,
                          stop=True, inc=(h == 3))
            kb.copy("act", ub[:], v3(bu))
            yield
            bo = P.next()
            for h in range(4):
                if not first:
                    kb.mm(bo[:, h * 128:(h + 1) * 128], qT[:, 4 + h, :], Sbf[:, h0 + h, :], start=True,
                          stop=False)
                kb.mm(bo[:, h * 128:(h + 1) * 128], qkdT[:, h, :], ub[:, h, :], start=first, stop=True,
                      inc=(h == 3))
            bs = P.next()
            for h in range(4):
                kb.mm(bs[:, h * 128:(h + 1) * 128], sv["kd"][:, h, :], ub[:, h, :], inc=(h == 3))
            for h in range(4):
                if first:
                    kb.copy("dve", S[:, h0 + h, :], bs[:, h * 128:(h + 1) * 128])
                else:
                    kb.stt(S[:, h0 + h, :], S[:, h0 + h, :], E[:, 8 + h0 + h:8 + h0 + h + 1],
                           bs[:, h * 128:(h + 1) * 128], ALU.mult, ALU.add)
            kb.copy("act", Sbf[:, h0:h0 + 4, :], S[:, h0:h0 + 4, :])
            yield
            o_src = bo[:]
        else:
            o_src = gdn_sample_rec(hg, Yc, sv, qT, sc8)
        chk(8)
        o3 = V(o_src.ap.rearrange("p (h d) -> p h d", h=4), o_src.res)
        kb.act(osq[:], o_src, AF.Square)
        kb.reduce_sum(sc8[:, 48:52], V(osq.t[:].rearrange("p (h d) -> p h d", h=4), osq.res))
        yield
        kb.ts("dve", sc8[:, 48:52], sc8[:, 48:52], 1.0 / 128, ALU.mult, EPS, ALU.add)
        kb.tt("pool", sc8[:, 52:56], sc8[:, 48:52], neghalf[:, 0:4], ALU.pow)
        yield
        kb.tt("dve", v3(otmp), o3, sc_b(sc8[:, 52:56]), ALU.mult)
        kb.tt("dve", on[:, hg * 512:(hg + 1) * 512], otmp[:], zs[hg][:], ALU.mult)
        if dbg and ti == dbg_tile and hg == 0:
            kb.dma("pool", ext(dbg_d["d_mixed"]), mx[:], ch_dbg)
            kb.dma("pool", ext(dbg_d["d_Y"]), otmp[:], ch_dbg)

    def gdn_epilogue(ti, is_sample):
        par = ti % 2
        xn = xnT[par]
        src = ext(xs_d[:, :]) if is_sample else ext(xp_d[ti * 128:(ti + 1) * 128, :])
        kb.dma("sp", yt[:], src, ch_yt)
        chk(9)
        bk = P.next()
        pv = bfv(bk)
        for k in range(8):
            kb.tr(V(pv.ap[:, k * 128:(k + 1) * 128], pv.res), on[:, k * 128:(k + 1) * 128], idb[:], inc=(k == 7))
        kb.copy("act", onT[:], V(pv.ap.rearrange("p (k t) -> p k t", k=8), pv.res))
        yield
        for n in range(2):
            bk = P.next()
            for k in range(8):
                kb.mm(bk[:], onT[:, k, :], woa[k][:, n * 512:(n + 1) * 512], start=(k == 0), stop=(k == 7))
            kb.tt("dve", yt[:, n * 512:(n + 1) * 512], bk[:], yt[:, n * 512:(n + 1) * 512], ALU.add)
            yield
        kb.dma("sp", V(x1_d[ti * 128:(ti + 1) * 128, :], x1_res[ti]), yt[:], ch_yt)
        if dbg:
            dr = n_ptiles if is_sample else ti
            kb.dma("sp", ext(dbg_d["d_y1"][dr * 128:(dr + 1) * 128, :]), yt[:], ch_yt)
        if is_sample or ti == n_ptiles - 1:
            for cb in range(3):
                for half in range(2):
                    bk = P.next()
                    col = cb * 1024 + half * 512
                    for k in range(8):
                        kb.mm(bk[:], xn[:, k, :], wia[k][:, col:col + 512], start=(k == 0), stop=(k == 7))
                    kb.copy("act" if half else "dve", xt[:, half * 512:(half + 1) * 512], bk[:])
                    yield
                if is_sample:
                    for s_ in range(NSEQ):
                        kb.dma("sp", ext(cas_d[s_ * 3:(s_ + 1) * 3, cb * 1024:(cb + 1) * 1024]),
                               xt[s_ * 8 + 5:s_ * 8 + 8, :], ch_xt)
                else:
                    kb.dma("sp", ext(cap_d[:, cb * 1024:(cb + 1) * 1024]), xt[125:128, :], ch_xt)
        if (not is_sample) and ti == n_ptiles - 1:
            kb.dma("sp", ext(sap_d.rearrange("h k v -> k h v")), S[:], ch_out)
        yield


    smpst = {}

    def vi(v, *idx):
        return V(v.ap[idx], v.res)

    def sample_slots():
        if smpst:
            return smpst
        bsl, fsl = [], []
        for i in range(16):
            r = Res(f"smb{i}")
            r.w = diag.res.w
            r.r = dict(diag.res.r)
            bsl.append(V(diag.t[:, i * 4:(i + 1) * 4, :], r))
        for i in range(4):
            r = Res(f"smf{i}")
            r.w = diag.res.w
            r.r = dict(diag.res.r)
            ap = diag.t[:, 64 + i * 8:64 + (i + 1) * 8, :].rearrange("p a b -> p (a b)").bitcast(F32)
            fsl.append(V(ap.rearrange("p (h d) -> p h d", h=4), r))
        smpst["b"] = bsl
        smpst["f"] = fsl
        smpst["chb"] = [kb.chan(f"smb{i}") for i in range(16)]
        smpst["chf"] = [kb.chan(f"smf{i}") for i in range(4)]
        smpst["chfo"] = [kb.chan(f"smfo{i}") for i in range(4)]
        return smpst

    def sample_ld_f(hg, q_):
        if q_ >= NSEQ:
            return
        sm = sample_slots()
        h0 = hg * 4
        kb.dma("sp", sm["f"][q_ % 4], ext(sg_d[q_, h0:h0 + 4].rearrange("h k v -> k h v")), sm["chf"][q_ % 4])

    def sample_prefetch(hg):
        sm = sample_slots()
        h0 = hg * 4
        for q_ in range(NSEQ):
            kb.dma("pool", sm["b"][q_], ext(sg_d[q_, h0:h0 + 4].rearrange("h k v -> k h v")), sm["chb"][q_])
        for q_ in range(3):
            sample_ld_f(hg, q_)

    def gdn_sample_rec(hg, Yc, sv, qT, sc8):
        sm = sample_slots()
        h0 = hg * 4
        kb.tt("dve", gmsk[:], bc(un(sc8[:, 24 + h0:24 + h0 + 4], 1), [128, 16, 4]),
              bc(un(C("seqmask"), 2), [128, 16, 4]), ALU.mult)
        bk = P.next()
        kb.mm(bk[:, 0:64], onesf[:], V(gmsk.t[:].rearrange("p s h -> p (s h)"), gmsk.res))
        kb.act(abc[:], bk[:, 0:64], AF.Exp)
        buT = P.next(hold=True)
        for h in range(4):
            kb.mm(buT[:, h * 128:(h + 1) * 128], sv["vb"][:, h, :], Yc[:, h, :], start=(h == 0), stop=False,
                  inc=False)
        for s_ in range(NSEQ):
            for h in range(4):
                lastmm = (s_ == NSEQ - 1 and h == 3)
                kb.mm(buT[:, h * 128 + s_ * 8:h * 128 + s_ * 8 + 8], vi(sm["b"][s_], slice(None), h, slice(None)),
                      negWT[:, h, s_ * 8:s_ * 8 + 8], start=False, stop=lastmm, inc=(h == 3))
        kb.copy("act", uTb[:], v3(buT))
        P.release(buT)
        bk = P.next()
        pv = bfv(bk)
        for h in range(4):
            kb.tr(V(pv.ap[:, h * 128:(h + 1) * 128], pv.res), uTb[:, h, :], idb[:], inc=(h == 3))
        kb.copy("dve", ub[:], V(pv.ap[:, 0:512].rearrange("p (k t) -> p k t", k=4), pv.res))
        boT = P.next(hold=True)
        for h in range(4):
            kb.mm(boT[:, h * 128:(h + 1) * 128], ub[:, h, :], qkdT[:, h, :], start=(h == 0), stop=False, inc=False)
        for s_ in range(NSEQ):
            sample_ld_f(hg, s_ + 3)
            fs_ = sm["f"][s_ % 4]
            for h in range(4):
                lastmm = (s_ == NSEQ - 1 and h == 3)
                kb.mm(boT[:, h * 128 + s_ * 8:h * 128 + s_ * 8 + 8], vi(sm["b"][s_], slice(None), h, slice(None)),
                      qT[:, 4 + h, s_ * 8:s_ * 8 + 8], start=False, stop=lastmm, inc=(h == 3))
            kdm = Pb[s_ % 2]
            kb.ts("dve", kdm[:], sv["kd"][:], C("seqmask", s_, s_ + 1), ALU.mult)
            bs = P.next()
            for h in range(4):
                kb.mm(bs[:, h * 128:(h + 1) * 128], kdm[:, h, :], ub[:, h, :], inc=(h == 3))
            for h in range(4):
                fh = vi(fs_, slice(None), h, slice(None))
                kb.stt(fh, fh, abc[:, s_ * 4 + h:s_ * 4 + h + 1], bs[:, h * 128:(h + 1) * 128], ALU.mult, ALU.add)
            kb.dma("act", ext(sas_d[s_, h0:h0 + 4].rearrange("h k v -> k h v")), fs_, sm["chfo"][s_ % 4])
        kb.copy("act", uTb[:], v3(boT))
        P.release(boT)
        bk = P.next()
        pv = bfv(bk)
        for h in range(4):
            kb.tr(V(pv.ap[:, h * 128:(h + 1) * 128], pv.res), uTb[:, h, :], idb[:], inc=(h == 3))
        return V(pv.ap[:, 0:512], pv.res)

    def sample_hist_setup():
        for piece in range(3):
            kb.dma("sp", yt[0:48, :], ext(sc_d[:, piece * 1024:(piece + 1) * 1024]), ch_yt)
            bk = P.next()
            for c in range(8):
                kb.tr(bk[:, c * 48:(c + 1) * 48], yt[0:48, c * 128:(c + 1) * 128], V(identf.ap[0:48, 0:48], identf.res),
                      inc=(c == 7))
            kb.copy("dve", histT[:, piece * 8:(piece + 1) * 8, :],
                    V(bk.t[:, 0:384].rearrange("p (c r) -> p c r", c=8), bk.res))

    dbg_tile = 1 if n_ptiles > 1 else 0
    try:
        load_masks("p")
        chk(1)
        if do_sample:
            sample_hist_setup()
        tiles = [(ti, False) for ti in range(n_ptiles)] + ([(NPT, True)] if do_sample else [])
        items = [(ti, hg, smp) for (ti, smp) in tiles for hg in range(2)]
        for _ in gdn_prologue(tiles[0][0], tiles[0][1]):
            pass
        masks_s = False
        def drive(gens):
            gens = [g_ for g_ in gens if g_ is not None]
            while gens:
                for g_ in list(gens):
                    try:
                        next(g_)
                    except StopIteration:
                        gens.remove(g_)
        carry = None
        ch_wc = kb.chan("wcast")
        cast_k = [0]

        def cast_chunk():
            k_ = cast_k[0]
            if k_ >= 8:
                return
            cast_k[0] += 1
            for p_, (c0, c1) in enumerate(((0, 2048), (2048, 4096), (4096, 6144))):
                kb.dma("pool", V(wibb_d[k_ * 128:(k_ + 1) * 128, c0:c1], wibb_res[k_][p_]),
                       ext(wib_d[k_ * 128:(k_ + 1) * 128, c0:c1]), ch_wc)
        for ii in range(len(items) + 1):
            if ii >= 2:
                cast_chunk()
            gens = []
            if carry is not None:
                gens.append(carry)
                carry = None
            hg2 = None
            if ii >= 1:
                ti2, hg2, smp2 = items[ii - 1]
                if smp2 and not masks_s:
                    load_masks("s")
                    masks_s = True
                gens.append(gdn_stage2(ti2, hg2, ii - 1, smp2))
            if ii < len(items):
                ti, hg, smp_ = items[ii]
                gens.append(gdn_stage1(ti, hg, ii, smp_))
                if hg == 1:
                    nxt = [tt_ for tt_ in tiles if tt_[0] > ti]
                    if nxt:
                        gens.append(gdn_prologue(nxt[0][0], nxt[0][1]))
            drive(gens)
            if hg2 == 1:
                carry = gdn_epilogue(ti2, smp2)
        if carry is not None:
            drive([carry])
        while cast_k[0] < 8:
            cast_chunk()
        for rk_ in wibb_res:
            for r_ in rk_:
                r_.w = (ch_wc.sem, ch_wc.count, "dma")
    except Cut:
        pass

    if not do_l1:
        kb.final_wait()
        es0.close()
        return nc

    kb.barrier()
    es0.close()
    es1 = ExitStack()
    ch_w1 = kb.chan("w1")
    wib = [kb.sb(f"wib{k}", [128, 6144], BF16, es1) for k in range(8)]
    for k in range(8):
        qn_ = "sp" if k % 2 == 0 else "act"
        kb.dma(qn_, wib[k][:], V(wibb_d[k * 128:(k + 1) * 128, :], tuple(wibb_res[k])), ch_w1)
    wob = [kb.sb(f"wob{k}", [128, 1024], BF16, es1) for k in range(16)]
    for k in range(16):
        st, chn = (xt, ch_xt) if k % 2 == 0 else (yt, ch_yt)
        kb.dma("sp", st[:], ext(wob_d[k * 128:(k + 1) * 128, :]), chn)
        kb.ts("dve", wob[k][:], st[:], C("onwbT", k, k + 1), ALU.mult)
    for k in range(8):
        wib[k].res.w = (ch_w1.sem, ch_w1.count, "dma")
    fnw = kb.sb("fnw", [128, D], F32, es1)
    ch_fnw = kb.chan("fnw")
    ch_dm = kb.chan("dm")
    ch_dms = kb.chan("dms")
    kb.dma("sp", fnw[:], ext(fnw_d[0:1, :].broadcast_to([128, D])), ch_fnw)
    dm = kb.sb("dm", [128, 4, 128], F32, es1)
    rot = kb.sb("rot", [128, 2, 128], F32, es1)
    ch_rot = kb.chan("rot")
    tmpA = kb.sb("tmpA", [128, 2, 128], F32, es1)
    tmpB = kb.sb("tmpB", [128, 2, 128], F32, es1)
    tmpC = kb.sb("tmpC", [128, 2, 128], F32, es1)
    tmpD = kb.sb("tmpD", [128, 2, 128], F32, es1)
    qkr = kb.sb("qkr", [128, 2, 2, 128], BF16, es1)
    qd = kb.sb("qd", [128, 256], BF16, es1)
    kdd = kb.sb("kdd", [128, 256], BF16, es1)
    qkT = kb.sb("qkT", [128, 6, 128], BF16, es1)
    vb1 = kb.sb("vb1", [128, 512], BF16, es1)
    gs1 = kb.sb("gs1", [128, 512], BF16, es1)
    qkd1 = kb.sb("qkd1", [128, 128], BF16, es1)
    S1 = kb.sbs("S1", [128, 4, 2, 512], F32, 1, es1)
    S1b = kb.sbs("S1b", [128, 4, 2, 512], BF16, 1, es1)
    on1 = kb.sb("on1", [128, 2048], BF16, es1)
    on1T = kb.sb("on1T", [128, 16, 128], BF16, es1)
    st1 = kb.sb("st1", [128, 8], F32, es1)
    zb = [kb.sb(f"zb{i}", [128, 2, 128], BF16, es1) for i in range(2)]
    kddm = [kb.sb(f"kddm{i}", [128, 256], BF16, es1) for i in range(2)]
    for z_ in zb:
        kb.memset("pool", z_[:], 0.0)
    ch_s1 = [kb.chan(f"s1_{i}") for i in range(4)]
    ch_s1o = [kb.chan(f"s1o_{i}") for i in range(4)]
    gam = [1.0 - 2.0 ** (-5.0 - h) for h in range(4)]

    kdd2 = [kdd, kb.sb("kdd_b", [128, 256], BF16, es1)]
    qkT2 = [qkT, kb.sb("qkT_b", [128, 6, 128], BF16, es1)]
    vb12 = [vb1, kb.sb("vb1_b", [128, 512], BF16, es1)]
    gs12 = [gs1, kb.sb("gs1_b", [128, 512], BF16, es1)]
    rot2 = [rot, kb.sb("rot_b", [128, 2, 128], F32, es1)]
    ch_rot2 = [ch_rot, kb.chan("rot_b")]
    junk1 = kb.sb("junk1", [128, 512], BF16, es1)

    def ret_prologue(ti, is_sample):
        src = V(x1_d[ti * 128:(ti + 1) * 128, :], x1_res[ti])
        yield from front_end(src, "nwT1", ti % 2)
        kb.dma("sp", rot2[ti % 2][:], ext(rot_d[ti]), ch_rot2[ti % 2])
        yield

    def ret_stage1(ti, h, ii, is_sample):
        xn = xnT[ti % 2]
        rt = rot2[ti % 2]
        kdd_, qkT_, vb_, gs_ = kdd2[ii % 2], qkT2[ii % 2], vb12[ii % 2], gs12[ii % 2]
        rd = "rdec_s" if is_sample else "rdec_p"
        bk = P.next()
        for part, c0 in ((0, h * 256), (1, 1024 + h * 256)):
            for k in range(8):
                kb.mm(bk[:, part * 256:(part + 1) * 256], xn[:, k, :], wib[k][:, c0:c0 + 256],
                      start=(k == 0), stop=(k == 7), inc=(k == 7 and part == 1))
        pq = V(bk.t[:].rearrange("p (a b d) -> p a b d", a=2, b=2), bk.res)
        x1v = V(pq.ap[:, :, 0, :], bk.res)
        x2v = V(pq.ap[:, :, 1, :], bk.res)
        cosb = bc(un(rt[:, 0, :], 1), [128, 2, 128])
        sinb = bc(un(rt[:, 1, :], 1), [128, 2, 128])
        kb.tt("dve", tmpA[:], x1v, cosb, ALU.mult)
        kb.tt("dve", tmpB[:], x2v, sinb, ALU.mult)
        kb.tt("dve", V(qkr.t[:, :, 0, :], qkr.res), tmpA[:], tmpB[:], ALU.subtract)
        yield
        kb.tt("dve", tmpC[:], x1v, sinb, ALU.mult)
        kb.tt("dve", tmpD[:], x2v, cosb, ALU.mult)
        kb.tt("dve", V(qkr.t[:, :, 1, :], qkr.res), tmpC[:], tmpD[:], ALU.add)
        qr = V(qkr.t[:, 0, :, :].rearrange("p b d -> p (b d)"), qkr.res)
        kr = V(qkr.t[:, 1, :, :].rearrange("p b d -> p (b d)"), qkr.res)
        kb.ts("dve", qd[:], qr, C(rd, h, h + 1), ALU.mult)
        kb.ts("dve", kdd_[:], kr, C(rd, 4 + h, 5 + h), ALU.mult)
        yield
        bk = P.next()
        for k in range(8):
            kb.mm(bk[:], xn[:, k, :], wib[k][:, 2048 + h * 512:2048 + (h + 1) * 512], start=(k == 0), stop=(k == 7))
        kb.copy("act", vb_[:], bk[:])
        yield
        bk = P.next()
        for k in range(8):
            kb.mm(bk[:], xn[:, k, :], wib[k][:, 4096 + h * 512:4096 + (h + 1) * 512], start=(k == 0), stop=(k == 7))
        kb.act(gs_[:], bk[:], AF.Silu)
        yield
        bk = P.next()
        pv = bfv(bk)
        srcs = [qkr[:, 0, 0, :], qkr[:, 0, 1, :], qkr[:, 1, 0, :], qkr[:, 1, 1, :], qd[:, 0:128], qd[:, 128:256]]
        for i_, sv_ in enumerate(srcs):
            kb.tr(V(pv.ap[:, i_ * 128:(i_ + 1) * 128], pv.res), sv_, idb[:], inc=(i_ == 5))
        kb.copy("act", qkT_[:], V(pv.ap[:, 0:768].rearrange("p (k t) -> p k t", k=6), pv.res))
        yield

    def ret_stage2(ti, h, ii, is_sample, n_tiles_first):
        kdd_, qkT_, vb_, gs_ = kdd2[ii % 2], qkT2[ii % 2], vb12[ii % 2], gs12[ii % 2]
        first = (ti == 0)
        dmk = dm
        bk = P.next()
        kb.mm(bk[:, 0:128], qkT_[:, 2, :], qkT_[:, 0, :], start=True, stop=False)
        kb.mm(bk[:, 0:128], qkT_[:, 3, :], qkT_[:, 1, :], start=False, stop=True)
        kb.tt("dve", qkd1[:], bk[:, 0:128], dmk[:, h, :], ALU.mult)
        yield
        bo = P.next(hold=True)
        if not is_sample:
            kb.mm(bo[:], qkd1[:], vb_[:], start=True, stop=first)
            if not first:
                kb.mm(bo[:], qkT_[:, 4, :], S1b[:, h, 0, :], start=False, stop=False)
                kb.mm(bo[:], qkT_[:, 5, :], S1b[:, h, 1, :], start=False, stop=True)
            for c in range(2):
                bs = P.next()
                kb.mm(bs[:], kdd_[:, c * 128:(c + 1) * 128], vb_[:])
                if first:
                    kb.copy("dve", S1[:, h, c, :], bs[:])
                else:
                    kb.stt(S1[:, h, c, :], S1[:, h, c, :], float(gam[h] ** 128), bs[:], ALU.mult, ALU.add)
            kb.copy("act", S1b[:, h, :, :], S1[:, h, :, :])
            yield
        else:
            kb.mm(bo[:], qkd1[:], vb_[:], start=True, stop=False, inc=False)

            def ld1(pi):
                if pi >= 4 * NSEQ:
                    return
                hh, ss = divmod(pi, NSEQ)
                kb.dma("sp", S1[:, pi % 4, :, :], ext(sr_d[ss, hh].rearrange("(c p) v -> p c v", p=128)),
                       ch_s1[pi % 4])
            def cast1(pi_):
                if pi_ < 4 * NSEQ:
                    kb.copy("act", S1b[:, pi_ % 4, :, :], S1[:, pi_ % 4, :, :])
            if h == 0:
                ld1(0)
                ld1(1)
                cast1(0)
            for s_ in range(NSEQ):
                pi = h * NSEQ + s_
                sl = pi % 4
                ld1(pi + 2)
                z_ = zb[s_ % 2]
                kb.copy("dve", z_[:, :, s_ * 8:s_ * 8 + 8], qkT_[:, 4:6, s_ * 8:s_ * 8 + 8])
                kb.mm(bo[:], z_[:, 0, :], S1b[:, sl, 0, :], start=False, stop=False, inc=False)
                kb.mm(bo[:], z_[:, 1, :], S1b[:, sl, 1, :], start=False, stop=(s_ == NSEQ - 1), inc=True)
                kb.memset("dve", z_[:, :, s_ * 8:s_ * 8 + 8], 0.0)
                km = kddm[s_ % 2]
                kb.ts("dve", km[:], kdd_[:], C("seqmask", s_, s_ + 1), ALU.mult)
                bss = []
                for c in range(2):
                    bs = P.next()
                    kb.mm(bs[:], km[:, c * 128:(c + 1) * 128], vb_[:])
                    bss.append(bs)
                cast1(pi + 1)
                for c in range(2):
                    kb.stt(S1[:, sl, c, :], S1[:, sl, c, :], float(gam[h] ** 8), bss[c][:], ALU.mult, ALU.add)
                kb.dma("act", ext(sbs_d[s_, h].rearrange("(c p) v -> p c v", p=128)), S1[:, sl, :, :], ch_s1o[sl])
                yield
        kb.act(junk1[:], bo[:], AF.Square, accum=st1[:, 0:1])
        kb.ts("dve", st1[:, 1:2], st1[:, 0:1], 1.0 / 512, ALU.mult, EPS, ALU.add)
        kb.tt("pool", st1[:, 2:3], st1[:, 1:2], neghalf[:, 0:1], ALU.pow)
        yield
        kb.stt(on1[:, h * 512:(h + 1) * 512], bo[:], st1[:, 2:3], gs_[:], ALU.mult, ALU.mult)
        P.release(bo)
        yield

    def ret_epilogue(ti, is_sample, last_prompt):
        src = V(x1_d[ti * 128:(ti + 1) * 128, :], x1_res[ti])
        kb.dma("sp", yt[:], src, ch_yt)
        for g_ in range(2):
            bk = P.next()
            pv = bfv(bk)
            for k in range(8):
                kk_ = g_ * 8 + k
                kb.tr(V(pv.ap[:, k * 128:(k + 1) * 128], pv.res), on1[:, kk_ * 128:(kk_ + 1) * 128], idb[:], inc=(k == 7))
            kb.copy("act" if g_ else "dve", on1T[:, g_ * 8:(g_ + 1) * 8, :], V(pv.ap.rearrange("p (k t) -> p k t", k=8), pv.res))
            yield
        for n in range(2):
            bk = P.next()
            for k in range(16):
                kb.mm(bk[:], on1T[:, k, :], wob[k][:, n * 512:(n + 1) * 512], start=(k == 0), stop=(k == 15))
            kb.tt("dve", yt[:, n * 512:(n + 1) * 512], bk[:], yt[:, n * 512:(n + 1) * 512], ALU.add)
            yield
        kb.act(V(on1T.t[:].rearrange("p k t -> p (k t)")[:, 0:1024], on1T.res), yt[:], AF.Square, accum=st1[:, 4:5])
        kb.ts("dve", st1[:, 5:6], st1[:, 4:5], 1.0 / D, ALU.mult, EPS, ALU.add)
        kb.tt("pool", st1[:, 6:7], st1[:, 5:6], neghalf[:, 0:1], ALU.pow)
        yield
        kb.stt(yt[:], yt[:], st1[:, 6:7], fnw[:], ALU.mult, ALU.mult)
        dst = ext(ys_d[:, :]) if is_sample else ext(yp_d[ti * 128:(ti + 1) * 128, :])
        kb.dma("sp", dst, yt[:], ch_yt)
        yield

    try:
        kb.dma("sp", dm[:], ext(dmask_d[0]), ch_dm)
        tiles = [(ti, False) for ti in range(n_ptiles)] + ([(NPT, True)] if do_sample else [])
        items = [(ti, h, smp) for (ti, smp) in tiles for h in range(4)]
        for _ in ret_prologue(tiles[0][0], tiles[0][1]):
            pass
        carry = None
        dm_s_loaded = False
        for ii in range(len(items) + 1):
            gens = []
            if carry is not None:
                gens.append(carry)
                carry = None
            h2 = None
            if ii >= 1:
                ti2, h2, smp2 = items[ii - 1]
                if smp2 and not dm_s_loaded:
                    kb.dma("sp", dm[:], ext(dmask_d[1]), ch_dms)
                    dm_s_loaded = True
                gens.append(ret_stage2(ti2, h2, ii - 1, smp2, None))
            if ii < len(items):
                ti, h, smp_ = items[ii]
                gens.append(ret_stage1(ti, h, ii, smp_))
                if h == 1:
                    nxt = [tt_ for tt_ in tiles if tt_[0] > ti]
                    if nxt:
                        gens.append(ret_prologue(nxt[0][0], nxt[0][1]))
            drive(gens)
            if h2 == 3:
                if (not smp2) and ti2 == n_ptiles - 1:
                    kb.dma("sp", ext(sbp_d.rearrange("h (c p) v -> p h c v", p=128)), S1[:], ch_out)
                carry = ret_epilogue(ti2, smp2, False)
        if carry is not None:
            drive([carry])
    except Cut:
        pass
    kb.final_wait()
    es1.close()
    return nc


def _rot_tables():
    half = 128
    inv = 1.0 / (10000.0 ** np.linspace(0.0, 1.0, half))
    gam = 1.0 - 2.0 ** (-5.0 - np.arange(4))
    rot = np.zeros((NPT + 1, 128, 2, 128), np.float32)
    for ti in range(NPT + 1):
        if ti < NPT:
            pos = ti * 128 + np.arange(128, dtype=np.float64)
        else:
            pos = 16384.0 + (np.arange(128) % 8).astype(np.float64)
        ang = pos[:, None] * inv[None, :]
        rot[ti, :, 0, :] = np.cos(ang)
        rot[ti, :, 1, :] = np.sin(ang)
    return rot


def make_in_maps(I):
    f = np.float32
    cst = _build_cst(I["norm_w"], I["conv_w_a"], I["a_log_a"], I["dt_bias_a"], I["onorm_a"], I["onorm_b"])
    idb = np.eye(128, dtype=np.float32).astype(ml_dtypes.bfloat16)
    rot = _rot_tables()
    gam = 1.0 - 2.0 ** (-5.0 - np.arange(4, dtype=np.float64))
    dmask = np.zeros((2, 128, 4, 128), np.float32)
    ii = np.arange(128)
    for bi, blk in enumerate((128, 8)):
        same = (ii[:, None] // blk) == (ii[None, :] // blk)
        dif = ii[None, :] - ii[:, None]
        ok = (dif >= 0) & same
        for h in range(4):
            dmask[bi, :, h, :] = np.where(ok, gam[h] ** np.maximum(dif, 0) * 256.0 ** -0.5, 0.0)
    common = dict(
        wia=np.ascontiguousarray(I["w_in_a"][0], f), woa=np.ascontiguousarray(I["w_out_a"][0], f),
        wib=np.ascontiguousarray(I["w_in_b"][0], f), wob=np.ascontiguousarray(I["w_out_b"][0], f),
        cst=cst, idb=idb, fnw=np.ascontiguousarray(I["final_norm_w"].reshape(1, D), f), rot=rot, dmask=dmask)
    maps = []
    for c in range(NCORES):
        m = dict(common)
        m["xp"] = np.ascontiguousarray(I["x_prompt"][c], f)
        m["xs"] = np.ascontiguousarray(I["x_sample"][c * NSEQ:(c + 1) * NSEQ].reshape(128, D), f)
        m["sg"] = np.ascontiguousarray(I["state_gdn_ssm"][0, c * NSEQ:(c + 1) * NSEQ], f)
        m["sc"] = np.ascontiguousarray(I["state_gdn_conv"][0, c * NSEQ:(c + 1) * NSEQ].reshape(48, 3072), f)
        m["sr"] = np.ascontiguousarray(I["state_ret"][0, c * NSEQ:(c + 1) * NSEQ], f)
        maps.append(m)
    return maps


_NC_CACHE = {}


def kernel(**inputs):
    I = {k: np.asarray(v) for k, v in inputs.items()}
    if "nc" not in _NC_CACHE:
        _NC_CACHE["nc"] = build_program()
    nc = _NC_CACHE["nc"]
    in_maps = make_in_maps(I)
    res = run_bass_kernel_spmd(nc, in_maps, core_ids=list(range(NCORES))).results
    f = np.float32
    yp = np.stack([res[c]["yp"] for c in range(NCORES)]).astype(f)
    ys = np.concatenate([res[c]["ys"].reshape(NSEQ, 8, D) for c in range(NCORES)], 0).astype(f)
    sap = np.stack([res[c]["sap"] for c in range(NCORES)])[None].astype(f)
    cap = np.stack([res[c]["cap"] for c in range(NCORES)])[None].astype(f)
    sbp = np.stack([res[c]["sbp"] for c in range(NCORES)])[None].astype(f)
    sas = np.concatenate([res[c]["sas"] for c in range(NCORES)], 0)[None].astype(f)
    cas = np.concatenate([res[c]["cas"].reshape(NSEQ, 3, 3072) for c in range(NCORES)], 0)[None].astype(f)
    sbs = np.concatenate([res[c]["sbs"] for c in range(NCORES)], 0)[None].astype(f)
    return (yp, ys, sap, cap, sbp, sas, cas, sbs)
```

```python
import numpy as np
import ml_dtypes
from contextlib import ExitStack
import concourse.bass as bass
import concourse.mybir as mybir
from concourse.bass_utils import run_bass_kernel_spmd

F32 = mybir.dt.float32
F32R = mybir.dt.float32r
BF16 = mybir.dt.bfloat16
AF = mybir.ActivationFunctionType
ALU = mybir.AluOpType
AX = mybir.AxisListType

NCORES = 8
D = 1024
LP = 2048
NPT = LP // 128
NSEQ = 16
EPS = 1e-6
NEG = -32768.0


class Res:
    __slots__ = ("name", "w", "r", "excl", "strict")

    def __init__(self, name):
        self.name = name
        self.excl = False
        self.strict = False
        self.w = None
        self.r = {}


class V:
    __slots__ = ("ap", "res")

    def __init__(self, ap, res):
        self.ap = ap
        self.res = res


class TT:
    def __init__(self, t, name, nres=1):
        self.t = t
        self.res = Res(name)

    def __getitem__(self, idx):
        return V(self.t[idx], self.res)

    def v(self, ap):
        return V(ap, self.res)


class STT(TT):
    def __init__(self, t, name, n, slot_size):
        self.t = t
        self.n = n
        self.ss = slot_size
        self.slots = [Res(f"{name}_{i}") for i in range(n // slot_size)]
        self.res = tuple(self.slots)

    def __getitem__(self, idx):
        key = idx[1] if isinstance(idx, tuple) and len(idx) > 1 else slice(None)
        if isinstance(key, int):
            lo = hi = key
        else:
            lo = key.start or 0
            hi = (key.stop if key.stop is not None else self.n) - 1
        rs = tuple(self.slots[lo // self.ss:hi // self.ss + 1])
        return V(self.t[idx], rs if len(rs) > 1 else rs[0])


def _flat(xs):
    out = []
    for x in xs:
        r = x.res if isinstance(x, (V, TT)) else x
        if isinstance(r, tuple):
            out.extend(r)
        else:
            out.append(r)
    return out


class Chan:
    def __init__(self, sem):
        self.sem = sem
        self.count = 0


class EngQ:
    def __init__(self, name, eng, sem):
        self.name = name
        self.eng = eng
        self.sem = sem
        self.count = 0
        self.seen = {}


class KB:
    def __init__(self, nc, es):
        self.nc = nc
        self.es = es
        self.q = {}
        for name, eng in (("pe", nc.tensor), ("act", nc.scalar), ("dve", nc.vector),
                          ("pool", nc.gpsimd), ("sp", nc.sync)):
            sem = es.enter_context(nc.semaphore("sem_" + name))
            self.q[name] = EngQ(name, eng, sem)
        self.chans = []
        self.n_instr = 0

    def sbs(self, name, shape, dt, slot_size, es=None):
        t = (es or self.es).enter_context(self.nc.sbuf_tensor("s_" + name, list(shape), dt))
        return STT(t, name, shape[1], slot_size)

    def sb(self, name, shape, dt, es=None):
        t = (es or self.es).enter_context(self.nc.sbuf_tensor("s_" + name, list(shape), dt))
        return TT(t, name)

    def chan(self, name):
        sem = self.es.enter_context(self.nc.semaphore("ch_" + name))
        c = Chan(sem)
        self.chans.append(c)
        return c

    def _need(self, q, ev):
        sem, val, owner = ev
        if q.seen.get(sem.num, 0) >= val:
            return
        q.eng.wait_ge(sem, val)
        q.seen[sem.num] = val

    def _deps(self, q, reads, writes):
        me = q.name
        for r in reads:
            if r is None:
                continue
            if r.w is not None:
                self._need(q, r.w)
            if r.excl:
                for ev in r.r.values():
                    if ev[2] != me:
                        self._need(q, ev)
        for w in writes:
            if w is None:
                continue
            if w.w is not None:
                if w.w[2] != me or me == "pool" or w.strict:
                    self._need(q, w.w)
            for ev in w.r.values():
                if ev[2] != me or me == "pool":
                    self._need(q, ev)

    def _record(self, ev, reads, writes):
        for r in reads:
            if r is not None:
                r.r[ev[0].num] = ev
        for w in writes:
            if w is not None:
                w.w = ev
                w.r = {}

    def op(self, eng, fn, reads, writes, inc=True):
        q = self.q[eng]
        reads = _flat(reads)
        writes = _flat(writes)
        self._deps(q, reads, writes)
        ins = fn(q.eng)
        self.n_instr += 1
        if inc:
            ins.then_inc(q.sem, 1)
            q.count += 1
            ev = (q.sem, q.count, q.name)
        else:
            ev = (q.sem, q.count + 1, q.name)
        self._record(ev, reads, writes)
        return ins

    def dma(self, qname, out, in_, chan, **kw):
        q = self.q[qname]
        reads = _flat([in_])
        writes = _flat([out])
        self._deps(q, reads, writes)
        ins = q.eng.dma_start(out=out.ap, in_=in_.ap, **kw)
        ins.then_inc(chan.sem, 16)
        chan.count += 16
        ev = (chan.sem, chan.count, "dma")
        self._record(ev, reads, writes)
        self.n_instr += 1
        return ev

    def barrier(self):
        evs = [(q.sem, q.count, q.name) for q in self.q.values() if q.count > 0]
        evs += [(c.sem, c.count, "dma") for c in self.chans if c.count > 0]
        for q in self.q.values():
            for ev in evs:
                if ev[2] == q.name:
                    continue
                self._need(q, ev)

    def final_wait(self):
        q = self.q["sp"]
        for c in self.chans:
            if c.count > 0:
                self._need(q, (c.sem, c.count, "dma"))
        for qq in self.q.values():
            if qq.name != "sp" and qq.count > 0:
                self._need(q, (qq.sem, qq.count, qq.name))

    def mm(self, out, lhsT, rhs, start=True, stop=True, inc=None):
        if inc is None:
            inc = stop
        return self.op("pe", lambda e: e.matmul(out.ap, lhsT.ap, rhs.ap, start=start, stop=stop,
                                                skip_group_check=True),
                       [lhsT, rhs], [out], inc=inc)

    def tr(self, out, in_, ident, inc=True):
        return self.op("pe", lambda e: e.transpose(out.ap, in_.ap, ident.ap), [in_, ident], [out], inc=inc)

    def act(self, out, in_, func, bias=None, scale=None, accum=None, extra_reads=()):
        kw = {}
        reads = [in_] + list(extra_reads)
        writes = [out]
        if bias is not None:
            if isinstance(bias, V):
                kw["bias"] = bias.ap
                reads.append(bias)
            else:
                kw["bias"] = bias
        if scale is not None:
            if isinstance(scale, V):
                kw["scale"] = scale.ap
                reads.append(scale)
            else:
                kw["scale"] = scale
        if accum is not None:
            kw["accum_out"] = accum.ap
            writes.append(accum)
        return self.op("act", lambda e: e.activation(out.ap, in_.ap, func, **kw), reads, writes)

    def tt(self, eng, out, in0, in1, op):
        return self.op(eng, lambda e: e.tensor_tensor(out.ap, in0.ap, in1.ap, op), [in0, in1], [out])

    def ts(self, eng, out, in0, s1, op0, s2=None, op1=None):
        reads = [in0]
        a1 = s1
        a2 = s2
        if isinstance(s1, V):
            reads.append(s1)
            a1 = s1.ap
        if isinstance(s2, V):
            reads.append(s2)
            a2 = s2.ap
        if op1 is None:
            return self.op(eng, lambda e: e.tensor_scalar(out.ap, in0.ap, a1, None, op0), reads, [out])
        return self.op(eng, lambda e: e.tensor_scalar(out.ap, in0.ap, a1, a2, op0, op1), reads, [out])

    def stt(self, out, in0, scalar, in1, op0, op1):
        reads = [in0, in1]
        a = scalar
        if isinstance(scalar, V):
            reads.append(scalar)
            a = scalar.ap
        return self.op("dve", lambda e: e.scalar_tensor_tensor(out.ap, in0.ap, a, in1.ap, op0, op1),
                       reads, [out])

    def copy(self, eng, out, in_):
        if eng == "act":
            return self.op("act", lambda e: e.copy(out.ap, in_.ap), [in_], [out])
        return self.op(eng, lambda e: e.tensor_copy(out.ap, in_.ap), [in_], [out])

    def memset(self, eng, out, val):
        return self.op(eng, lambda e: e.memset(out.ap, val), [], [out])

    def reduce_sum(self, out, in_):
        return self.op("dve", lambda e: e.tensor_reduce(out.ap, in_.ap, AX.X, ALU.add), [in_], [out])


def bc(v, shape):
    return V(v.ap.broadcast_to(list(shape)), v.res)


def un(v, axis):
    return V(v.ap.unsqueeze(axis), v.res)


def _mask_set(blk):
    i = np.arange(128)
    same = (i[:, None] // blk) == (i[None, :] // blk)
    triU = ((i[:, None] <= i[None, :]) & same).astype(np.float32)
    blkm = same.astype(np.float32)
    triSU = ((i[:, None] > i[None, :]) & same).astype(np.float32)
    strict = ((i[:, None] > i[None, :]) & same).astype(np.float32)
    incl = (i[:, None] >= i[None, :]) & same
    maskneg = np.where(incl, 0.0, NEG).astype(np.float32)
    masknegT = np.ascontiguousarray(maskneg.T)
    return dict(triU=triU, blk=blkm, triSU=triSU, strict=strict, maskneg=maskneg, masknegT=masknegT)


def _cst_layout():
    off = {}
    o = 0

    def add(name, n):
        nonlocal o
        off[name] = (o, n)
        o += n
    add("identf", 128)
    for s in ("p", "s"):
        for nm in ("triU", "blk", "triSU", "strict", "maskneg", "masknegT"):
            add(nm + "_" + s, 128)
    add("nwT0", 8)
    add("nwT1", 8)
    add("cwT", 96)
    add("dtb", 8)
    add("alog", 8)
    add("onwa", 1)
    add("onwbT", 16)
    add("seqmask", 16)
    add("rdec_p", 8)
    add("rdec_s", 8)
    return off, o


CST_OFF, CST_N = _cst_layout()


def _build_cst(norm_w, conv_w_a, a_log_a, dt_bias_a, onorm_a, onorm_b):
    c = np.zeros((128, CST_N), np.float32)

    def put(name, arr):
        o, n = CST_OFF[name]
        c[:, o:o + n] = arr
    put("identf", np.eye(128, dtype=np.float32))
    for s, blk in (("p", 128), ("s", 8)):
        m = _mask_set(blk)
        for nm in ("triU", "blk", "triSU", "strict", "maskneg", "masknegT"):
            put(nm + "_" + s, m[nm])
    put("nwT0", norm_w[0].reshape(8, 128).T)
    put("nwT1", norm_w[1].reshape(8, 128).T)
    cw = conv_w_a[0].reshape(4, 24, 128)
    put("cwT", np.transpose(cw, (2, 1, 0)).reshape(128, 96))
    put("dtb", np.broadcast_to(dt_bias_a[0][None, :], (128, 8)))
    put("alog", np.broadcast_to(a_log_a[0][None, :], (128, 8)))
    put("onwa", onorm_a[0].reshape(128, 1))
    put("onwbT", onorm_b[0].reshape(16, 128).T)
    sm = np.zeros((128, 16), np.float32)
    sm[np.arange(128), np.arange(128) // 8] = 1.0
    put("seqmask", sm)
    gam = 1.0 - 2.0 ** (-5.0 - np.arange(4, dtype=np.float64))
    for nm, blk in (("rdec_p", 128), ("rdec_s", 8)):
        t = (np.arange(128) % blk).astype(np.float64)
        qd = gam[None, :] ** (t[:, None] + 1.0)
        kd = gam[None, :] ** (blk - 1.0 - t[:, None]) * 256.0 ** -0.5
        put(nm, np.concatenate([qd, kd], 1))
    return c


def ext(ap):
    return V(ap, None)


class PsumPool:
    def __init__(self, kb, n=8):
        self.banks = [TT(kb.es.enter_context(kb.nc.psum_tensor(f"psb{i}", [128, 512], F32)), f"psb{i}")
                      for i in range(n)]
        for b in self.banks:
            b.res.excl = True
        self.i = 0

        self.held = set()

    def next(self, hold=False):
        while True:
            b = self.banks[self.i % len(self.banks)]
            self.i += 1
            if id(b) not in self.held:
                break
        if hold:
            self.held.add(id(b))
        return b

    def release(self, b):
        self.held.discard(id(b))


def bfv(bank):
    return V(bank.t[:].bitcast(BF16), bank.res)


class Ring:
    def __init__(self, kb, name, n, shape, dt, es=None, chan=True):
        self.slots = [kb.sb(f"{name}{i}", shape, dt, es) for i in range(n)]
        self.chans = [kb.chan(f"{name}{i}") for i in range(n)] if chan else None
        self.i = 0

    def next(self):
        j = self.i % len(self.slots)
        self.i += 1
        return self.slots[j], (self.chans[j] if self.chans else None)


class Cut(Exception):
    pass


def build_program(n_ptiles=NPT, do_sample=True, do_l1=True, dbg=False, cut=None):
    def chk(n):
        if cut is not None and cut == n:
            raise Cut()

    nc = bass.Bass("TRN2", target_bir_lowering=False)
    es = ExitStack()
    kb = KB(nc, es)

    def din(name, shape, dt=F32):
        return nc.dram_tensor(name, list(shape), dt, kind="ExternalInput").ap()

    def dout(name, shape, dt=F32):
        return nc.dram_tensor(name, list(shape), dt, kind="ExternalOutput").ap()

    xp_d = din("xp", [LP, D])
    xs_d = din("xs", [128, D])
    sg_d = din("sg", [NSEQ, 8, 128, 128])
    sc_d = din("sc", [48, 3072])
    sr_d = din("sr", [NSEQ, 4, 256, 512])
    wia_d = din("wia", [D, 4112])
    woa_d = din("woa", [D, D])
    wib_d = din("wib", [D, 6144])
    wob_d = din("wob", [2048, D])
    cst_d = din("cst", [128, CST_N])
    idb_d = din("idb", [128, 128], BF16)
    fnw_d = din("fnw", [1, D])
    rot_d = din("rot", [NPT + 1, 128, 2, 128])
    dmask_d = din("dmask", [2, 128, 4, 128])

    yp_d = dout("yp", [LP, D])
    ys_d = dout("ys", [128, D])
    sap_d = dout("sap", [8, 128, 128])
    cap_d = dout("cap", [3, 3072])
    sbp_d = dout("sbp", [4, 256, 512])
    sas_d = dout("sas", [NSEQ, 8, 128, 128])
    cas_d = dout("cas", [48, 3072])
    sbs_d = dout("sbs", [NSEQ, 4, 256, 512])
    x1_d = nc.dram_tensor("x1s", [LP + 128, D], F32, kind="Internal").ap()
    x1_res = [Res(f"x1_{i}") for i in range(NPT + 1)]
    wibb_d = nc.dram_tensor("wibb", [D, 6144], BF16, kind="Internal").ap()
    wibb_res = [[Res(f"wibb{k}_{p}") for p in range(3)] for k in range(8)]
    dbg_d = {}
    if dbg:
        for nm, shp in (("d_y1", [128 * (n_ptiles + 1), D]), ("d_mixed", [128, 1536]), ("d_misc", [128, 64]),
                        ("d_dec", [128, 512]), ("d_Y", [128, 512]), ("d_on", [128, 1024])):
            dbg_d[nm] = dout(nm, shp)

    P = PsumPool(kb)
    ch_c = kb.chan("const")
    ch_dbg = kb.chan("dbg")

    cst = kb.sb("cst", [128, CST_N], F32)
    idb = kb.sb("idb_s", [128, 128], BF16)
    kb.dma("sp", cst[:], ext(cst_d), ch_c)
    kb.dma("sp", idb[:], ext(idb_d), ch_c)
    cst.res.w = (ch_c.sem, ch_c.count, "dma")
    idb.res.w = (ch_c.sem, ch_c.count, "dma")

    def C(name, lo=0, hi=None):
        o, n = CST_OFF[name]
        hi = n if hi is None else hi
        return cst[:, o + lo:o + hi]

    neghalf = kb.sb("neghalf", [128, 16], F32)
    kb.memset("dve", neghalf[:], -0.5)
    identf = C("identf")

    def load_masks(s):
        kb.copy("dve", mr["triU"][:], C("triU_" + s))
        kb.ts("dve", mr["negtriU"][:], C("triU_" + s), -1.0, ALU.mult)
        kb.ts("dve", negtriUf[:], C("triU_" + s), -1.0, ALU.mult)
        kb.copy("dve", mr["ones"][:], onesf[:])
        kb.ts("dve", mr["negones"][:], onesf[:], -1.0, ALU.mult)
        kb.copy("dve", mr["ident"][:], identf)
        for nm in ("maskneg", "masknegT"):
            src = C(nm + "_" + s)
            kb.copy("dve", V(mr[nm].t[:].rearrange("p (h j) -> p h j", h=4), mr[nm].res),
                    bc(un(src, 1), [128, 4, 128]))

    xt = kb.sb("xt", [128, D], F32)
    yt = kb.sb("yt", [128, D], F32)
    ch_xt = kb.chan("xt")
    ch_yt = kb.chan("yt")
    xs_b = kb.sb("xs_b", [128, D], BF16)
    xs_b.res.strict = True
    xnT = [kb.sb(f"xnT{i}", [128, 8, 128], BF16) for i in range(2)]
    st4 = kb.sb("st4", [128, 16], F32)
    es0 = ExitStack()
    mr = {}
    for nm in ("triU", "negtriU", "ones", "negones", "ident", "maskneg", "masknegT"):
        w = 512 if nm.startswith("maskneg") else 128
        mr[nm] = kb.sb("mr_" + nm, [128, w], F32R, es0)
    onesf = kb.sb("onesf", [128, 128], F32, es0)
    kb.memset("dve", onesf[:], 1.0)
    negtriUf = kb.sb("negtriUf", [128, 128], F32, es0)


    wia = [kb.sb(f"wia{k}", [128, 4112], BF16, es0) for k in range(8)]
    woa = [kb.sb(f"woa{k}", [128, 1024], BF16, es0) for k in range(8)]
    diag = kb.sb("diag", [128, 96, 128], BF16, es0)
    stg = []
    for i in range(3):
        ap_ = diag.t[:, i * 32:(i + 1) * 32, :].rearrange("p a b -> p (a b)").bitcast(F32)
        stg.append(V(ap_, Res(f"wstg{i}")))
    ch_stg = [kb.chan(f"wstg{i}") for i in range(3)]
    n_st = 0
    for k in range(8):
        for (c0, c1) in ((0, 2048), (2048, 4096), (4096, 4112)):
            sl_ = stg[n_st % 3]
            w_ = c1 - c0
            kb.dma("sp", V(sl_.ap[:, 0:w_], sl_.res), ext(wia_d[k * 128:(k + 1) * 128, c0:c1]), ch_stg[n_st % 3])
            kb.copy("dve" if n_st % 2 == 0 else "act", wia[k][:, c0:c1], V(sl_.ap[:, 0:w_], sl_.res))
            n_st += 1
    for k in range(8):
        st, chn = (xt, ch_xt) if k % 2 == 0 else (yt, ch_yt)
        kb.dma("sp", st[:], ext(woa_d[k * 128:(k + 1) * 128, :]), chn)
        kb.ts("dve", woa[k][:], st[:], C("onwa"), ALU.mult)

    for sl_ in stg:
        if sl_.res.w is not None:
            kb._need(kb.q["dve"], sl_.res.w)
        for ev_ in sl_.res.r.values():
            kb._need(kb.q["dve"], ev_)
    for i in range(96):
        kb.ts("dve", diag[:, i, :], idb[:], C("cwT", i, i + 1), ALU.mult)
    negA = kb.sb("negA", [128, 8], F32, es0)
    kb.act(negA[:], C("alog"), AF.Exp)
    kb.ts("dve", negA[:], negA[:], -1.0, ALU.mult)

    def front_end(src_v, nwname, par):
        kb.dma("sp", xt[:], src_v, ch_xt)
        kb.act(xs_b[:], xt[:], AF.Square, accum=st4[:, 0:1])
        yield
        kb.ts("dve", st4[:, 1:2], st4[:, 0:1], 1.0 / D, ALU.mult, EPS, ALU.add)
        kb.tt("pool", st4[:, 2:3], st4[:, 1:2], neghalf[:, 0:1], ALU.pow)
        yield
        kb.ts("dve", xs_b[:], xt[:], st4[:, 2:3], ALU.mult)
        yield
        bank = P.next()
        pv = bfv(bank)
        for k in range(8):
            kb.tr(V(pv.ap[:, k * 128:(k + 1) * 128], pv.res), xs_b[:, k * 128:(k + 1) * 128], idb[:], inc=(k == 7))
        kb.tt("dve", xnT[par][:], V(pv.ap.rearrange("p (k t) -> p k t", k=8), pv.res),
              bc(un(C(nwname), 2), [128, 8, 128]), ALU.mult)
        yield

    pT = [kb.sb(f"pT{i}", [128, 12, 176], BF16, es0) for i in range(2)]
    hist = [kb.sb(f"hist{i}", [128, 12, 3], BF16, es0) for i in range(2)]
    for h_ in hist:
        kb.memset("pool", h_[:], 0.0)
    mixed = [kb.sb(f"mixed{i}", [128, 1536], BF16, es0) for i in range(2)]
    zs = [kb.sb(f"zs{i}", [128, 512], BF16, es0) for i in range(2)]
    ba = kb.sb("ba", [128, 16], F32, es0)
    sc8 = kb.sb("sc8", [128, 64], F32, es0)
    E = kb.sb("E", [128, 24], F32, es0)
    sqb = kb.sb("sqb", [128, 1024], BF16, es0)
    r8 = kb.sb("r8", [128, 16], F32, es0)
    sv = {nm: kb.sb("sv_" + nm, [128, 4, 128], BF16, es0) for nm in ("qn", "qe", "kn", "kw", "kd", "vb")}
    qT = kb.sb("qT", [128, 8, 128], BF16, es0)
    knT = kb.sb("knT", [128, 4, 128], BF16, es0)
    gm = kb.sb("gm", [128, 4, 128], F32R, es0)
    gb = kb.sb("gb", [128, 4, 128], F32R, es0)
    dec = kb.sb("dec", [128, 4, 128], F32, es0)
    decT = kb.sb("decT", [128, 4, 128], BF16, es0)
    Pb = [kb.sb(f"Pb{i}", [128, 4, 128], BF16, es0) for i in range(2)]
    PTb = [kb.sb(f"PTb{i}", [128, 4, 128], BF16, es0) for i in range(2)]
    Yb = [kb.sb(f"Yb{i}", [128, 4, 128], BF16, es0) for i in range(2)]
    Mb = kb.sb("Mb", [128, 4, 128], BF16, es0)
    MTb = kb.sb("MTb", [128, 4, 128], BF16, es0)
    negidb = kb.sb("negidb", [128, 128], BF16, es0)
    kb.ts("dve", negidb[:], idb[:], -1.0, ALU.mult)
    negWT = kb.sb("negWT", [128, 4, 128], BF16, es0)
    qkdT = kb.sb("qkdT", [128, 4, 128], BF16, es0)
    S = kb.sbs("S", [128, 8, 128], F32, 4, es0)
    Sbf = kb.sbs("Sbf", [128, 8, 128], BF16, 4, es0)
    ub = kb.sb("ub", [128, 4, 128], BF16, es0)
    otmp = kb.sb("otmp", [128, 512], BF16, es0)
    on = kb.sb("on", [128, 1024], BF16, es0)
    onT = kb.sb("onT", [128, 8, 128], BF16, es0)
    ch_out = kb.chan("out_small")
    ch_sbf = [kb.chan("sbf0"), kb.chan("sbf1")]
    ch_sf = [kb.chan("sf0"), kb.chan("sf1")]
    ch_sfo = [kb.chan("sfo0"), kb.chan("sfo1")]
    if do_sample:
        cv = kb.sb("cv", [128, 12, 128], BF16, es0)
        histT = kb.sb("histT", [128, 24, 48], BF16, es0)
        abc = kb.sb("abc", [128, 64], F32, es0)
        gmsk = kb.sb("gmsk", [128, 16, 4], F32, es0)
        uTb = kb.sb("uTb", [128, 4, 128], BF16, es0)

    beta = sc8[:, 0:8]
    negbeta = sc8[:, 8:16]
    gv = sc8[:, 24:32]

    def v3(tt_, h=4):
        return V(tt_.t[:].rearrange("p (h d) -> p h d", h=h), tt_.res)

    def sc_b(vw):
        return bc(un(vw, 2), [128, 4, 128])

    sc8_2 = [sc8, kb.sb("sc8_b", [128, 64], F32, es0)]
    E_2 = [E, kb.sb("E_b", [128, 24], F32, es0)]
    sv_2 = [sv, {nm: kb.sb("svb_" + nm, [128, 4, 128], BF16, es0) for nm in ("qn", "qe", "kn", "kw", "kd", "vb")}]
    qT_2 = [qT, kb.sb("qT_b", [128, 8, 128], BF16, es0)]
    knT_2 = [knT, kb.sb("knT_b", [128, 4, 128], BF16, es0)]
    osq = kb.sb("osq", [128, 512], BF16, es0)

    def gdn_prologue(ti, is_sample):
        sc8 = sc8_2[ti % 2]
        E = E_2[ti % 2]
        beta = sc8[:, 0:8]
        negbeta = sc8[:, 8:16]
        gv = sc8[:, 24:32]
        par = ti % 2
        src = ext(xs_d[:, :]) if is_sample else ext(xp_d[ti * 128:(ti + 1) * 128, :])
        yield from front_end(src, "nwT0", par)
        xn = xnT[par]
        chk(2)
        bk = P.next()
        for k in range(8):
            kb.mm(bk[:, 0:16], xn[:, k, :], wia[k][:, 4096:4112], start=(k == 0), stop=(k == 7))
        kb.copy("dve", ba[:], bk[:, 0:16])
        yield
        kb.act(sc8[:, 56:64], ba[:, 0:8], AF.Tanh, scale=0.5)
        kb.ts("dve", negbeta, sc8[:, 56:64], -0.5, ALU.mult, -0.5, ALU.add)
        kb.ts("dve", beta, sc8[:, 56:64], 0.5, ALU.mult, 0.5, ALU.add)
        yield
        kb.tt("dve", sc8[:, 16:24], ba[:, 8:16], C("dtb"), ALU.add)
        kb.act(sc8[:, 16:24], sc8[:, 16:24], AF.Exp)
        kb.act(sc8[:, 16:24], sc8[:, 16:24], AF.Ln, bias=1.0)
        yield
        kb.tt("dve", gv, sc8[:, 16:24], negA[:], ALU.mult)
        yield
        sfx = "_s" if is_sample else "_p"
        bk = P.next()
        kb.mm(bk[:, 0:8], C("triU" + sfx), gv)
        kb.mm(bk[:, 8:16], C("blk" + sfx), gv)
        kb.mm(bk[:, 16:24], C("triSU" + sfx), gv)
        kb.act(E[:], bk[:, 0:24], AF.Exp)
        chk(3)
        yield

    def gdn_stage1(ti, hg, ii, is_sample):
        par = ti % 2
        xn = xnT[par]
        sc8 = sc8_2[ti % 2]
        E = E_2[ti % 2]
        sv, qT, knT = sv_2[ii % 2], qT_2[ii % 2], knT_2[ii % 2]
        sfx = "_s" if is_sample else "_p"
        h0 = hg * 4
        pt = pT[hg]
        chunks = [h0 + i for i in range(4)] + [8 + h0 + i for i in range(4)] + [16 + h0 + i for i in range(4)]
        if is_sample:
            F4 = V(pt.t[:].rearrange("p c (s r) -> p c s r", r=11), pt.res)
            hT4 = V(histT.t[:].rearrange("p c (s r) -> p c s r", r=3), histT.res)
        else:
            kb.copy("pool", pt[:, :, 0:3], hist[hg][:])
        for grp in range(3):
            bk = P.next()
            for ci in range(4):
                col = chunks[grp * 4 + ci] * 128
                for k in range(8):
                    kb.mm(bk[:, ci * 128:(ci + 1) * 128], wia[k][:, col:col + 128], xn[:, k, :],
                          start=(k == 0), stop=(k == 7), inc=(k == 7 and ci == 3))
            eng = "act" if grp % 2 == 0 else "dve"
            if is_sample:
                c0 = chunks[grp * 4]
                kb.copy(eng, V(F4.ap[:, grp * 4:(grp + 1) * 4, :, 3:11], pt.res),
                        V(bk.t[:].rearrange("p (c s t) -> p c s t", c=4, s=16), bk.res))
                kb.copy("pool", V(F4.ap[:, grp * 4:(grp + 1) * 4, :, 0:3], pt.res),
                        V(hT4.ap[:, c0:c0 + 4, :, :], histT.res))
            else:
                kb.copy(eng, pt[:, grp * 4:(grp + 1) * 4, 3:131], v3(bk))
                yield
        if not is_sample:
            kb.copy("pool", hist[hg][:], pt[:, :, 128:131])
        bk = P.next()
        for k in range(8):
            kb.mm(bk[:], xn[:, k, :], wia[k][:, 3072 + hg * 512:3072 + (hg + 1) * 512],
                  start=(k == 0), stop=(k == 7))
        kb.act(zs[hg][:], bk[:], AF.Silu)
        yield
        mx = mixed[hg]
        if is_sample:
            for c in range(12):
                cg = chunks[c]
                cvv = V(cv.t[:, c, :].rearrange("p (s t) -> p s t", t=8), cv.res)
                kb.ts("dve", cvv, V(F4.ap[:, c, :, 0:8], pt.res), C("cwT", cg * 4, cg * 4 + 1), ALU.mult)
                for j in range(1, 4):
                    kb.stt(cvv, V(F4.ap[:, c, :, j:j + 8], pt.res), C("cwT", cg * 4 + j, cg * 4 + j + 1),
                           cvv, ALU.mult, ALU.add)
        for grp in range(3):
            bk = P.next()
            for ci in range(4):
                cg = chunks[grp * 4 + ci]
                if is_sample:
                    kb.mm(bk[:, ci * 128:(ci + 1) * 128], cv[:, grp * 4 + ci, :], idb[:], inc=(ci == 3))
                else:
                    for j in range(4):
                        kb.mm(bk[:, ci * 128:(ci + 1) * 128], pt[:, grp * 4 + ci, j:j + 128],
                              diag[:, cg * 4 + j, :], start=(j == 0), stop=(j == 3), inc=(j == 3 and ci == 3))
            kb.act(mx[:, grp * 512:(grp + 1) * 512], bk[:], AF.Silu)
            yield
        chk(4)
        kb.act(sqb[:], mx[:, 0:1024], AF.Square)
        kb.reduce_sum(r8[:, 0:8], v3(sqb, 8))
        yield
        kb.ts("dve", r8[:, 0:8], r8[:, 0:8], EPS, ALU.add)
        kb.tt("pool", r8[:, 8:16], r8[:, 0:8], neghalf[:, 0:8], ALU.pow)
        yield
        rq = r8[:, 8:12]
        rk = r8[:, 12:16]
        eG = E[:, h0:h0 + 4]
        ekl = E[:, 16 + h0:16 + h0 + 4]
        kb.ts("dve", sc8[:, 32:36], rq, 128 ** -0.5, ALU.mult)
        kb.tt("dve", sc8[:, 36:40], sc8[:, 32:36], eG, ALU.mult)
        kb.tt("dve", sc8[:, 40:44], rk, eG, ALU.mult)
        kb.tt("dve", sc8[:, 40:44], sc8[:, 40:44], sc8[:, h0:h0 + 4], ALU.mult)
        kb.tt("dve", sc8[:, 44:48], rk, ekl, ALU.mult)
        yield
        qv = V(mx.t[:, 0:512].rearrange("p (h d) -> p h d", h=4), mx.res)
        kv = V(mx.t[:, 512:1024].rearrange("p (h d) -> p h d", h=4), mx.res)
        vv = V(mx.t[:, 1024:1536].rearrange("p (h d) -> p h d", h=4), mx.res)
        kb.tt("dve", sv["qn"][:], qv, sc_b(sc8[:, 32:36]), ALU.mult)
        kb.tt("pool", sv["qe"][:], qv, sc_b(sc8[:, 36:40]), ALU.mult)
        yield
        kb.tt("dve", sv["kn"][:], kv, sc_b(rk), ALU.mult)
        kb.tt("pool", sv["kw"][:], kv, sc_b(sc8[:, 40:44]), ALU.mult)
        yield
        kb.tt("dve", sv["kd"][:], kv, sc_b(sc8[:, 44:48]), ALU.mult)
        kb.tt("pool", sv["vb"][:], vv, sc_b(sc8[:, h0:h0 + 4]), ALU.mult)
        yield
        bk = P.next()
        pv = bfv(bk)
        for i, nm in enumerate(("qn", "qe")):
            for h in range(4):
                c0 = (i * 4 + h) * 128
                kb.tr(V(pv.ap[:, c0:c0 + 128], pv.res), sv[nm][:, h, :], idb[:], inc=(i == 1 and h == 3))
        kb.copy("act", qT[:], V(pv.ap.rearrange("p (k t) -> p k t", k=8), pv.res))
        yield
        bk = P.next()
        pv = bfv(bk)
        for h in range(4):
            kb.tr(V(pv.ap[:, h * 128:(h + 1) * 128], pv.res), sv["kn"][:, h, :], idb[:], inc=(h == 3))
        kb.copy("dve", knT[:], V(pv.ap[:, 0:512].rearrange("p (k t) -> p k t", k=4), pv.res))

    def gdn_stage2(ti, hg, ii, is_sample):
        if is_sample:
            sample_prefetch(hg)
        sc8 = sc8_2[ti % 2]
        E = E_2[ti % 2]
        sv, qT, knT = sv_2[ii % 2], qT_2[ii % 2], knT_2[ii % 2]
        sfx = "_s" if is_sample else "_p"
        h0 = hg * 4
        mx = mixed[hg]
        chk(5)
        kb.tt("dve", gm[:], bc(un(sc8[:, 24 + h0:24 + h0 + 4], 2), [128, 4, 128]),
              bc(un(negtriUf[:], 1), [128, 4, 128]), ALU.mult)
        kb.copy("act", gb[:], bc(un(sc8[:, 24 + h0:24 + h0 + 4], 2), [128, 4, 128]))
        yield
        gmf = V(gm.t[:].rearrange("p h j -> p (h j)"), gm.res)
        gbf = V(gb.t[:].rearrange("p h j -> p (h j)"), gb.res)
        chk(51)
        bk = P.next()
        kb.mm(bk[:], mr["triU"][:], gbf, start=True, stop=False)
        kb.mm(bk[:], mr["ones"][:], gmf, start=False, stop=False)
        kb.mm(bk[:], mr["ident"][:], mr["maskneg"][:], start=False, stop=True)
        chk(52)
        kb.act(V(dec.t[:].rearrange("p h j -> p (h j)"), dec.res), bk[:], AF.Exp)
        yield
        chk(53)
        bk = P.next()
        kb.mm(bk[:], mr["negtriU"][:], gbf, start=True, stop=False)
        kb.mm(bk[:], mr["negones"][:], gmf, start=False, stop=False)
        kb.mm(bk[:], mr["ident"][:], mr["masknegT"][:], start=False, stop=True)
        chk(54)
        kb.act(V(decT.t[:].rearrange("p h j -> p (h j)"), decT.res), bk[:], AF.Exp)
        yield
        chk(55)
        chk(6)
        bk = P.next()
        for h in range(4):
            kb.mm(bk[:, h * 128:(h + 1) * 128], knT[:, h, :], knT[:, h, :], inc=(h == 3))
        kb.tt("dve", dec[:], dec[:], bc(un(C("strict" + sfx), 1), [128, 4, 128]), ALU.mult)
        kb.tt("dve", dec[:], dec[:], sc_b(sc8[:, 8 + h0:8 + h0 + 4]), ALU.mult)
        yield
        Pc, PTc, Yc = Mb, MTb, Yb[0]
        kb.tt("dve", Pc[:], v3(bk), dec[:], ALU.mult)
        yield
        chk(61)
        bk = P.next()
        pv = bfv(bk)
        for h in range(4):
            kb.tr(V(pv.ap[:, h * 128:(h + 1) * 128], pv.res), Pc[:, h, :], idb[:], inc=(h == 3))
        pv4 = V(pv.ap[:, 0:512].rearrange("p (k t) -> p k t", k=4), pv.res)
        kb.copy("act", PTc[:], pv4)
        kb.tt("dve", Yc[:], pv4, bc(un(idb[:], 1), [128, 4, 128]), ALU.add)
        yield
        chk(62)
        nsteps = 3 if is_sample else 6
        for stp in range(1, nsteps):
            Pn, PTn, Yn = Pb[stp % 2], PTb[stp % 2], Yb[stp % 2]
            bkA = P.next()
            for h in range(4):
                kb.mm(bkA[:, h * 128:(h + 1) * 128], PTc[:, h, :], Pc[:, h, :], inc=(h == 3))
            last = (stp == nsteps - 1)
            if not last:
                bkB = P.next()
                for h in range(4):
                    kb.mm(bkB[:, h * 128:(h + 1) * 128], Pc[:, h, :], PTc[:, h, :], inc=(h == 3))
            kb.copy("act", Pn[:], v3(bkA))
            if not last:
                kb.copy("dve", PTn[:], v3(bkB))
                yield
            bkC = P.next()
            for h in range(4):
                kb.mm(bkC[:, h * 128:(h + 1) * 128], Pn[:, h, :], Yc[:, h, :], inc=(h == 3))
            kb.tt("dve", Yn[:], v3(bkC), Yc[:], ALU.add)
            yield
            Pc, PTc, Yc = Pn, PTn, Yn
        bk = P.next()
        pv = bfv(bk)
        for h in range(4):
            kb.tr(V(pv.ap[:, h * 128:(h + 1) * 128], pv.res), Yc[:, h, :], idb[:], inc=(h == 3))
        X0b, Rb = PTb[0], Pb[0]
        kb.copy("act", X0b[:], V(pv.ap[:, 0:512].rearrange("p (k t) -> p k t", k=4), pv.res))
        yield
        bk = P.next()
        for h in range(4):
            kb.mm(bk[:, h * 128:(h + 1) * 128], MTb[:, h, :], X0b[:, h, :], start=True, stop=False)
            kb.mm(bk[:, h * 128:(h + 1) * 128], negidb[:], X0b[:, h, :], start=False, stop=True, inc=(h == 3))
        kb.tt("dve", Rb[:], v3(bk), bc(un(idb[:], 1), [128, 4, 128]), ALU.add)
        yield
        bk = P.next()
        for h in range(4):
            kb.mm(bk[:, h * 128:(h + 1) * 128], Rb[:, h, :], Yc[:, h, :], inc=(h == 3))
        Yn = Yb[1] if Yc is Yb[0] else Yb[0]
        kb.tt("dve", Yn[:], v3(bk), Yc[:], ALU.add)
        Yc = Yn
        chk(63)
        bk = P.next()
        for h in range(4):
            kb.mm(bk[:, h * 128:(h + 1) * 128], sv["kw"][:, h, :], Yc[:, h, :], inc=(h == 3))
        kb.act(negWT[:], v3(bk), AF.Copy, scale=-1.0)
        yield
        bk = P.next()
        for h in range(4):
            kb.mm(bk[:, h * 128:(h + 1) * 128], knT[:, h, :], qT[:, h, :], inc=(h == 3))
        kb.tt("dve", qkdT[:], v3(bk), decT[:], ALU.mult)
        yield
        if dbg and ti == dbg_tile and hg == 0:
            kb.dma("sp", ext(dbg_d["d_misc"][:, 0:64]), sc8[:], ch_dbg)
            kb.dma("sp", ext(dbg_d["d_dec"]), V(dec.t[:].rearrange("p h j -> p (h j)"), dec.res), ch_dbg)
        chk(7)
        if not is_sample:
            first = (ti == 0)
            bu = P.next()
            for h in range(4):
                kb.mm(bu[:, h * 128:(h + 1) * 128], Yc[:, h, :], sv["vb"][:, h, :], start=True, stop=first,
                      inc=(first and h == 3))
                if not first:
                    kb.mm(bu[:, h * 128:(h + 1) * 128], negWT[:, h, :], Sbf[:, h0 + h, :], start=False,
                          stop=True, inc=(h == 3))
            kb.copy("act", ub[:], v3(bu))
            yield
            bo = P.next()
            for h in range(4):
                if not first:
                    kb.mm(bo[:, h * 128:(h + 1) * 128], qT[:, 4 + h, :], Sbf[:, h0 + h, :], start=True,
                          stop=False)
                kb.mm(bo[:, h * 128:(h + 1) * 128], qkdT[:, h, :], ub[:, h, :], start=first, stop=True,
                      inc=(h == 3))
            bs = P.next()
            for h in range(4):
                kb.mm(bs[:, h * 128:(h + 1) * 128], sv["kd"][:, h, :], ub[:, h, :], inc=(h == 3))
            for h in range(4):
                if first:
                    kb.copy("dve", S[:, h0 + h, :], bs[:, h * 128:(h + 1) * 128])
                else:
                    kb.stt(S[:, h0 + h, :], S[:, h0 + h, :], E[:, 8 + h0 + h:8 + h0 + h + 1],
                           bs[:, h * 128:(h + 1) * 128], ALU.mult, ALU.add)
            kb.copy("act", Sbf[:, h0:h0 + 4, :], S[:, h0:h0 + 4, :])
            yield
            o_src = bo[:]
        else:
            o_src = gdn_sample_rec(hg, Yc, sv, qT, sc8)
        chk(8)
        o3 = V(o_src.ap.rearrange("p (h d) -> p h d", h=4), o_src.res)
        kb.act(osq[:], o_src, AF.Square)
        kb.reduce_sum(sc8[:, 48:52], V(osq.t[:].rearrange("p (h d) -> p h d", h=4), osq.res))
        yield
        kb.ts("dve", sc8[:, 48:52], sc8[:, 48:52], 1.0 / 128, ALU.mult, EPS, ALU.add)
        kb.tt("pool", sc8[:, 52:56], sc8[:, 48:52], neghalf[:, 0:4], ALU.pow)
        yield
        kb.tt("dve", v3(otmp), o3, sc_b(sc8[:, 52:56]), ALU.mult)
        kb.tt("dve", on[:, hg * 512:(hg + 1) * 512], otmp[:], zs[hg][:], ALU.mult)
        if dbg and ti == dbg_tile and hg == 0:
            kb.dma("pool", ext(dbg_d["d_mixed"]), mx[:], ch_dbg)
            kb.dma("pool", ext(dbg_d["d_Y"]), otmp[:], ch_dbg)

    def gdn_epilogue(ti, is_sample):
        par = ti % 2
        xn = xnT[par]
        src = ext(xs_d[:, :]) if is_sample else ext(xp_d[ti * 128:(ti + 1) * 128, :])
        kb.dma("sp", yt[:], src, ch_yt)
        chk(9)
        bk = P.next()
        pv = bfv(bk)
        for k in range(8):
            kb.tr(V(pv.ap[:, k * 128:(k + 1) * 128], pv.res), on[:, k * 128:(k + 1) * 128], idb[:], inc=(k == 7))
        kb.copy("act", onT[:], V(pv.ap.rearrange("p (k t) -> p k t", k=8), pv.res))
        yield
        for n in range(2):
            bk = P.next()
            for k in range(8):
                kb.mm(bk[:], onT[:, k, :], woa[k][:, n * 512:(n + 1) * 512], start=(k == 0), stop=(k == 7))
            kb.tt("dve", yt[:, n * 512:(n + 1) * 512], bk[:], yt[:, n * 512:(n + 1) * 512], ALU.add)
            yield
        kb.dma("sp", V(x1_d[ti * 128:(ti + 1) * 128, :], x1_res[ti]), yt[:], ch_yt)
        if dbg:
            dr = n_ptiles if is_sample else ti
            kb.dma("sp", ext(dbg_d["d_y1"][dr * 128:(dr + 1) * 128, :]), yt[:], ch_yt)
        if is_sample or ti == n_ptiles - 1:
            for cb in range(3):
                for half in range(2):
                    bk = P.next()
                    col = cb * 1024 + half * 512
                    for k in range(8):
                        kb.mm(bk[:], xn[:, k, :], wia[k][:, col:col + 512], start=(k == 0), stop=(k == 7))
                    kb.copy("act" if half else "dve", xt[:, half * 512:(half + 1) * 512], bk[:])
                    yield
                if is_sample:
                    for s_ in range(NSEQ):
                        kb.dma("sp", ext(cas_d[s_ * 3:(s_ + 1) * 3, cb * 1024:(cb + 1) * 1024]),
                               xt[s_ * 8 + 5:s_ * 8 + 8, :], ch_xt)
                else:
                    kb.dma("sp", ext(cap_d[:, cb * 1024:(cb + 1) * 1024]), xt[125:128, :], ch_xt)
        if (not is_sample) and ti == n_ptiles - 1:
            kb.dma("sp", ext(sap_d.rearrange("h k v -> k h v")), S[:], ch_out)
        yield


    smpst = {}

    def vi(v, *idx):
        return V(v.ap[idx], v.res)

    def sample_slots():
        if smpst:
            return smpst
        bsl, fsl = [], []
        for i in range(16):
            r = Res(f"smb{i}")
            r.w = diag.res.w
            r.r = dict(diag.res.r)
            bsl.append(V(diag.t[:, i * 4:(i + 1) * 4, :], r))
        for i in range(4):
            r = Res(f"smf{i}")
            r.w = diag.res.w
            r.r = dict(diag.res.r)
            ap = diag.t[:, 64 + i * 8:64 + (i + 1) * 8, :].rearrange("p a b -> p (a b)").bitcast(F32)
            fsl.append(V(ap.rearrange("p (h d) -> p h d", h=4), r))
        smpst["b"] = bsl
        smpst["f"] = fsl
        smpst["chb"] = [kb.chan(f"smb{i}") for i in range(16)]
        smpst["chf"] = [kb.chan(f"smf{i}") for i in range(4)]
        smpst["chfo"] = [kb.chan(f"smfo{i}") for i in range(4)]
        return smpst

    def sample_ld_f(hg, q_):
        if q_ >= NSEQ:
            return
        sm = sample_slots()
        h0 = hg * 4
        kb.dma("sp", sm["f"][q_ % 4], ext(sg_d[q_, h0:h0 + 4].rearrange("h k v -> k h v")), sm["chf"][q_ % 4])

    def sample_prefetch(hg):
        sm = sample_slots()
        h0 = hg * 4
        for q_ in range(NSEQ):
            kb.dma("pool", sm["b"][q_], ext(sg_d[q_, h0:h0 + 4].rearrange("h k v -> k h v")), sm["chb"][q_])
        for q_ in range(3):
            sample_ld_f(hg, q_)

    def gdn_sample_rec(hg, Yc, sv, qT, sc8):
        sm = sample_slots()
        h0 = hg * 4
        kb.tt("dve", gmsk[:], bc(un(sc8[:, 24 + h0:24 + h0 + 4], 1), [128, 16, 4]),
              bc(un(C("seqmask"), 2), [128, 16, 4]), ALU.mult)
        bk = P.next()
        kb.mm(bk[:, 0:64], onesf[:], V(gmsk.t[:].rearrange("p s h -> p (s h)"), gmsk.res))
        kb.act(abc[:], bk[:, 0:64], AF.Exp)
        buT = P.next(hold=True)
        for h in range(4):
            kb.mm(buT[:, h * 128:(h + 1) * 128], sv["vb"][:, h, :], Yc[:, h, :], start=(h == 0), stop=False,
                  inc=False)
        for s_ in range(NSEQ):
            for h in range(4):
                lastmm = (s_ == NSEQ - 1 and h == 3)
                kb.mm(buT[:, h * 128 + s_ * 8:h * 128 + s_ * 8 + 8], vi(sm["b"][s_], slice(None), h, slice(None)),
                      negWT[:, h, s_ * 8:s_ * 8 + 8], start=False, stop=lastmm, inc=(h == 3))
        kb.copy("act", uTb[:], v3(buT))
        P.release(buT)
        bk = P.next()
        pv = bfv(bk)
        for h in range(4):
            kb.tr(V(pv.ap[:, h * 128:(h + 1) * 128], pv.res), uTb[:, h, :], idb[:], inc=(h == 3))
        kb.copy("dve", ub[:], V(pv.ap[:, 0:512].rearrange("p (k t) -> p k t", k=4), pv.res))
        boT = P.next(hold=True)
        for h in range(4):
            kb.mm(boT[:, h * 128:(h + 1) * 128], ub[:, h, :], qkdT[:, h, :], start=(h == 0), stop=False, inc=False)
        for s_ in range(NSEQ):
            sample_ld_f(hg, s_ + 3)
            fs_ = sm["f"][s_ % 4]
            for h in range(4):
                lastmm = (s_ == NSEQ - 1 and h == 3)
                kb.mm(boT[:, h * 128 + s_ * 8:h * 128 + s_ * 8 + 8], vi(sm["b"][s_], slice(None), h, slice(None)),
                      qT[:, 4 + h, s_ * 8:s_ * 8 + 8], start=False, stop=lastmm, inc=(h == 3))
            kdm = Pb[s_ % 2]
            kb.ts("dve", kdm[:], sv["kd"][:], C("seqmask", s_, s_ + 1), ALU.mult)
            bs = P.next()
            for h in range(4):
                kb.mm(bs[:, h * 128:(h + 1) * 128], kdm[:, h, :], ub[:, h, :], inc=(h == 3))
            for h in range(4):
                fh = vi(fs_, slice(None), h, slice(None))
                kb.stt(fh, fh, abc[:, s_ * 4 + h:s_ * 4 + h + 1], bs[:, h * 128:(h + 1) * 128], ALU.mult, ALU.add)
            kb.dma("act", ext(sas_d[s_, h0:h0 + 4].rearrange("h k v -> k h v")), fs_, sm["chfo"][s_ % 4])
        kb.copy("act", uTb[:], v3(boT))
        P.release(boT)
        bk = P.next()
        pv = bfv(bk)
        for h in range(4):
            kb.tr(V(pv.ap[:, h * 128:(h + 1) * 128], pv.res), uTb[:, h, :], idb[:], inc=(h == 3))
        return V(pv.ap[:, 0:512], pv.res)

    def sample_hist_setup():
        for piece in range(3):
            kb.dma("sp", yt[0:48, :], ext(sc_d[:, piece * 1024:(piece + 1) * 1024]), ch_yt)
            bk = P.next()
            for c in range(8):
                kb.tr(bk[:, c * 48:(c + 1) * 48], yt[0:48, c * 128:(c + 1) * 128], V(identf.ap[0:48, 0:48], identf.res),
                      inc=(c == 7))
            kb.copy("dve", histT[:, piece * 8:(piece + 1) * 8, :],
                    V(bk.t[:, 0:384].rearrange("p (c r) -> p c r", c=8), bk.res))

    dbg_tile = 1 if n_ptiles > 1 else 0
    try:
        load_masks("p")
        chk(1)
        if do_sample:
            sample_hist_setup()
        tiles = [(ti, False) for ti in range(n_ptiles)] + ([(NPT, True)] if do_sample else [])
        items = [(ti, hg, smp) for (ti, smp) in tiles for hg in range(2)]
        for _ in gdn_prologue(tiles[0][0], tiles[0][1]):
            pass
        masks_s = False
        def drive(gens):
            gens = [g_ for g_ in gens if g_ is not None]
            while gens:
                for g_ in list(gens):
                    try:
                        next(g_)
                    except StopIteration:
                        gens.remove(g_)
        carry = None
        ch_wc = kb.chan("wcast")
        cast_k = [0]

        def cast_chunk():
            k_ = cast_k[0]
            if k_ >= 8:
                return
            cast_k[0] += 1
            for p_, (c0, c1) in enumerate(((0, 2048), (2048, 4096), (4096, 6144))):
                kb.dma("pool", V(wibb_d[k_ * 128:(k_ + 1) * 128, c0:c1], wibb_res[k_][p_]),
                       ext(wib_d[k_ * 128:(k_ + 1) * 128, c0:c1]), ch_wc)
        for ii in range(len(items) + 1):
            if ii >= 2:
                cast_chunk()
            gens = []
            if carry is not None:
                gens.append(carry)
                carry = None
            hg2 = None
            if ii >= 1:
                ti2, hg2, smp2 = items[ii - 1]
                if smp2 and not masks_s:
                    load_masks("s")
                    masks_s = True
                gens.append(gdn_stage2(ti2, hg2, ii - 1, smp2))
            if ii < len(items):
                ti, hg, smp_ = items[ii]
                gens.append(gdn_stage1(ti, hg, ii, smp_))
                if hg == 1:
                    nxt = [tt_ for tt_ in tiles if tt_[0] > ti]
                    if nxt:
                        gens.append(gdn_prologue(nxt[0][0], nxt[0][1]))
            drive(gens)
            if hg2 == 1:
                carry = gdn_epilogue(ti2, smp2)
        if carry is not None:
            drive([carry])
        while cast_k[0] < 8:
            cast_chunk()
        for rk_ in wibb_res:
            for r_ in rk_:
                r_.w = (ch_wc.sem, ch_wc.count, "dma")
    except Cut:
        pass

    if not do_l1:
        kb.final_wait()
        es0.close()
        return nc

    kb.barrier()
    es0.close()
    es1 = ExitStack()
    ch_w1 = kb.chan("w1")
    wib = [kb.sb(f"wib{k}", [128, 6144], BF16, es1) for k in range(8)]
    wob = [kb.sb(f"wob{k}", [128, 1024], BF16, es1) for k in range(16)]
    for k in range(16):
        st, chn = (xt, ch_xt) if k % 2 == 0 else (yt, ch_yt)
        kb.dma("sp", st[:], ext(wob_d[k * 128:(k + 1) * 128, :]), chn)
        kb.ts("dve", wob[k][:], st[:], C("onwbT", k, k + 1), ALU.mult)
    for k in range(8):
        qn_ = "act"
        kb.dma(qn_, wib[k][:], V(wibb_d[k * 128:(k + 1) * 128, :], tuple(wibb_res[k])), ch_w1)
    for k in range(8):
        wib[k].res.w = (ch_w1.sem, ch_w1.count, "dma")
    fnw = kb.sb("fnw", [128, D], F32, es1)
    ch_fnw = kb.chan("fnw")
    ch_dm = kb.chan("dm")
    ch_dms = kb.chan("dms")
    kb.dma("sp", fnw[:], ext(fnw_d[0:1, :].broadcast_to([128, D])), ch_fnw)
    dm = kb.sb("dm", [128, 4, 128], F32, es1)
    rot = kb.sb("rot", [128, 2, 128], F32, es1)
    ch_rot = kb.chan("rot")
    tmpA = kb.sb("tmpA", [128, 2, 128], F32, es1)
    tmpB = kb.sb("tmpB", [128, 2, 128], F32, es1)
    tmpC = kb.sb("tmpC", [128, 2, 128], F32, es1)
    tmpD = kb.sb("tmpD", [128, 2, 128], F32, es1)
    qkr = kb.sb("qkr", [128, 2, 2, 128], BF16, es1)
    qd = kb.sb("qd", [128, 256], BF16, es1)
    kdd = kb.sb("kdd", [128, 256], BF16, es1)
    qkT = kb.sb("qkT", [128, 6, 128], BF16, es1)
    vb1 = kb.sb("vb1", [128, 512], BF16, es1)
    gs1 = kb.sb("gs1", [128, 512], BF16, es1)
    qkd1 = kb.sb("qkd1", [128, 128], BF16, es1)
    S1 = kb.sbs("S1", [128, 4, 2, 512], F32, 1, es1)
    S1b = kb.sbs("S1b", [128, 4, 2, 512], BF16, 1, es1)
    on1 = kb.sb("on1", [128, 2048], BF16, es1)
    on1T = kb.sb("on1T", [128, 16, 128], BF16, es1)
    st1 = kb.sb("st1", [128, 8], F32, es1)
    zb = [kb.sb(f"zb{i}", [128, 2, 128], BF16, es1) for i in range(2)]
    kddm = [kb.sb(f"kddm{i}", [128, 256], BF16, es1) for i in range(2)]
    for z_ in zb:
        kb.memset("pool", z_[:], 0.0)
    ch_s1 = [kb.chan(f"s1_{i}") for i in range(4)]
    ch_s1o = [kb.chan(f"s1o_{i}") for i in range(4)]
    gam = [1.0 - 2.0 ** (-5.0 - h) for h in range(4)]

    kdd2 = [kdd, kb.sb("kdd_b", [128, 256], BF16, es1)]
    qkT2 = [qkT, kb.sb("qkT_b", [128, 6, 128], BF16, es1)]
    vb12 = [vb1, kb.sb("vb1_b", [128, 512], BF16, es1)]
    gs12 = [gs1, kb.sb("gs1_b", [128, 512], BF16, es1)]
    rot2 = [rot, kb.sb("rot_b", [128, 2, 128], F32, es1)]
    ch_rot2 = [ch_rot, kb.chan("rot_b")]
    junk1 = kb.sb("junk1", [128, 512], BF16, es1)

    def ret_prologue(ti, is_sample):
        src = V(x1_d[ti * 128:(ti + 1) * 128, :], x1_res[ti])
        yield from front_end(src, "nwT1", ti % 2)
        kb.dma("sp", rot2[ti % 2][:], ext(rot_d[ti]), ch_rot2[ti % 2])
        yield

    def ret_stage1(ti, h, ii, is_sample):
        xn = xnT[ti % 2]
        rt = rot2[ti % 2]
        kdd_, qkT_, vb_, gs_ = kdd2[ii % 2], qkT2[ii % 2], vb12[ii % 2], gs12[ii % 2]
        rd = "rdec_s" if is_sample else "rdec_p"
        bk = P.next()
        for part, c0 in ((0, h * 256), (1, 1024 + h * 256)):
            for k in range(8):
                kb.mm(bk[:, part * 256:(part + 1) * 256], xn[:, k, :], wib[k][:, c0:c0 + 256],
                      start=(k == 0), stop=(k == 7), inc=(k == 7 and part == 1))
        pq = V(bk.t[:].rearrange("p (a b d) -> p a b d", a=2, b=2), bk.res)
        x1v = V(pq.ap[:, :, 0, :], bk.res)
        x2v = V(pq.ap[:, :, 1, :], bk.res)
        cosb = bc(un(rt[:, 0, :], 1), [128, 2, 128])
        sinb = bc(un(rt[:, 1, :], 1), [128, 2, 128])
        kb.tt("dve", tmpA[:], x1v, cosb, ALU.mult)
        kb.tt("dve", tmpB[:], x2v, sinb, ALU.mult)
        kb.tt("dve", V(qkr.t[:, :, 0, :], qkr.res), tmpA[:], tmpB[:], ALU.subtract)
        yield
        kb.tt("dve", tmpC[:], x1v, sinb, ALU.mult)
        kb.tt("dve", tmpD[:], x2v, cosb, ALU.mult)
        kb.tt("dve", V(qkr.t[:, :, 1, :], qkr.res), tmpC[:], tmpD[:], ALU.add)
        qr = V(qkr.t[:, 0, :, :].rearrange("p b d -> p (b d)"), qkr.res)
        kr = V(qkr.t[:, 1, :, :].rearrange("p b d -> p (b d)"), qkr.res)
        kb.ts("dve", qd[:], qr, C(rd, h, h + 1), ALU.mult)
        kb.ts("dve", kdd_[:], kr, C(rd, 4 + h, 5 + h), ALU.mult)
        yield
        bk = P.next()
        for k in range(8):
            kb.mm(bk[:], xn[:, k, :], wib[k][:, 2048 + h * 512:2048 + (h + 1) * 512], start=(k == 0), stop=(k == 7))
        kb.copy("act", vb_[:], bk[:])
        yield
        bk = P.next()
        for k in range(8):
            kb.mm(bk[:], xn[:, k, :], wib[k][:, 4096 + h * 512:4096 + (h + 1) * 512], start=(k == 0), stop=(k == 7))
        kb.act(gs_[:], bk[:], AF.Silu)
        yield
        bk = P.next()
        pv = bfv(bk)
        srcs = [qkr[:, 0, 0, :], qkr[:, 0, 1, :], qkr[:, 1, 0, :], qkr[:, 1, 1, :], qd[:, 0:128], qd[:, 128:256]]
        for i_, sv_ in enumerate(srcs):
            kb.tr(V(pv.ap[:, i_ * 128:(i_ + 1) * 128], pv.res), sv_, idb[:], inc=(i_ == 5))
        kb.copy("act", qkT_[:], V(pv.ap[:, 0:768].rearrange("p (k t) -> p k t", k=6), pv.res))
        yield

    def ret_stage2(ti, h, ii, is_sample, n_tiles_first):
        kdd_, qkT_, vb_, gs_ = kdd2[ii % 2], qkT2[ii % 2], vb12[ii % 2], gs12[ii % 2]
        first = (ti == 0)
        dmk = dm
        bk = P.next()
        kb.mm(bk[:, 0:128], qkT_[:, 2, :], qkT_[:, 0, :], start=True, stop=False)
        kb.mm(bk[:, 0:128], qkT_[:, 3, :], qkT_[:, 1, :], start=False, stop=True)
        kb.tt("dve", qkd1[:], bk[:, 0:128], dmk[:, h, :], ALU.mult)
        yield
        bo = P.next(hold=True)
        if not is_sample:
            kb.mm(bo[:], qkd1[:], vb_[:], start=True, stop=first)
            if not first:
                kb.mm(bo[:], qkT_[:, 4, :], S1b[:, h, 0, :], start=False, stop=False)
                kb.mm(bo[:], qkT_[:, 5, :], S1b[:, h, 1, :], start=False, stop=True)
            for c in range(2):
                bs = P.next()
                kb.mm(bs[:], kdd_[:, c * 128:(c + 1) * 128], vb_[:])
                if first:
                    kb.copy("dve", S1[:, h, c, :], bs[:])
                else:
                    kb.stt(S1[:, h, c, :], S1[:, h, c, :], float(gam[h] ** 128), bs[:], ALU.mult, ALU.add)
            kb.copy("act", S1b[:, h, :, :], S1[:, h, :, :])
            yield
        else:
            kb.mm(bo[:], qkd1[:], vb_[:], start=True, stop=False, inc=False)

            def ld1(pi):
                if pi >= 4 * NSEQ:
                    return
                hh, ss = divmod(pi, NSEQ)
                kb.dma("sp", S1[:, pi % 4, :, :], ext(sr_d[ss, hh].rearrange("(c p) v -> p c v", p=128)),
                       ch_s1[pi % 4])
            def cast1(pi_):
                if pi_ < 4 * NSEQ:
                    kb.copy("act", S1b[:, pi_ % 4, :, :], S1[:, pi_ % 4, :, :])
            if h == 0:
                ld1(0)
                ld1(1)
                cast1(0)
            for s_ in range(NSEQ):
                pi = h * NSEQ + s_
                sl = pi % 4
                ld1(pi + 2)
                z_ = zb[s_ % 2]
                kb.copy("dve", z_[:, :, s_ * 8:s_ * 8 + 8], qkT_[:, 4:6, s_ * 8:s_ * 8 + 8])
                kb.mm(bo[:], z_[:, 0, :], S1b[:, sl, 0, :], start=False, stop=False, inc=False)
                kb.mm(bo[:], z_[:, 1, :], S1b[:, sl, 1, :], start=False, stop=(s_ == NSEQ - 1), inc=True)
                kb.memset("dve", z_[:, :, s_ * 8:s_ * 8 + 8], 0.0)
                km = kddm[s_ % 2]
                kb.ts("dve", km[:], kdd_[:], C("seqmask", s_, s_ + 1), ALU.mult)
                bss = []
                for c in range(2):
                    bs = P.next()
                    kb.mm(bs[:], km[:, c * 128:(c + 1) * 128], vb_[:])
                    bss.append(bs)
                cast1(pi + 1)
                for c in range(2):
                    kb.stt(S1[:, sl, c, :], S1[:, sl, c, :], float(gam[h] ** 8), bss[c][:], ALU.mult, ALU.add)
                kb.dma("act", ext(sbs_d[s_, h].rearrange("(c p) v -> p c v", p=128)), S1[:, sl, :, :], ch_s1o[sl])
                yield
        kb.act(junk1[:], bo[:], AF.Square, accum=st1[:, 0:1])
        kb.ts("dve", st1[:, 1:2], st1[:, 0:1], 1.0 / 512, ALU.mult, EPS, ALU.add)
        kb.tt("pool", st1[:, 2:3], st1[:, 1:2], neghalf[:, 0:1], ALU.pow)
        yield
        kb.stt(on1[:, h * 512:(h + 1) * 512], bo[:], st1[:, 2:3], gs_[:], ALU.mult, ALU.mult)
        P.release(bo)
        yield

    def ret_epilogue(ti, is_sample, last_prompt):
        src = V(x1_d[ti * 128:(ti + 1) * 128, :], x1_res[ti])
        kb.dma("sp", yt[:], src, ch_yt)
        for g_ in range(2):
            bk = P.next()
            pv = bfv(bk)
            for k in range(8):
                kk_ = g_ * 8 + k
                kb.tr(V(pv.ap[:, k * 128:(k + 1) * 128], pv.res), on1[:, kk_ * 128:(kk_ + 1) * 128], idb[:], inc=(k == 7))
            kb.copy("act" if g_ else "dve", on1T[:, g_ * 8:(g_ + 1) * 8, :], V(pv.ap.rearrange("p (k t) -> p k t", k=8), pv.res))
            yield
        for n in range(2):
            bk = P.next()
            for k in range(16):
                kb.mm(bk[:], on1T[:, k, :], wob[k][:, n * 512:(n + 1) * 512], start=(k == 0), stop=(k == 15))
            kb.tt("dve", yt[:, n * 512:(n + 1) * 512], bk[:], yt[:, n * 512:(n + 1) * 512], ALU.add)
            yield
        kb.act(V(on1T.t[:].rearrange("p k t -> p (k t)")[:, 0:1024], on1T.res), yt[:], AF.Square, accum=st1[:, 4:5])
        kb.ts("dve", st1[:, 5:6], st1[:, 4:5], 1.0 / D, ALU.mult, EPS, ALU.add)
        kb.tt("pool", st1[:, 6:7], st1[:, 5:6], neghalf[:, 0:1], ALU.pow)
        yield
        kb.stt(yt[:], yt[:], st1[:, 6:7], fnw[:], ALU.mult, ALU.mult)
        dst = ext(ys_d[:, :]) if is_sample else ext(yp_d[ti * 128:(ti + 1) * 128, :])
        kb.dma("sp", dst, yt[:], ch_yt)
        yield

    try:
        kb.dma("sp", dm[:], ext(dmask_d[0]), ch_dm)
        tiles = [(ti, False) for ti in range(n_ptiles)] + ([(NPT, True)] if do_sample else [])
        items = [(ti, h, smp) for (ti, smp) in tiles for h in range(4)]
        for _ in ret_prologue(tiles[0][0], tiles[0][1]):
            pass
        carry = None
        dm_s_loaded = False
        for ii in range(len(items) + 1):
            gens = []
            if carry is not None:
                gens.append(carry)
                carry = None
            h2 = None
            if ii >= 1:
                ti2, h2, smp2 = items[ii - 1]
                if smp2 and not dm_s_loaded:
                    kb.dma("sp", dm[:], ext(dmask_d[1]), ch_dms)
                    dm_s_loaded = True
                gens.append(ret_stage2(ti2, h2, ii - 1, smp2, None))
            if ii < len(items):
                ti, h, smp_ = items[ii]
                gens.append(ret_stage1(ti, h, ii, smp_))
                if h == 1:
                    nxt = [tt_ for tt_ in tiles if tt_[0] > ti]
                    if nxt:
                        gens.append(ret_prologue(nxt[0][0], nxt[0][1]))
            drive(gens)
            if h2 == 3:
                if (not smp2) and ti2 == n_ptiles - 1:
                    kb.dma("sp", ext(sbp_d.rearrange("h (c p) v -> p h c v", p=128)), S1[:], ch_out)
                carry = ret_epilogue(ti2, smp2, False)
        if carry is not None:
            drive([carry])
    except Cut:
        pass
    kb.final_wait()
    es1.close()
    return nc


def _rot_tables():
    half = 128
    inv = 1.0 / (10000.0 ** np.linspace(0.0, 1.0, half))
    gam = 1.0 - 2.0 ** (-5.0 - np.arange(4))
    rot = np.zeros((NPT + 1, 128, 2, 128), np.float32)
    for ti in range(NPT + 1):
        if ti < NPT:
            pos = ti * 128 + np.arange(128, dtype=np.float64)
        else:
            pos = 16384.0 + (np.arange(128) % 8).astype(np.float64)
        ang = pos[:, None] * inv[None, :]
        rot[ti, :, 0, :] = np.cos(ang)
        rot[ti, :, 1, :] = np.sin(ang)
    return rot


def make_in_maps(I):
    f = np.float32
    cst = _build_cst(I["norm_w"], I["conv_w_a"], I["a_log_a"], I["dt_bias_a"], I["onorm_a"], I["onorm_b"])
    idb = np.eye(128, dtype=np.float32).astype(ml_dtypes.bfloat16)
    rot = _rot_tables()
    gam = 1.0 - 2.0 ** (-5.0 - np.arange(4, dtype=np.float64))
    dmask = np.zeros((2, 128, 4, 128), np.float32)
    ii = np.arange(128)
    for bi, blk in enumerate((128, 8)):
        same = (ii[:, None] // blk) == (ii[None, :] // blk)
        dif = ii[None, :] - ii[:, None]
        ok = (dif >= 0) & same
        for h in range(4):
            dmask[bi, :, h, :] = np.where(ok, gam[h] ** np.maximum(dif, 0) * 256.0 ** -0.5, 0.0)
    common = dict(
        wia=np.ascontiguousarray(I["w_in_a"][0], f), woa=np.ascontiguousarray(I["w_out_a"][0], f),
        wib=np.ascontiguousarray(I["w_in_b"][0], f), wob=np.ascontiguousarray(I["w_out_b"][0], f),
        cst=cst, idb=idb, fnw=np.ascontiguousarray(I["final_norm_w"].reshape(1, D), f), rot=rot, dmask=dmask)
    maps = []
    for c in range(NCORES):
        m = dict(common)
        m["xp"] = np.ascontiguousarray(I["x_prompt"][c], f)
        m["xs"] = np.ascontiguousarray(I["x_sample"][c * NSEQ:(c + 1) * NSEQ].reshape(128, D), f)
        m["sg"] = np.ascontiguousarray(I["state_gdn_ssm"][0, c * NSEQ:(c + 1) * NSEQ], f)
        m["sc"] = np.ascontiguousarray(I["state_gdn_conv"][0, c * NSEQ:(c + 1) * NSEQ].reshape(48, 3072), f)
        m["sr"] = np.ascontiguousarray(I["state_ret"][0, c * NSEQ:(c + 1) * NSEQ], f)
        maps.append(m)
    return maps


_NC_CACHE = {}


def kernel(**inputs):
    I = {k: np.asarray(v) for k, v in inputs.items()}
    if "nc" not in _NC_CACHE:
        _NC_CACHE["nc"] = build_program()
    nc = _NC_CACHE["nc"]
    in_maps = make_in_maps(I)
    res = run_bass_kernel_spmd(nc, in_maps, core_ids=list(range(NCORES))).results
    f = np.float32
    yp = np.stack([res[c]["yp"] for c in range(NCORES)]).astype(f)
    ys = np.concatenate([res[c]["ys"].reshape(NSEQ, 8, D) for c in range(NCORES)], 0).astype(f)
    sap = np.stack([res[c]["sap"] for c in range(NCORES)])[None].astype(f)
    cap = np.stack([res[c]["cap"] for c in range(NCORES)])[None].astype(f)
    sbp = np.stack([res[c]["sbp"] for c in range(NCORES)])[None].astype(f)
    sas = np.concatenate([res[c]["sas"] for c in range(NCORES)], 0)[None].astype(f)
    cas = np.concatenate([res[c]["cas"].reshape(NSEQ, 3, 3072) for c in range(NCORES)], 0)[None].astype(f)
    sbs = np.concatenate([res[c]["sbs"] for c in range(NCORES)], 0)[None].astype(f)
    return (yp, ys, sap, cap, sbp, sas, cas, sbs)
```
